# Optimizing a Trainium2 kernel written in Bass

```python
import math
import jax, jax.numpy as jnp
from jax import lax
import numpy as np

D_MODEL = 1024
BATCH = 16
SEQ = 4096
DEPTH = 4

MEM_LEN = 256
EPS = 1e-6

D_MIX = D_MODEL

MLA_HEADS = 8
MLA_NOPE = 64
MLA_ROPE = 32
MLA_V = 64
MLA_QK = MLA_NOPE + MLA_ROPE
MLA_Q_RANK = 384
MLA_KV_RANK = 256
MLA_WIDTH = MLA_HEADS * MLA_V
ROPE_BASE = 10000.0
Q_BLOCK = 128

HG_HEADS = 4
HG_DK = 64
HG_DV = 64
HG_WIDTH = HG_HEADS * HG_DV
HG_CHUNK = 64

CONV_GROUPS = 4
CONV_WIDTH = D_MIX - MLA_WIDTH - HG_WIDTH
CONV_K = 3

X_HEADS = 4
X_HEAD_DIM = 128

D_FF = 4 * D_MODEL

SPLIT_SIZES = (MLA_Q_RANK, MLA_KV_RANK, MLA_ROPE,
               HG_HEADS * HG_DK, HG_HEADS * HG_DK, HG_HEADS * HG_DV, HG_HEADS * HG_DV,
               CONV_WIDTH, CONV_WIDTH, CONV_WIDTH)
N_IN = sum(SPLIT_SIZES)
SPLIT_POINTS = tuple(int(v) for v in np.cumsum(SPLIT_SIZES)[:-1])

kernel_name = "hybrid_mla_hgrn2_shortconv_block"


def rmsnorm(x, g):
    xf = x.astype(jnp.float32)
    y = xf * lax.rsqrt(jnp.mean(xf * xf, axis=-1, keepdims=True) + EPS) * g.astype(jnp.float32)
    return y.astype(x.dtype)


def rotate(x, cos, sin):
    half = x.shape[-1] // 2
    x1, x2 = x[..., :half], x[..., half:]
    return jnp.concatenate([x1 * cos - x2 * sin, x2 * cos + x1 * sin], axis=-1)


def mla_mixer(cq, ckv, kr, positions, q_norm_g, kv_norm_g, w_uq, w_ukv, qn_g, kn_g):
    B, S = cq.shape[0], cq.shape[1]
    q = (rmsnorm(cq, q_norm_g) @ w_uq).reshape(B, S, MLA_HEADS, MLA_QK)
    kv = (rmsnorm(ckv, kv_norm_g) @ w_ukv).reshape(B, S, MLA_HEADS, MLA_NOPE + MLA_V)
    k_nope, v = kv[..., :MLA_NOPE], kv[..., MLA_NOPE:]
    k = jnp.concatenate([k_nope, jnp.broadcast_to(kr[:, :, None, :], (B, S, MLA_HEADS, MLA_ROPE))], axis=-1)
    q = rmsnorm(q, qn_g)
    k = rmsnorm(k, kn_g)
    inv_freq = ROPE_BASE ** (-jnp.arange(0, MLA_ROPE, 2, dtype=jnp.float32) / MLA_ROPE)
    ang = positions.astype(jnp.float32)[:, None] * inv_freq[None, :]
    cos = jnp.cos(ang)[None, :, None, :].astype(q.dtype)
    sin = jnp.sin(ang)[None, :, None, :].astype(q.dtype)
    q = jnp.concatenate([q[..., :MLA_NOPE], rotate(q[..., MLA_NOPE:], cos, sin)], axis=-1)
    k = jnp.concatenate([k[..., :MLA_NOPE], rotate(k[..., MLA_NOPE:], cos, sin)], axis=-1)
    qh = q.transpose(0, 2, 1, 3)
    kh = k.transpose(0, 2, 1, 3)
    vh = v.transpose(0, 2, 1, 3)
    nb = S // Q_BLOCK
    qb = qh.reshape(B, MLA_HEADS, nb, Q_BLOCK, MLA_QK).transpose(2, 0, 1, 3, 4)
    pb = positions.reshape(nb, Q_BLOCK)
    scale = 1.0 / math.sqrt(MLA_QK)

    def block(args):
        qblk, pblk = args
        s = jnp.einsum('bhqd,bhkd->bhqk', qblk, kh).astype(jnp.float32) * scale
        mask = positions[None, :] <= pblk[:, None]
        s = jnp.where(mask, s, -jnp.inf)
        p = jax.nn.softmax(s, axis=-1).astype(vh.dtype)
        return jnp.einsum('bhqk,bhkd->bhqd', p, vh)

    ob = lax.map(block, (qb, pb))
    return ob.transpose(1, 0, 3, 2, 4).reshape(B, S, MLA_WIDTH)


def hgrn2_mixer(q_in, f_in, i_in, g_in, lb, o_norm_g):
    dt = q_in.dtype
    B, S = q_in.shape[0], q_in.shape[1]
    q = jax.nn.silu(q_in.astype(jnp.float32))
    lbf = jnp.maximum(lb.astype(jnp.float32), 0.0)
    log_f = jnp.logaddexp(jnp.log(lbf), jnp.log1p(-lbf) + jax.nn.log_sigmoid(f_in.astype(jnp.float32)))
    k = -jnp.expm1(log_f)
    v = i_in.astype(jnp.float32)
    nc = S // HG_CHUNK

    def to_chunks(t, d):
        return t.reshape(B, nc, HG_CHUNK, HG_HEADS, d).transpose(1, 0, 3, 2, 4)

    qc, kc, fc, vc = to_chunks(q, HG_DK), to_chunks(k, HG_DK), to_chunks(log_f, HG_DK), to_chunks(v, HG_DV)
    causal = jnp.tril(jnp.ones((HG_CHUNK, HG_CHUNK), dtype=bool))

    def step(state, xs):
        qx, kx, vx, fx = xs
        b = jnp.cumsum(fx, axis=2)
        o_inter = jnp.einsum('bhtd,bhde->bhte', qx * jnp.exp(b), state)
        diff = b[:, :, :, None, :] - b[:, :, None, :, :]
        decay = jnp.exp(jnp.where(causal[:, :, None], diff, -jnp.inf))
        attn = jnp.einsum('bhtd,bhtsd,bhsd->bhts', qx, decay, kx)
        o_intra = jnp.einsum('bhts,bhse->bhte', attn, vx)
        b_last = b[:, :, -1:, :]
        state = (jnp.exp(b_last[:, :, 0, :])[..., None] * state
                 + jnp.einsum('bhsd,bhse->bhde', kx * jnp.exp(b_last - b), vx))
        return state, o_inter + o_intra

    s0 = jnp.zeros((B, HG_HEADS, HG_DK, HG_DV), jnp.float32)
    _, oc = lax.scan(step, s0, (qc, kc, vc, fc))
    o = oc.transpose(1, 0, 3, 2, 4).reshape(B, S, HG_HEADS, HG_DV)
    o = rmsnorm(o, o_norm_g.reshape(HG_HEADS, HG_DV)).reshape(B, S, HG_WIDTH)
    o = o * jax.nn.silu(g_in.astype(jnp.float32))
    return o.astype(dt)


def short_conv_mixer(b_in, c_in, x_in, w_conv):
    u = c_in * x_in
    kern = w_conv[:, None, :].astype(u.dtype)
    y = lax.conv_general_dilated(u, kern, window_strides=(1,), padding=[(CONV_K - 1, 0)],
                                 dimension_numbers=('NWC', 'WIO', 'NWC'),
                                 feature_group_count=CONV_WIDTH)
    return b_in * y


def memory_cross_attention(h, mem_n, w_q, w_kv, qn_g, kn_g, w_o):
    B, S = h.shape[0], h.shape[1]
    M = mem_n.shape[1]
    q = (h @ w_q).reshape(B, S, X_HEADS, X_HEAD_DIM)
    kv = (mem_n @ w_kv).reshape(B, M, 2, X_HEADS, X_HEAD_DIM)
    k, v = kv[:, :, 0], kv[:, :, 1]
    q = rmsnorm(q, qn_g)
    k = rmsnorm(k, kn_g)
    s = jnp.einsum('bshd,bmhd->bhsm', q, k).astype(jnp.float32) * (1.0 / math.sqrt(X_HEAD_DIM))
    p = jax.nn.softmax(s, axis=-1).astype(v.dtype)
    o = jnp.einsum('bhsm,bmhd->bshd', p, v).reshape(B, S, X_HEADS * X_HEAD_DIM)
    return o @ w_o


def setup_inputs(seed: int = 0) -> dict:
    key = jax.random.key(seed)
    ks = iter(jax.random.split(key, 32))

    def nrm(shape, scale):
        return jax.random.normal(next(ks), shape, jnp.float32) * scale

    def gain(shape):
        return 1.0 + 0.02 * jax.random.normal(next(ks), shape, jnp.float32)

    L = DEPTH
    return {
        "x": nrm((BATCH, SEQ, D_MODEL), 1.0),
        "mem": nrm((BATCH, MEM_LEN, D_MODEL), 1.0),
        "positions": jnp.arange(SEQ, dtype=jnp.int32),
        "mix_norm_g": gain((L, D_MODEL)),
        "w_in": nrm((L, D_MODEL, N_IN), D_MODEL ** -0.5),
        "mla_q_norm_g": gain((L, MLA_Q_RANK)),
        "mla_kv_norm_g": gain((L, MLA_KV_RANK)),
        "w_uq": nrm((L, MLA_Q_RANK, MLA_HEADS * MLA_QK), MLA_Q_RANK ** -0.5),
        "w_ukv": nrm((L, MLA_KV_RANK, MLA_HEADS * (MLA_NOPE + MLA_V)), MLA_KV_RANK ** -0.5),
        "mla_qn_g": gain((L, MLA_QK)),
        "mla_kn_g": gain((L, MLA_QK)),
        "hgrn_lb_logits": nrm((L, HG_HEADS * HG_DK), 0.5),
        "hgrn_o_norm_g": gain((L, HG_WIDTH)),
        "conv_w": nrm((L, CONV_K, CONV_WIDTH), CONV_K ** -0.5),
        "w_out": nrm((L, D_MIX, D_MODEL), D_MIX ** -0.5),
        "xattn_norm_g": gain((L, D_MODEL)),
        "mem_norm_g": gain((L, D_MODEL)),
        "w_xq": nrm((L, D_MODEL, X_HEADS * X_HEAD_DIM), D_MODEL ** -0.5),
        "w_xkv": nrm((L, D_MODEL, 2 * X_HEADS * X_HEAD_DIM), D_MODEL ** -0.5),
        "xq_norm_g": gain((L, X_HEAD_DIM)),
        "xk_norm_g": gain((L, X_HEAD_DIM)),
        "w_xo": nrm((L, X_HEADS * X_HEAD_DIM, D_MODEL), (X_HEADS * X_HEAD_DIM) ** -0.5),
        "mlp_norm_g": gain((L, D_MODEL)),
        "w_up": nrm((L, D_MODEL, D_FF), D_MODEL ** -0.5),
        "w_down": nrm((L, D_FF, D_MODEL), D_FF ** -0.5),
    }


def reference(x, mem, positions, mix_norm_g, w_in, mla_q_norm_g, mla_kv_norm_g, w_uq, w_ukv,
              mla_qn_g, mla_kn_g, hgrn_lb_logits, hgrn_o_norm_g, conv_w, w_out,
              xattn_norm_g, mem_norm_g, w_xq, w_xkv, xq_norm_g, xk_norm_g, w_xo,
              mlp_norm_g, w_up, w_down):
    lb_soft = jax.nn.softmax(hgrn_lb_logits.astype(jnp.float32), axis=0)
    lower_bounds = jnp.cumsum(lb_soft, axis=0) - lb_soft[0:1]
    for l in range(DEPTH):
        h = rmsnorm(x, mix_norm_g[l])
        proj = h @ w_in[l]
        cq, ckv, kr, hq, hf, hi, hg, cb, cc, cx = jnp.split(proj, SPLIT_POINTS, axis=-1)
        y_mla = mla_mixer(cq, ckv, kr, positions, mla_q_norm_g[l], mla_kv_norm_g[l],
                          w_uq[l], w_ukv[l], mla_qn_g[l], mla_kn_g[l])
        y_hg = hgrn2_mixer(hq, hf, hi, hg, lower_bounds[l], hgrn_o_norm_g[l])
        y_cv = short_conv_mixer(cb, cc, cx, conv_w[l])
        x = x + jnp.concatenate([y_mla, y_hg, y_cv], axis=-1) @ w_out[l]
        x = x + memory_cross_attention(rmsnorm(x, xattn_norm_g[l]), rmsnorm(mem, mem_norm_g[l]),
                                       w_xq[l], w_xkv[l], xq_norm_g[l], xk_norm_g[l], w_xo[l])
        hm = rmsnorm(x, mlp_norm_g[l])
        x = x + jnp.square(jax.nn.relu(hm @ w_up[l])) @ w_down[l]
    return x
```

```python
import contextlib
import math
A1PARTS = 'QKHC'
QSTAGE = 9
import numpy as np
import concourse.bass as bass
import concourse.mybir as mybir
from concourse.bass_utils import run_bass_kernel_spmd

F32 = mybir.dt.float32
BF16 = mybir.dt.bfloat16
AF = mybir.ActivationFunctionType
ALU = mybir.AluOpType
EPS = 1e-6

D = 1024
DFF = 4096
NIN = 2496
O_CQ, O_CKV, O_KR, O_HQ, O_HF, O_HI, O_HG, O_CB, O_CC, O_CX = 0, 384, 640, 704, 960, 1216, 1472, 1728, 1984, 2240
NV = 56
V_GMIX, V_GQ1, V_GKV, V_GQN, V_GQNSW, V_GKN, V_GKNSW, V_GO, V_CW, V_GXA, V_GXQ, V_GXK, V_GMLP, V_GMEM = \
    0, 8, 11, 13, 14, 15, 16, 17, 21, 27, 35, 36, 37, 45
VROW = 8 * 128
NEG = -30000.0

ENG_NAMES = ("pe", "act", "dve", "pool", "sp")


class Op:
    __slots__ = ("eng", "fn", "deps", "inc_idx", "dsem", "dval", "needed")

    def __init__(self, eng, fn):
        self.eng = eng
        self.fn = fn
        self.deps = []
        self.inc_idx = None
        self.dsem = None
        self.dval = None
        self.needed = False


class Tracker:
    def __init__(self, nc, es):
        self.nc = nc
        self.es = es
        self.ops = []
        self.last_w = {}
        self.readers = {}
        self.dsem_cnt = {}
        self.engobj = {"pe": nc.tensor, "act": nc.scalar, "dve": nc.vector, "pool": nc.gpsimd, "sp": nc.sync}

    mute = False

    def op(self, eng, fn, reads=(), writes=()):
        if self.mute:
            return None
        o = Op(eng, fn)
        self._deps(o, reads, writes)
        self.ops.append(o)
        return o

    def dma(self, queue, dsem, fn, reads=(), writes=()):
        if self.mute:
            return None
        o = Op(queue, fn)
        o.dsem = dsem
        self.dsem_cnt[dsem] = self.dsem_cnt.get(dsem, 0) + 16
        o.dval = self.dsem_cnt[dsem]
        self._deps(o, reads, writes)
        self.ops.append(o)
        return o

    def _deps(self, o, reads, writes):
        deps = []
        for k in reads:
            w = self.last_w.get(k)
            if w is not None:
                deps.append((w, False))
            if isinstance(k, str) and k[:2] == "ps" and k[2:].isdigit():
                for r in self.readers.get(k, ()):
                    if r.eng != o.eng:
                        deps.append((r, False))
        for k in writes:
            w = self.last_w.get(k)
            if w is not None:
                deps.append((w, True))
            for r in self.readers.get(k, ()):
                deps.append((r, False))
        seen = set()
        for d, waw in deps:
            if id(d) in seen or d is o:
                continue
            if d.dsem is None and o.dsem is None and d.eng == "pe" and o.eng == "pe":
                continue
            if waw and d.dsem is not None and o.dsem is not None and d.dsem == o.dsem and d.eng == o.eng:
                continue
            seen.add(id(d))
            o.deps.append(d)
            d.needed = True
        for k in writes:
            self.last_w[k] = o
            self.readers[k] = []
        for k in reads:
            lst = self.readers.setdefault(k, [])
            if o.dsem is None:
                lst[:] = [r for r in lst if not (r.dsem is None and r.eng == o.eng)]
            lst.append(o)

    def _lasts(self):
        last = {}
        for o in self.ops:
            if o.fn is None:
                continue
            last[(o.eng, o.dsem)] = o
        return list(last.values())

    def full_barrier(self):
        lasts = self._lasts()
        for e in ENG_NAMES:
            o = Op(e, None)
            for d in lasts:
                o.deps.append(d)
                d.needed = True
            self.ops.append(o)
        self.last_w.clear()
        self.readers.clear()

    def final_wait(self, eng="sp"):
        o = Op(eng, None)
        for d in self._lasts():
            o.deps.append(d)
            d.needed = True
        self.ops.append(o)

    def emit(self):
        nc = self.nc
        cnt = {e: 0 for e in ENG_NAMES}
        for o in self.ops:
            if o.dsem is None and o.needed and o.fn is not None:
                cnt[o.eng] += 1
                o.inc_idx = cnt[o.eng]
        sems = {}
        for e in ENG_NAMES:
            if cnt[e] > 0:
                sems[e] = self.es.enter_context(nc.semaphore("s_" + e))
        dsems = {}
        for name in self.dsem_cnt:
            dsems[name] = self.es.enter_context(nc.semaphore("d_" + name))
        waited = {e: {} for e in ENG_NAMES}
        n_wait = 0
        for o in self.ops:
            eng = self.engobj[o.eng]
            wt = waited[o.eng]
            need = {}
            for d in o.deps:
                if d.dsem is not None:
                    key = ("d", d.dsem)
                    val = d.dval
                else:
                    key = ("c", d.eng)
                    val = d.inc_idx
                if wt.get(key, 0) >= val:
                    continue
                if need.get(key, 0) < val:
                    need[key] = val
            for key, val in need.items():
                sem = dsems[key[1]] if key[0] == "d" else sems[key[1]]
                eng.wait_ge(sem, val)
                wt[key] = val
                n_wait += 1
            if o.fn is None:
                continue
            ins = o.fn(eng)
            if o.dsem is not None:
                ins.then_inc(dsems[o.dsem], 16)
            elif o.inc_idx is not None:
                ins.then_inc(sems[o.eng], 1)
        return dict(n_ops=len(self.ops), n_wait=n_wait, incs=cnt, n_dsems=len(dsems))


class Arena:
    def __init__(self, nc, es, nbytes):
        self.cap = nbytes
        self.t = es.enter_context(nc.sbuf_tensor("arena", [128, nbytes // 4], F32))
        self.off = 0

    def reset(self, off=0):
        self.off = off

    def alloc(self, shape, dt):
        n = 1
        for s in shape:
            n *= s
        size = n * (4 if dt == F32 else 2)
        size = (size + 3) // 4 * 4
        off = (self.off + 63) // 64 * 64
        assert off + size <= self.cap, f"arena overflow {off + size} > {self.cap}"
        ap = self.t[:, off // 4:(off + size) // 4]
        if dt != F32:
            ap = ap.bitcast(dt)
            if ap.shape[-1] != n:
                ap = ap[:, 0:n]
        if len(shape) == 2:
            ap = ap.rearrange("p (a b) -> p a b", b=shape[1])
        elif len(shape) == 3:
            ap = ap.rearrange("p (a b c) -> p a b c", b=shape[1], c=shape[2])
        self.off = off + size
        return ap


class Builder:
    def __init__(self, NB, S, L, MEM=256, phases="A1,A2,B,C"):
        self.NB, self.S, self.L, self.MEM = NB, S, L, MEM
        self.NT = NB * S
        self.phases = phases.split(",")
        nc = bass.Bass("TRN2", target_bir_lowering=False)
        self.nc = nc
        NT = self.NT
        dt = nc.dram_tensor
        self.xT = dt("xT", [D, NT], F32, kind="ExternalInput").ap()
        self.yT = dt("yT", [D, NT], F32, kind="ExternalOutput").ap()
        self.memT = dt("memT", [NB, D, MEM], F32, kind="ExternalInput").ap()
        self.w_in = dt("w_in", [L, 128, 8 * NIN], F32, kind="ExternalInput").ap()
        self.w_uq = dt("w_uq", [L, 128, 3 * 768], F32, kind="ExternalInput").ap()
        self.w_uqsw = dt("w_uqsw", [L, 128, 3 * 768], F32, kind="ExternalInput").ap()
        self.w_uk = dt("w_uk", [L, 128, 2 * 512], F32, kind="ExternalInput").ap()
        self.w_uv = dt("w_uv", [L, 128, 2 * 512], F32, kind="ExternalInput").ap()
        self.w_om = dt("w_om", [L, 64, 8 * D], F32, kind="ExternalInput").ap()
        self.w_oh = dt("w_oh", [L, 64, 4 * D], F32, kind="ExternalInput").ap()
        self.w_oc = dt("w_oc", [L, 128, 2 * D], F32, kind="ExternalInput").ap()
        self.w_xq = dt("w_xq", [L, 128, 8 * 512], F32, kind="ExternalInput").ap()
        self.w_xkv = dt("w_xkv", [L, 128, 8 * 1024], F32, kind="ExternalInput").ap()
        self.w_xo = dt("w_xo", [L, 128, 4 * D], F32, kind="ExternalInput").ap()
        self.w_up = dt("w_up", [L, 128, 8 * DFF], F32, kind="ExternalInput").ap()
        self.w_dn = dt("w_dn", [L, 128, 32 * D], F32, kind="ExternalInput").ap()
        self.vecs = dt("vecs", [L, 128, NV], F32, kind="ExternalInput").ap()
        self.lbl = dt("lbl", [64, 4 * 4], F32, kind="ExternalInput").ap()
        self.cosT = dt("cosT", [96, S], F32, kind="ExternalInput").ap()
        self.sinS = dt("sinS", [96, S], F32, kind="ExternalInput").ap()
        self.cmask = dt("cmask", [128, 4 * 512], F32, kind="ExternalInput").ap()
        self.consts = dt("consts", [128, 128 + 64 + 256], F32, kind="ExternalInput").ap()
        self.q_s = dt("q_s", [NB, 8, 96, S], BF16, kind="Internal").ap()
        self.k_s = dt("k_s", [NB, 8, 96, S], BF16, kind="Internal").ap()
        self.v_s = dt("v_s", [NB, S, VROW], BF16, kind="Internal").ap()
        self.yh_s = dt("yh_s", [NB, 4, 64, S], BF16, kind="Internal").ap()
        self.yc_s = dt("yc_s", [NB, 256, S], BF16, kind="Internal").ap()

    def build(self):
        nc = self.nc
        with contextlib.ExitStack() as es:
            self.es = es
            self.T = Tracker(nc, es)
            self.A = Arena(nc, es, 212480)
            self.ps = [es.enter_context(nc.psum_tensor(f"psb{i}", [128, 512], F32)) for i in range(8)]
            T = self.T
            for l in range(self.L):
                src = self.xT if l == 0 else self.yT
                if "A1" in self.phases:
                    self.phase_A1(l, src)
                    T.full_barrier()
                if "A2" in self.phases:
                    self.phase_A2(l, src)
                    T.full_barrier()
                elif l == 0:
                    self.copy_x()
                    T.full_barrier()
                if "B" in self.phases:
                    self.phase_B(l)
                    T.full_barrier()
                if "C" in self.phases:
                    self.phase_C(l)
                    T.full_barrier()
            T.final_wait("sp")
            self.stats = T.emit()
        return nc

    def copy_x(self):
        T = self.T
        for t in range(self.NT // 512):
            T.dma("sp", "cp", lambda e, t=t: e.dma_start(out=self.yT[:, t * 512:(t + 1) * 512], in_=self.xT[:, t * 512:(t + 1) * 512]),
                  writes=[("y", t)])

    def wload(self, dst, src, key, piece_cols, ncols):
        T = self.T
        for c0 in range(0, ncols, piece_cols):
            c1 = min(ncols, c0 + piece_cols)
            T.dma("pool", key, lambda e, c0=c0, c1=c1: e.dma_start(out=dst[:, c0:c1], in_=src[:, c0:c1]), writes=[key])

    def rstd_from(self, out_ap, in_ap, tmp_ap, scale, bias_ap_or_f, keys_r, key_tmp, key_out):
        T = self.T
        T.op("act", lambda e: e.activation(out=tmp_ap, in_=in_ap, func=AF.Ln, scale=scale, bias=bias_ap_or_f),
             reads=keys_r, writes=[key_tmp])
        T.op("act", lambda e: e.activation(out=out_ap, in_=tmp_ap, func=AF.Exp, scale=-0.5), reads=[key_tmp], writes=[key_out])

    def phase_C(self, l):
        T, A, nc, ps = self.T, self.A, self.nc, self.ps
        NT = self.NT
        TS = 512
        KC, FC = 8, 32
        A.reset()
        wup = A.alloc([KC, DFF], BF16)
        wdn = A.alloc([FC, D], BF16)
        vec = A.alloc([NV], F32)
        ones = A.alloc([128], BF16)
        xin = A.alloc([KC, TS], F32)
        xb = A.alloc([KC, TS], BF16)
        sq = A.alloc([KC, TS], BF16)
        a = A.alloc([FC, TS], BF16)
        lnt = A.alloc([TS], F32)
        rstd = A.alloc([TS], F32)
        t1 = [A.alloc([TS], F32) for _ in range(2)]
        xr = [A.alloc([TS], F32) for _ in range(3)]
        yT = self.yT
        T.op("pool", lambda e: e.memset(ones, 1.0), writes=["ones"])
        T.dma("sp", "vec", lambda e: e.dma_start(out=vec, in_=self.vecs[l]), writes=["vec"])
        self.wload(wup.rearrange("p a b -> p (a b)"), self.w_up[l], "wup", DFF, KC * DFF)
        self.wload(wdn.rearrange("p a b -> p (a b)"), self.w_dn[l], "wdn", 4 * D, FC * D)
        g = vec[:, V_GMLP:V_GMLP + 8]
        yT_t = yT.rearrange("(kc p) n -> p kc n", p=128)
        ntile = NT // TS
        T.dma("sp", "xin", lambda e: e.dma_start(out=xin, in_=yT_t[:, :, 0:TS]), reads=[("y", 0)], writes=["xin"])
        psi = 0
        xri = 0
        for t in range(ntile):
            c0 = t * TS
            T.op("dve", lambda e: e.tensor_tensor(out=xb, in0=xin, in1=g.unsqueeze(2).to_broadcast([128, KC, TS]), op=ALU.mult),
                 reads=["xin", "vec"], writes=["xb"])
            T.op("act", lambda e: e.activation(out=sq, in_=xin, func=AF.Square), reads=["xin"], writes=["sq"])
            if t + 1 < ntile:
                T.dma("sp", "xin", lambda e, c1=c0 + TS: e.dma_start(out=xin, in_=yT_t[:, :, c1:c1 + TS]),
                      reads=[("y", t + 1)], writes=["xin"])
            pss = ps[7]
            for kc in range(KC):
                T.op("pe", lambda e, kc=kc: e.matmul(pss[:], lhsT=ones, rhs=sq[:, kc, :], start=(kc == 0), stop=(kc == KC - 1)),
                     reads=["ones", "sq"], writes=["ps7"])
            self.rstd_from(rstd, pss[:], lnt, 1.0 / D, EPS, ["ps7"], "lnt", "rstd")
            for oc in range(FC):
                b_ = psi % 4; p = ps[b_]; pk = f"ps{b_}"; tt = t1[psi % 2]; tk = f"t1_{psi % 2}"; psi += 1
                for kc in range(KC):
                    T.op("pe", lambda e, kc=kc, oc=oc, p=p: e.matmul(p[:], lhsT=wup[:, kc, oc * 128:(oc + 1) * 128], rhs=xb[:, kc, :],
                                                                  start=(kc == 0), stop=(kc == KC - 1)),
                         reads=["wup", "xb"], writes=[pk])
                T.op("dve", lambda e, p=p, tt=tt: e.scalar_tensor_tensor(out=tt, in0=p[:], scalar=0.0, in1=rstd, op0=ALU.max, op1=ALU.mult),
                     reads=[pk, "rstd"], writes=[tk])
                T.op("act", lambda e, tt=tt, oc=oc: e.activation(out=a[:, oc, :], in_=tt, func=AF.Square), reads=[tk], writes=[("a", oc)])
            for oc in range(8):
                b_ = psi % 4; p = ps[b_]; pk = f"ps{b_}"; psi += 1
                x_ = xr[xri % 3]; xk = f"xr_{xri % 3}"; xri += 1
                T.dma("sp", xk, lambda e, x_=x_, oc=oc, c0=c0: e.dma_start(out=x_, in_=yT[oc * 128:(oc + 1) * 128, c0:c0 + TS]),
                      reads=[("y", t)], writes=[xk])
                for kc in range(FC):
                    T.op("pe", lambda e, kc=kc, oc=oc, p=p: e.matmul(p[:], lhsT=wdn[:, kc, oc * 128:(oc + 1) * 128], rhs=a[:, kc, :],
                                                                  start=(kc == 0), stop=(kc == FC - 1)),
                         reads=["wdn", ("a", kc)], writes=[pk])
                T.op("dve", lambda e, p=p, x_=x_: e.tensor_tensor(out=x_, in0=p[:], in1=x_, op=ALU.add), reads=[pk, xk], writes=[xk])
                T.dma("sp", xk, lambda e, x_=x_, oc=oc, c0=c0: e.dma_start(out=yT[oc * 128:(oc + 1) * 128, c0:c0 + TS], in_=x_),
                      reads=[xk], writes=[("y", t)])

    def phase_B(self, l):
        T, A, nc, ps = self.T, self.A, self.nc, self.ps
        NB, S, MEM = self.NB, self.S, self.MEM
        TS = 512
        KC = 8
        MB = MEM // 128
        A.reset()
        wq = A.alloc([KC, 512], BF16)
        wkv = A.alloc([KC, 1024], BF16)
        wo = A.alloc([4, D], BF16)
        vec = A.alloc([NV], F32)
        ones = A.alloc([128], BF16)
        kT = A.alloc([NB, 4, MEM], BF16)
        vv = A.alloc([NB, MB, 512], BF16)
        memf = A.alloc([KC, MEM], F32)
        memb = A.alloc([KC, MEM], BF16)
        msq = A.alloc([KC, MEM], BF16)
        rsm = A.alloc([MEM], F32)
        tmpm = A.alloc([MEM], F32)
        kraw = A.alloc([MEM], F32)
        ksq = A.alloc([MEM], BF16)
        rk = A.alloc([MEM], F32)
        rtok = A.alloc([2 * MB], F32)
        ttok = A.alloc([2 * MB], F32)
        xin = [A.alloc([KC, TS], F32) for _ in range(2)]
        xb = A.alloc([KC, TS], BF16)
        sq = A.alloc([KC, TS], BF16)
        epsx = A.alloc([TS], F32)
        psq = A.alloc([TS], BF16)
        tq = A.alloc([TS], F32)
        lq = A.alloc([TS], F32)
        rq = A.alloc([TS], F32)
        qT = A.alloc([4, TS], BF16)
        PT = [A.alloc([TS], BF16) for _ in range(4)]
        rec = A.alloc([TS], F32)
        ob = A.alloc([4, TS], BF16)
        xo = [A.alloc([TS], F32) for _ in range(3)]
        yT = self.yT
        T.op("pool", lambda e: e.memset(ones, 1.0), writes=["ones"])
        T.dma("sp", "vec", lambda e: e.dma_start(out=vec, in_=self.vecs[l]), writes=["vec"])
        self.wload(wq.rearrange("p a b -> p (a b)"), self.w_xq[l], "wq", 4096, KC * 512)
        self.wload(wkv.rearrange("p a b -> p (a b)"), self.w_xkv[l], "wkv", 4096, KC * 1024)
        self.wload(wo.rearrange("p a b -> p (a b)"), self.w_xo[l], "wo", 4096, 4 * D)
        gmem = vec[:, V_GMEM:V_GMEM + 8]
        gxa = vec[:, V_GXA:V_GXA + 8]
        gxq = vec[:, V_GXQ:V_GXQ + 1]
        gxk = vec[:, V_GXK:V_GXK + 1]
        for b in range(NB):
            T.dma("sp", "memf", lambda e, b=b: e.dma_start(out=memf, in_=self.memT[b].rearrange("(kc p) m -> p kc m", p=128)),
                  writes=["memf"])
            T.op("dve", lambda e: e.tensor_tensor(out=memb, in0=memf, in1=gmem.unsqueeze(2).to_broadcast([128, KC, MEM]), op=ALU.mult),
                 reads=["memf", "vec"], writes=["memb"])
            T.op("act", lambda e: e.activation(out=msq, in_=memf, func=AF.Square), reads=["memf"], writes=["msq"])
            for kc in range(KC):
                T.op("pe", lambda e, kc=kc: e.matmul(ps[7][:, 0:MEM], lhsT=ones, rhs=msq[:, kc, :], start=(kc == 0), stop=(kc == KC - 1)),
                     reads=["ones", "msq"], writes=["ps7"])
            self.rstd_from(rsm, ps[7][:, 0:MEM], tmpm, 1.0 / D, EPS, ["ps7"], "tmpm", "rsm")
            for mb in range(MB):
                for kc in range(KC):
                    T.op("pe", lambda e, kc=kc, mb=mb: e.matmul(ps[6][:, 2 * mb:2 * mb + 2], lhsT=msq[:, kc, mb * 128:(mb + 1) * 128], rhs=ones[:, 0:2],
                                                               start=(kc == 0), stop=(kc == KC - 1)),
                         reads=["ones", "msq"], writes=["ps6"])
            self.rstd_from(rtok, ps[6][:, 0:2 * MB], ttok, 1.0 / D, EPS, ["ps6"], "ttok", "rtok")
            for h in range(4):
                p = ps[h % 2]; pk = f"ps{h % 2}"
                for kc in range(KC):
                    T.op("pe", lambda e, kc=kc, h=h, p=p: e.matmul(p[:, 0:MEM], lhsT=wkv[:, kc, h * 128:(h + 1) * 128], rhs=memb[:, kc, :],
                                                                start=(kc == 0), stop=(kc == KC - 1)),
                         reads=["wkv", "memb"], writes=[pk])
                T.op("dve", lambda e, p=p: e.tensor_tensor(out=kraw, in0=p[:, 0:MEM], in1=rsm, op=ALU.mult), reads=[pk, "rsm"], writes=["kraw"])
                T.op("act", lambda e: e.activation(out=ksq, in_=kraw, func=AF.Square), reads=["kraw"], writes=["ksq"])
                T.op("pe", lambda e: e.matmul(ps[5][:, 0:MEM], lhsT=ones, rhs=ksq, start=True, stop=True), reads=["ones", "ksq"], writes=["ps5"])
                self.rstd_from(rk, ps[5][:, 0:MEM], tmpm, 1.0 / 128, EPS, ["ps5"], "tmpm", "rk")
                T.op("dve", lambda e, b=b, h=h: e.scalar_tensor_tensor(out=kT[:, b, h, :], in0=kraw, scalar=gxk, in1=rk, op0=ALU.mult, op1=ALU.mult),
                     reads=["kraw", "rk", "vec"], writes=["kT"])
            for mb in range(MB):
                p = ps[2 + mb % 2]; pk = f"ps{2 + mb % 2}"
                for kc in range(KC):
                    T.op("pe", lambda e, kc=kc, mb=mb, p=p: e.matmul(p[:], lhsT=memb[:, kc, mb * 128:(mb + 1) * 128], rhs=wkv[:, kc, 512:1024],
                                                                  start=(kc == 0), stop=(kc == KC - 1)),
                         reads=["wkv", "memb"], writes=[pk])
                T.op("dve", lambda e, p=p, b=b, mb=mb: e.tensor_scalar(out=vv[:, b, mb, :], in0=p[:], scalar1=rtok[:, 2 * mb:2 * mb + 1], scalar2=None, op0=ALU.mult),
                     reads=[pk, "rtok"], writes=["vv"])
        yT_t = yT.rearrange("(kc p) n -> p kc n", p=128)
        ntile = self.NT // TS
        T.dma("sp", "xinB0", lambda e: e.dma_start(out=xin[0], in_=yT_t[:, :, 0:TS]), reads=[("y", 0)], writes=["xinB0"])
        sc_i = 0
        xoi = 0
        for t in range(ntile):
            c0 = t * TS
            b = c0 // S
            xi = xin[t % 2]; xik = f"xinB{t % 2}"
            if t + 1 < ntile:
                T.dma("sp", f"xinB{(t + 1) % 2}", lambda e, c1=c0 + TS, x2=xin[(t + 1) % 2]: e.dma_start(out=x2, in_=yT_t[:, :, c1:c1 + TS]),
                      reads=[("y", t + 1)], writes=[f"xinB{(t + 1) % 2}"])
            T.op("dve", lambda e, xi=xi: e.tensor_tensor(out=xb, in0=xi, in1=gxa.unsqueeze(2).to_broadcast([128, KC, TS]), op=ALU.mult),
                 reads=[xik, "vec"], writes=["xb"])
            T.op("act", lambda e, xi=xi: e.activation(out=sq, in_=xi, func=AF.Square), reads=[xik], writes=["sq"])
            for kc in range(KC):
                T.op("pe", lambda e, kc=kc: e.matmul(ps[7][:], lhsT=ones, rhs=sq[:, kc, :], start=(kc == 0), stop=(kc == KC - 1)),
                     reads=["ones", "sq"], writes=["ps7"])
            T.op("dve", lambda e: e.tensor_scalar(out=epsx, in0=ps[7][:], scalar1=EPS / D, scalar2=EPS * EPS, op0=ALU.mult, op1=ALU.add),
                 reads=["ps7"], writes=["epsx"])
            for h in range(4):
                p = ps[h % 2]; pk = f"ps{h % 2}"
                for kc in range(KC):
                    T.op("pe", lambda e, kc=kc, h=h, p=p: e.matmul(p[:], lhsT=wq[:, kc, h * 128:(h + 1) * 128], rhs=xb[:, kc, :],
                                                                start=(kc == 0), stop=(kc == KC - 1)),
                         reads=["wq", "xb"], writes=[pk])
                T.op("act", lambda e, p=p: e.activation(out=psq, in_=p[:], func=AF.Square), reads=[pk], writes=["psq"])
                T.op("pe", lambda e: e.matmul(ps[6][:], lhsT=ones, rhs=psq, start=True, stop=True), reads=["ones", "psq"], writes=["ps6"])
                T.op("dve", lambda e: e.scalar_tensor_tensor(out=tq, in0=ps[6][:], scalar=1.0 / 128, in1=epsx, op0=ALU.mult, op1=ALU.add),
                     reads=["ps6", "epsx"], writes=["tq"])
                self.rstd_from(rq, tq, lq, 1.0, 0.0, ["tq"], "lq", "rq")
                T.op("dve", lambda e, p=p, h=h: e.scalar_tensor_tensor(out=qT[:, h, :], in0=p[:], scalar=gxq, in1=rq, op0=ALU.mult, op1=ALU.mult),
                     reads=[pk, "rq", "vec"], writes=[("qT", h)])
            for h in range(4):
                pts = []
                for mb in range(MB):
                    sb_ = 2 + sc_i % 2; sc_i += 1
                    pt = PT[(h * MB + mb) % 4]; ptk = f"PT{(h * MB + mb) % 4}"
                    T.op("pe", lambda e, h=h, mb=mb, sb_=sb_, b=b: e.matmul(ps[sb_][:], lhsT=kT[:, b, h, mb * 128:(mb + 1) * 128], rhs=qT[:, h, :],
                                                                         start=True, stop=True),
                         reads=["kT", ("qT", h)], writes=[f"ps{sb_}"])
                    T.op("act", lambda e, pt=pt, sb_=sb_: e.activation(out=pt, in_=ps[sb_][:], func=AF.Exp, scale=1.0 / math.sqrt(128.0)),
                         reads=[f"ps{sb_}"], writes=[ptk])
                    pts.append((pt, ptk))
                for mb, (pt, ptk) in enumerate(pts):
                    T.op("pe", lambda e, h=h, mb=mb, pt=pt, b=b: e.matmul(ps[4][:], lhsT=vv[:, b, mb, h * 128:(h + 1) * 128], rhs=pt,
                                                                       start=(mb == 0), stop=(mb == MB - 1)),
                         reads=["vv", ptk], writes=["ps4"])
                for mb, (pt, ptk) in enumerate(pts):
                    T.op("pe", lambda e, mb=mb, pt=pt: e.matmul(ps[5][:], lhsT=ones, rhs=pt, start=(mb == 0), stop=(mb == MB - 1)),
                         reads=["ones", ptk], writes=["ps5"])
                T.op("dve", lambda e: e.reciprocal(out=rec, in_=ps[5][:]), reads=["ps5"], writes=["rec"])
                T.op("dve", lambda e, h=h: e.tensor_tensor(out=ob[:, h, :], in0=ps[4][:], in1=rec, op=ALU.mult), reads=["ps4", "rec"], writes=[("ob", h)])
            for oc in range(8):
                p = ps[oc % 2]; pk = f"ps{oc % 2}"
                x_ = xo[xoi % 3]; xk = f"xoB{xoi % 3}"; xoi += 1
                for h in range(4):
                    T.op("pe", lambda e, h=h, oc=oc, p=p: e.matmul(p[:], lhsT=wo[:, h, oc * 128:(oc + 1) * 128], rhs=ob[:, h, :],
                                                                start=(h == 0), stop=(h == 3)),
                         reads=["wo", ("ob", h)], writes=[pk])
                T.op("dve", lambda e, p=p, x_=x_, xi=xi, oc=oc: e.tensor_tensor(out=x_, in0=p[:], in1=xi[:, oc, :], op=ALU.add),
                     reads=[pk, xik], writes=[xk])
                T.dma("sp", xk, lambda e, x_=x_, oc=oc, c0=c0: e.dma_start(out=yT[oc * 128:(oc + 1) * 128, c0:c0 + TS], in_=x_),
                      reads=[xk], writes=[("y", t)])

    def phase_A1(self, l, src):
        T, A, nc, ps = self.T, self.A, self.nc, self.ps
        NB, S = self.NB, self.S
        TS = 256
        KC = 8
        A.reset()
        al = A.alloc
        win = al([KC, NIN], BF16)
        wuq = al([3, 768], BF16)
        wuqs = al([3, 768], BF16)
        wuk = al([2, 512], BF16)
        wuv = al([2, 512], BF16)
        vec = al([NV], F32)
        ones = al([128], BF16)
        cst = al([128 + 64 + 256], F32)
        ident = al([128], BF16)
        tri = al([4, 4, 64], F32)
        m01 = al([4, TS], F32)
        lbt = al([4, 4], F32)
        lbe = al([4, 4], F32)
        lbs = al([4], F32)
        lbv = al([4], F32)
        xin = [al([KC, TS], F32) for _ in range(2)]
        xb = al([KC, TS], BF16)
        sq = al([KC, TS], BF16)
        rs0 = al([TS], F32)
        lt0 = al([TS], F32)
        rtok = al([8], F32)
        ttok = al([8], F32)
        cosb = [al([TS], F32) for _ in range(2)]
        sinb = [al([TS], F32) for _ in range(2)]
        cq = al([3, TS], F32)
        cqb = al([3, TS], BF16)
        cqs = al([3, TS], BF16)
        epsq = al([TS], F32)
        ckv = al([2, TS], F32)
        ckvs = al([2, TS], BF16)
        rskv = al([TS], F32)
        ckvb = al([2, TS], BF16)
        krt = al([TS], F32)
        kcat = al([8, TS], F32)
        ksw = al([TS], F32)
        kbt = al([TS], F32)
        ksq = al([8, TS], BF16)
        gen = [al([2, TS], F32) for _ in range(6)]
        qTo = al([8, TS], BF16)
        kTo = al([8, TS], BF16)
        vto = al([2, 8, 128], BF16)
        hq = al([4, TS], F32)
        hf = al([4, TS], F32)
        hg = al([4, TS], F32)
        hgen = [al([4, TS], F32) for _ in range(5)]
        qtl = al([4, TS], BF16)
        ktl = al([4, TS], BF16)
        ktok = al([4, 4, 64], BF16)
        vtok = al([4, 256], BF16)
        attb = al([4, 4, 64], BF16)
        sc8 = [al([4, 4], F32) for _ in range(6)]
        Sst = al([4, 64], F32)
        Ssc = al([4, 4, 64], BF16)
        tmpU = al([4, 64], F32)
        osq = al([4, TS], BF16)
        yho = al([4, TS], BF16)
        cbt = al([2, TS], F32)
        cct = al([2, TS], F32)
        cxt = al([2, TS], F32)
        ubuf = al([2, TS + 2], F32)
        yacc = al([2, TS], F32)
        yco = al([2, TS], BF16)

        T.op("pool", lambda e: e.memset(ones, 1.0), writes=["ones"])
        T.dma("sp", "vec", lambda e: e.dma_start(out=vec, in_=self.vecs[l]), writes=["vec"])
        T.dma("sp", "cst", lambda e: e.dma_start(out=cst, in_=self.consts), writes=["cst"])
        T.dma("sp", "lbt", lambda e: e.dma_start(out=lbt[0:64].rearrange("p a b -> p (a b)"), in_=self.lbl), writes=["lbt"])
        T.op("dve", lambda e: e.tensor_copy(out=ident, in_=cst[:, 0:128]), reads=["cst"], writes=["ident"])
        T.op("dve", lambda e: e.tensor_copy(out=tri[0:64], in_=cst[0:64, 128:192].unsqueeze(1).unsqueeze(1).to_broadcast([64, 4, 4, 64])),
             reads=["cst"], writes=["tri"])
        T.op("dve", lambda e: e.tensor_copy(out=m01[0:64], in_=cst[0:64, 192:448].unsqueeze(1).to_broadcast([64, 4, TS])),
             reads=["cst"], writes=["m01"])
        T.op("pool", lambda e: e.memset(ksw, 0.0), writes=["ksw"])
        T.op("pool", lambda e: e.memset(vto, 1.0), writes=["vto"])
        T.op("act", lambda e: e.activation(out=lbe[0:64], in_=lbt[0:64], func=AF.Exp), reads=["lbt"], writes=["lbe"])
        T.op("dve", lambda e: e.tensor_reduce(out=lbs[0:64], in_=lbe[0:64], axis=mybir.AxisListType.X, op=ALU.add), reads=["lbe"], writes=["lbs"])
        T.op("dve", lambda e: e.reciprocal(out=lbs[0:64], in_=lbs[0:64]), reads=["lbs"], writes=["lbs"])
        if l == 0:
            T.op("pool", lambda e: e.memset(lbv, 0.0), writes=["lbv"])
        else:
            T.op("dve", lambda e: e.tensor_reduce(out=lbv[0:64], in_=lbe[0:64, :, 1:l + 1], axis=mybir.AxisListType.X, op=ALU.add),
                 reads=["lbe"], writes=["lbv"])
            T.op("dve", lambda e: e.tensor_tensor(out=lbv[0:64], in0=lbv[0:64], in1=lbs[0:64], op=ALU.mult), reads=["lbv", "lbs"], writes=["lbv"])
        self.wload(win.rearrange("p a b -> p (a b)"), self.w_in[l], "win", NIN, KC * NIN)
        self.wload(wuq.rearrange("p a b -> p (a b)"), self.w_uq[l], "wuq", 2304, 2304)
        self.wload(wuqs.rearrange("p a b -> p (a b)"), self.w_uqsw[l], "wuqs", 2304, 2304)
        self.wload(wuk.rearrange("p a b -> p (a b)"), self.w_uk[l], "wuk", 1024, 1024)
        self.wload(wuv.rearrange("p a b -> p (a b)"), self.w_uv[l], "wuv", 1024, 1024)

        gmix = vec[:, V_GMIX:V_GMIX + 8]
        gq1 = vec[:, V_GQ1:V_GQ1 + 3]
        gkv = vec[:, V_GKV:V_GKV + 2]
        gqn = vec[0:96, V_GQN:V_GQN + 1]
        gqns = vec[0:96, V_GQNSW:V_GQNSW + 1]
        gkn = vec[0:96, V_GKN:V_GKN + 1]
        gkns = vec[0:96, V_GKNSW:V_GKNSW + 1]
        go = vec[0:64, V_GO:V_GO + 4]
        cw = vec[:, V_CW:V_CW + 6]
        src_t = src.rearrange("(kc p) n -> p kc n", p=128)
        ntile = self.NT // TS
        tps = S // TS
        T.dma("sp", "xinA0", lambda e: e.dma_start(out=xin[0], in_=src_t[:, :, 0:TS]), reads=[("y", 0)], writes=["xinA0"])
        pcnt = [0]

        def pbank():
            b_ = pcnt[0] % 4
            pcnt[0] += 1
            return ps[b_], f"ps{b_}"

        for t in range(ntile):
            c0 = t * TS
            b = c0 // S
            s0 = c0 - b * S
            xi = xin[t % 2]; xik = f"xinA{t % 2}"
            cs = cosb[t % 2]; csk = f"cos{t % 2}"; sn = sinb[t % 2]; snk = f"sin{t % 2}"
            T.dma("sp", csk, lambda e, cs=cs, s0=s0: e.dma_start(out=cs[0:96], in_=self.cosT[:, s0:s0 + TS]), writes=[csk])
            T.dma("sp", snk, lambda e, sn=sn, s0=s0: e.dma_start(out=sn[0:96], in_=self.sinS[:, s0:s0 + TS]), writes=[snk])
            if t + 1 < ntile:
                T.dma("sp", f"xinA{(t + 1) % 2}", lambda e, c1=c0 + TS, x2=xin[(t + 1) % 2]: e.dma_start(out=x2, in_=src_t[:, :, c1:c1 + TS]),
                      reads=[("y", (t + 1) // 2)], writes=[f"xinA{(t + 1) % 2}"])
            if s0 == 0:
                T.op("pool", lambda e: e.memset(Sst, 0.0), writes=["Sst"])
                T.op("pool", lambda e: e.memset(ubuf, 0.0), writes=["ubuf"])
            T.op("dve", lambda e, xi=xi: e.tensor_tensor(out=xb, in0=xi, in1=gmix.unsqueeze(2).to_broadcast([128, KC, TS]), op=ALU.mult),
                 reads=[xik, "vec"], writes=["xb"])
            T.op("act", lambda e, xi=xi: e.activation(out=sq, in_=xi, func=AF.Square), reads=[xik], writes=["sq"])
            for kc in range(KC):
                T.op("pe", lambda e, kc=kc: e.matmul(ps[7][:, 0:TS], lhsT=ones, rhs=sq[:, kc, :], start=(kc == 0), stop=(kc == KC - 1)),
                     reads=["ones", "sq"], writes=["ps7"])
            self.rstd_from(rs0, ps[7][:, 0:TS], lt0, 1.0 / D, EPS, ["ps7"], "lt0", "rs0")
            for c in range(4):
                for kc in range(KC):
                    T.op("pe", lambda e, kc=kc, c=c: e.matmul(ps[6][0:64, 2 * c:2 * c + 2], lhsT=sq[:, kc, c * 64:(c + 1) * 64], rhs=ones[:, 0:2],
                                                             start=(kc == 0), stop=(kc == KC - 1)),
                         reads=["ones", "sq"], writes=["ps6"])
            self.rstd_from(rtok[0:64], ps[6][0:64, 0:8], ttok[0:64], 1.0 / D, EPS, ["ps6"], "ttok", "rtok")

            def inproj(col0, M, dst, dkey, extra_reads=()):
                p, pk = pbank()
                for kc in range(KC):
                    T.op("pe", lambda e, kc=kc, p=p: e.matmul(p[0:M, 0:TS], lhsT=win[:, kc, col0:col0 + M], rhs=xb[:, kc, :],
                                                             start=(kc == 0), stop=(kc == KC - 1)),
                         reads=["win", "xb"], writes=[pk])
                T.op("dve", lambda e, p=p: e.tensor_tensor(out=dst, in0=p[0:M, 0:TS], in1=rs0[0:M], op=ALU.mult),
                     reads=[pk, "rs0"], writes=[dkey])

            T.mute = 'Q' not in A1PARTS
            for j in range(3):
                inproj(O_CQ + j * 128, 128, cq[:, j, :], ("cq", j))
            T.op("act", lambda e: e.activation(out=cqs, in_=cq, func=AF.Square), reads=[("cq", 0), ("cq", 1), ("cq", 2)], writes=["cqs"])
            T.op("dve", lambda e: e.tensor_tensor(out=cqb, in0=cq, in1=gq1.unsqueeze(2).to_broadcast([128, 3, TS]), op=ALU.mult),
                 reads=[("cq", 0), ("cq", 1), ("cq", 2), "vec"], writes=["cqb"])
            for j in range(3):
                T.op("pe", lambda e, j=j: e.matmul(ps[7][:, 0:TS], lhsT=ones, rhs=cqs[:, j, :], start=(j == 0), stop=(j == 2)),
                     reads=["ones", "cqs"], writes=["ps7"])
            self.rstd_from(epsq, ps[7][:, 0:TS], lt0, 1.0 / 384, EPS, ["ps7"], "lt0", "epsq")
            T.op("dve", lambda e: e.tensor_tensor(out=cqb, in0=cqb, in1=epsq.unsqueeze(1).to_broadcast([128, 3, TS]), op=ALU.mult),
                 reads=["cqb", "epsq"], writes=["cqb"])
            for hp in range(4):
                pa, pak = pbank()
                pb, pbk = pbank()
                for hh in range(2):
                    h = hp * 2 + hh
                    for j in range(3):
                        T.op("pe", lambda e, j=j, h=h, hh=hh, pa=pa: e.matmul(pa[0:96, hh * TS:(hh + 1) * TS], lhsT=wuq[:, j, h * 96:(h + 1) * 96], rhs=cqb[:, j, :],
                                                                            start=(j == 0), stop=(j == 2)),
                             reads=["wuq", "cqb"], writes=[pak])
                for hh in range(2):
                    h = hp * 2 + hh
                    for j in range(3):
                        T.op("pe", lambda e, j=j, h=h, hh=hh, pb=pb: e.matmul(pb[0:96, hh * TS:(hh + 1) * TS], lhsT=wuqs[:, j, h * 96:(h + 1) * 96], rhs=cqb[:, j, :],
                                                                            start=(j == 0), stop=(j == 2)),
                             reads=["wuqs", "cqb"], writes=[pbk])
                if QSTAGE < 1:
                    continue
                g0, g1, g2, g3 = gen[0], gen[1], gen[2], gen[3]
                g0f = g0.rearrange("p a b -> p (a b)"); g1f = g1.rearrange("p a b -> p (a b)")
                g2f = g2.rearrange("p a b -> p (a b)"); g3f = g3.rearrange("p a b -> p (a b)")
                sqp = ksq[0:96, 0:2, :].rearrange("p a b -> p (a b)")
                T.op("act", lambda e, pa=pa, sqp=sqp: e.activation(out=sqp, in_=pa[0:96, :], func=AF.Square), reads=[pak], writes=["ksq"])
                T.op("pe", lambda e, sqp=sqp: e.matmul(ps[5][0:96, :], lhsT=ones[0:96, 0:96], rhs=sqp, start=True, stop=True),
                     reads=["ones", "ksq"], writes=["ps5"])
                self.rstd_from(g1f[0:96], ps[5][0:96, :], g0f[0:96], 1.0 / 96, EPS, ["ps5"], "g0", "g1")
                if QSTAGE < 2:
                    continue
                csb = cs[0:96].unsqueeze(1).to_broadcast([96, 2, TS])
                snb = sn[0:96].unsqueeze(1).to_broadcast([96, 2, TS])
                for hh in range(2):
                    T.op("dve", lambda e, pa=pa, g2=g2, hh=hh, cs=cs: e.scalar_tensor_tensor(out=g2[0:96, hh, :], in0=pa[0:96, hh * TS:(hh + 1) * TS], scalar=gqn, in1=cs[0:96],
                                                                                       op0=ALU.mult, op1=ALU.mult),
                         reads=[pak, csk, "vec"], writes=["g2"])
                    T.op("dve", lambda e, pb=pb, g3=g3, hh=hh, sn=sn: e.scalar_tensor_tensor(out=g3[0:96, hh, :], in0=pb[0:96, hh * TS:(hh + 1) * TS], scalar=gqns, in1=sn[0:96],
                                                                                       op0=ALU.mult, op1=ALU.mult),
                         reads=[pbk, snk, "vec"], writes=["g3"])
                T.op("dve", lambda e, g2=g2, g3=g3: e.tensor_tensor(out=g2[0:96], in0=g2[0:96], in1=g3[0:96], op=ALU.add), reads=["g2", "g3"], writes=["g2"])
                T.op("dve", lambda e, g2=g2, g1=g1, hp=hp: e.tensor_tensor(out=qTo[0:96, hp * 2:hp * 2 + 2, :], in0=g2[0:96], in1=g1[0:96], op=ALU.mult),
                     reads=["g2", "g1"], writes=["qTo"])
            T.dma("sp", "qst", lambda e, b=b, s0=s0: e.dma_start(out=self.q_s[b, :, :, s0:s0 + TS].rearrange("h p n -> p h n"), in_=qTo[0:96]),
                  reads=["qTo"], writes=[("q_s", b)])

            T.mute = 'K' not in A1PARTS
            for j in range(2):
                inproj(O_CKV + j * 128, 128, ckv[:, j, :], ("ckv", j))
            T.op("act", lambda e: e.activation(out=ckvs, in_=ckv, func=AF.Square), reads=[("ckv", 0), ("ckv", 1)], writes=["ckvs"])
            for j in range(2):
                T.op("pe", lambda e, j=j: e.matmul(ps[7][:, 0:TS], lhsT=ones, rhs=ckvs[:, j, :], start=(j == 0), stop=(j == 1)),
                     reads=["ones", "ckvs"], writes=["ps7"])
            self.rstd_from(rskv, ps[7][:, 0:TS], lt0, 1.0 / 256, EPS, ["ps7"], "lt0", "rskv")
            T.op("dve", lambda e: e.tensor_tensor(out=ckv, in0=ckv, in1=rskv.unsqueeze(1).to_broadcast([128, 2, TS]), op=ALU.mult),
                 reads=[("ckv", 0), ("ckv", 1), "rskv"], writes=[("ckv", 0), ("ckv", 1)])
            T.op("dve", lambda e: e.tensor_tensor(out=ckvb, in0=ckv, in1=gkv.unsqueeze(2).to_broadcast([128, 2, TS]), op=ALU.mult),
                 reads=[("ckv", 0), ("ckv", 1), "vec"], writes=["ckvb"])
            inproj(O_KR, 64, krt[0:64], "krt")
            T.op("act", lambda e: e.activation(out=kcat[64:96], in_=krt[0:32].unsqueeze(1).to_broadcast([32, 8, TS]), func=AF.Copy),
                 reads=["krt"], writes=["kcat_r"])
            T.op("act", lambda e: e.activation(out=ksw[64:96], in_=krt[32:64], func=AF.Copy), reads=["krt"], writes=["ksw"])
            for hp in range(4):
                p, pk = pbank()
                for hh in range(2):
                    h = hp * 2 + hh
                    for j in range(2):
                        T.op("pe", lambda e, j=j, h=h, hh=hh, p=p: e.matmul(p[0:64, hh * TS:(hh + 1) * TS], lhsT=wuk[:, j, h * 64:(h + 1) * 64], rhs=ckvb[:, j, :],
                                                                          start=(j == 0), stop=(j == 1)),
                             reads=["wuk", "ckvb"], writes=[pk])
                T.op("act", lambda e, p=p, hp=hp: e.activation(out=kcat[0:64, hp * 2:hp * 2 + 2, :], in_=p[0:64, :].rearrange("p (a b) -> p a b", a=2), func=AF.Copy),
                     reads=[pk], writes=[("kcat_n", hp)])
            kcat_keys = ["kcat_r"] + [("kcat_n", hp) for hp in range(4)]
            T.op("act", lambda e: e.activation(out=ksq[0:96], in_=kcat[0:96], func=AF.Square), reads=kcat_keys, writes=["ksq"])
            T.op("dve", lambda e, sn=sn: e.scalar_tensor_tensor(out=kbt[0:96], in0=ksw[0:96], scalar=gkns, in1=sn[0:96], op0=ALU.mult, op1=ALU.mult),
                 reads=["ksw", snk, "vec"], writes=["kbt"])
            for hp in range(4):
                g0, g1, g2 = gen[0], gen[1], gen[2]
                g0f = g0.rearrange("p a b -> p (a b)"); g1f = g1.rearrange("p a b -> p (a b)")
                T.op("pe", lambda e, hp=hp: e.matmul(ps[5][0:96, :], lhsT=ones[0:96, 0:96], rhs=ksq[0:96, hp * 2:hp * 2 + 2, :].rearrange("p a b -> p (a b)"),
                                                    start=True, stop=True),
                     reads=["ones", "ksq"], writes=["ps5"])
                self.rstd_from(g1f[0:96], ps[5][0:96, :], g0f[0:96], 1.0 / 96, EPS, ["ps5"], "g0", "g1")
                csb = cs[0:96].unsqueeze(1).to_broadcast([96, 2, TS])
                T.op("dve", lambda e, g2=g2, hp=hp, csb=csb: e.scalar_tensor_tensor(out=g2[0:96], in0=kcat[0:96, hp * 2:hp * 2 + 2, :], scalar=gkn, in1=csb,
                                                                                op0=ALU.mult, op1=ALU.mult),
                     reads=kcat_keys + [csk, "vec"], writes=["g2"])
                T.op("dve", lambda e, g2=g2: e.tensor_tensor(out=g2[0:96], in0=g2[0:96], in1=kbt[0:96].unsqueeze(1).to_broadcast([96, 2, TS]), op=ALU.add),
                     reads=["g2", "kbt"], writes=["g2"])
                T.op("dve", lambda e, g2=g2, g1=g1, hp=hp: e.tensor_tensor(out=kTo[0:96, hp * 2:hp * 2 + 2, :], in0=g2[0:96], in1=g1[0:96], op=ALU.mult),
                     reads=["g2", "g1"], writes=["kTo"])
            T.dma("sp", "kst", lambda e, b=b, s0=s0: e.dma_start(out=self.k_s[b, :, :, s0:s0 + TS].rearrange("h p n -> p h n"), in_=kTo[0:96]),
                  reads=["kTo"], writes=[("k_s", b)])
            for blk in range(2):
                p, pk = pbank()
                for j in range(2):
                    T.op("pe", lambda e, j=j, blk=blk, p=p: e.matmul(p[:], lhsT=ckvb[:, j, blk * 128:(blk + 1) * 128], rhs=wuv[:, j, :],
                                                                  start=(j == 0), stop=(j == 1)),
                         reads=["wuv", "ckvb"], writes=[pk])
                T.op("act", lambda e, p=p, blk=blk: e.activation(out=vto[:, blk, :, 0:64], in_=p[:].rearrange("p (h d) -> p h d", h=8), func=AF.Copy),
                     reads=[pk], writes=["vto"])
            T.dma("sp", "vst", lambda e, b=b, s0=s0: e.dma_start(out=self.v_s[b, s0:s0 + TS, :].rearrange("(k p) c -> p k c", p=128),
                                                               in_=vto.rearrange("p k h w -> p k (h w)")),
                  reads=["vto"], writes=[("v_s", b)])

            T.mute = 'H' not in A1PARTS
            for h in range(4):
                inproj(O_HQ + h * 64, 64, hq[0:64, h, :], ("hq", h))
                inproj(O_HF + h * 64, 64, hf[0:64, h, :], ("hf", h))
                inproj(O_HG + h * 64, 64, hg[0:64, h, :], ("hg", h))
            hqk = [("hq", h) for h in range(4)]; hfk = [("hf", h) for h in range(4)]; hgk = [("hg", h) for h in range(4)]
            for c in range(4):
                p, pk = pbank()
                for kc in range(KC):
                    T.op("pe", lambda e, kc=kc, c=c, p=p: e.matmul(p[0:64, 0:256], lhsT=xb[:, kc, c * 64:(c + 1) * 64], rhs=win[:, kc, O_HI:O_HI + 256],
                                                                start=(kc == 0), stop=(kc == KC - 1)),
                         reads=["win", "xb"], writes=[pk])
                T.op("dve", lambda e, p=p, c=c: e.tensor_scalar(out=vtok[0:64, c, :], in0=p[0:64, 0:256], scalar1=rtok[0:64, 2 * c:2 * c + 1], scalar2=None, op0=ALU.mult),
                     reads=[pk, "rtok"], writes=["vtok"])
            E, L1, L2, Bc, Wk = hgen
            H = slice(0, 64)
            lbb = lbv[0:64].unsqueeze(2).to_broadcast([64, 4, TS])
            T.op("act", lambda e: e.activation(out=E[H], in_=hf[H], func=AF.Exp, scale=-1.0), reads=hfk, writes=["E"])
            T.op("act", lambda e: e.activation(out=L1[H], in_=E[H], func=AF.Ln, scale=1.0, bias=1.0), reads=["E"], writes=["L1"])
            T.op("dve", lambda e: e.tensor_tensor(out=E[H], in0=E[H], in1=lbb, op=ALU.mult), reads=["E", "lbv"], writes=["E"])
            T.op("act", lambda e: e.activation(out=L2[H], in_=E[H], func=AF.Ln, scale=1.0, bias=1.0), reads=["E"], writes=["L2"])
            T.op("dve", lambda e: e.tensor_tensor(out=L2[H], in0=L2[H], in1=L1[H], op=ALU.subtract), reads=["L1", "L2"], writes=["L2"])
            T.op("dve", lambda e: e.tensor_tensor_scan(out=Bc[H].rearrange("p a b -> p (a b)"), data0=m01[H].rearrange("p a b -> p (a b)"),
                                                       data1=L2[H].rearrange("p a b -> p (a b)"), initial=0.0, op0=ALU.mult, op1=ALU.add),
                 reads=["m01", "L2"], writes=["Bc"])
            B4 = Bc[H].rearrange("p h (c t) -> p h c t", t=64)
            blast, cmid, e1, e2, ec, scx = sc8
            T.op("dve", lambda e: e.tensor_copy(out=blast[H], in_=B4[:, :, :, 63]), reads=["Bc"], writes=["blast"])
            T.op("dve", lambda e: e.tensor_copy(out=cmid[H], in_=B4[:, :, :, 31]), reads=["Bc"], writes=["cmid"])
            T.op("act", lambda e: e.activation(out=e1[H], in_=blast[H], func=AF.Exp), reads=["blast"], writes=["e1"])
            T.op("act", lambda e: e.activation(out=ec[H], in_=cmid[H], func=AF.Exp), reads=["cmid"], writes=["ec"])
            T.op("dve", lambda e: e.tensor_tensor(out=scx[H], in0=blast[H], in1=cmid[H], op=ALU.subtract), reads=["blast", "cmid"], writes=["scx"])
            T.op("act", lambda e: e.activation(out=e2[H], in_=scx[H], func=AF.Exp), reads=["scx"], writes=["e2"])
            T.op("dve", lambda e: e.tensor_tensor(out=B4, in0=B4, in1=cmid[H].unsqueeze(3).to_broadcast([64, 4, 4, 64]), op=ALU.subtract),
                 reads=["Bc", "cmid"], writes=["Bc"])
            T.op("act", lambda e: e.activation(out=L1[H], in_=L2[H], func=AF.Exp), reads=["L2"], writes=["L1"])
            T.op("dve", lambda e: e.tensor_scalar(out=L1[H], in0=L1[H], scalar1=-1.0, scalar2=1.0, op0=ALU.mult, op1=ALU.add), reads=["L1"], writes=["L1"])
            T.op("act", lambda e: e.activation(out=Wk[H], in_=Bc[H], func=AF.Exp, scale=-1.0), reads=["Bc"], writes=["Wk"])
            T.op("dve", lambda e: e.tensor_tensor(out=ktl[H], in0=L1[H], in1=Wk[H], op=ALU.mult), reads=["L1", "Wk"], writes=["ktl"])
            T.op("act", lambda e: e.activation(out=E[H], in_=hq[H], func=AF.Exp, scale=-1.0), reads=hqk, writes=["E"])
            T.op("act", lambda e: e.activation(out=E[H], in_=E[H], func=AF.Ln, scale=1.0, bias=1.0), reads=["E"], writes=["E"])
            T.op("dve", lambda e: e.tensor_tensor(out=E[H], in0=Bc[H], in1=E[H], op=ALU.subtract), reads=["E", "Bc"], writes=["E"])
            T.op("act", lambda e: e.activation(out=Wk[H], in_=E[H], func=AF.Exp), reads=["E"], writes=["Wk"])
            T.op("dve", lambda e: e.tensor_tensor(out=qtl[H], in0=hq[H], in1=Wk[H], op=ALU.mult), reads=hqk + ["Wk"], writes=["qtl"])
            T.op("act", lambda e: e.activation(out=L2[H], in_=hg[H], func=AF.Exp, scale=-1.0), reads=hgk, writes=["L2"])
            T.op("act", lambda e: e.activation(out=L2[H], in_=L2[H], func=AF.Ln, scale=1.0, bias=1.0), reads=["L2"], writes=["L2"])
            T.op("act", lambda e: e.activation(out=L2[H], in_=L2[H], func=AF.Exp, scale=-1.0), reads=["L2"], writes=["L2"])
            T.op("dve", lambda e: e.tensor_tensor(out=L2[H], in0=L2[H], in1=hg[H], op=ALU.mult), reads=["L2"] + hgk, writes=["L2"])
            ptr = ps[4][:].bitcast(BF16)
            for c in range(4):
                for h in range(4):
                    T.op("pe", lambda e, c=c, h=h: e.transpose(ptr[0:64, (c * 4 + h) * 64:(c * 4 + h + 1) * 64], ktl[0:64, h, c * 64:(c + 1) * 64], ident[0:64, 0:64]),
                         reads=["ktl", "ident"], writes=["ps4"])
            T.op("act", lambda e: e.activation(out=ktok[H].rearrange("p a b c -> p (a b c)"), in_=ptr[0:64, 0:1024], func=AF.Copy), reads=["ps4"], writes=["ktok"])
            for h in range(4):
                pb_ = ps[5] if h < 2 else ps[6]
                for c in range(4):
                    T.op("pe", lambda e, h=h, c=c, pb_=pb_: e.matmul(pb_[0:64, ((h % 2) * 4 + c) * 64:((h % 2) * 4 + c + 1) * 64],
                                                                   lhsT=ktl[0:64, h, c * 64:(c + 1) * 64], rhs=qtl[0:64, h, c * 64:(c + 1) * 64], start=True, stop=True),
                         reads=["ktl", "qtl"], writes=["ps5" if h < 2 else "ps6"])
            for hh2 in range(2):
                pb_ = ps[5 + hh2]
                T.op("dve", lambda e, hh2=hh2, pb_=pb_: e.tensor_tensor(out=attb[0:64, hh2 * 2:hh2 * 2 + 2], in0=pb_[0:64, :].rearrange("p (a b c) -> p a b c", a=2, b=4),
                                                                      in1=tri[0:64, 0:2], op=ALU.mult),
                     reads=[f"ps{5 + hh2}", "tri"], writes=["attb"])
            for h in range(4):
                pb_ = ps[7] if h < 2 else ps[4]
                for c in range(4):
                    T.op("pe", lambda e, h=h, c=c, pb_=pb_: e.matmul(pb_[0:64, ((h % 2) * 4 + c) * 64:((h % 2) * 4 + c + 1) * 64],
                                                                   lhsT=ktok[0:64, c, h, :], rhs=vtok[0:64, c, h * 64:(h + 1) * 64], start=True, stop=True),
                         reads=["ktok", "vtok"], writes=["ps7" if h < 2 else "ps4"])
            U7 = ps[7][0:64, :].rearrange("p (a c d) -> p a c d", a=2, c=4)
            U4 = ps[4][0:64, :].rearrange("p (a c d) -> p a c d", a=2, c=4)
            for c in range(4):
                T.op("dve", lambda e, c=c: e.tensor_tensor(out=Ssc[0:64, :, c, :], in0=Sst[H], in1=ec[0:64, :, c:c + 1].to_broadcast([64, 4, 64]), op=ALU.mult),
                     reads=["Sst", "ec"], writes=[("Ssc", c)])
                T.op("dve", lambda e, c=c: e.tensor_tensor(out=tmpU[0:64, 0:2, :], in0=U7[:, :, c, :], in1=e2[0:64, 0:2, c:c + 1].to_broadcast([64, 2, 64]), op=ALU.mult),
                     reads=["ps7", "e2"], writes=["tmpU0"])
                T.op("dve", lambda e, c=c: e.tensor_tensor(out=tmpU[0:64, 2:4, :], in0=U4[:, :, c, :], in1=e2[0:64, 2:4, c:c + 1].to_broadcast([64, 2, 64]), op=ALU.mult),
                     reads=["ps4", "e2"], writes=["tmpU1"])
                T.op("dve", lambda e, c=c: e.tensor_tensor(out=Sst[H], in0=Sst[H], in1=e1[0:64, :, c:c + 1].to_broadcast([64, 4, 64]), op=ALU.mult),
                     reads=["Sst", "e1", ("Ssc", c)], writes=["Sst"])
                T.op("dve", lambda e: e.tensor_tensor(out=Sst[H], in0=Sst[H], in1=tmpU[H], op=ALU.add), reads=["Sst", "tmpU0", "tmpU1"], writes=["Sst"])
            for h in range(4):
                p, pk = pbank()
                for c in range(4):
                    T.op("pe", lambda e, h=h, c=c, p=p: e.matmul(p[0:64, c * 64:(c + 1) * 64], lhsT=vtok[0:64, c, h * 64:(h + 1) * 64], rhs=attb[0:64, h, c, :],
                                                              start=True, stop=False),
                         reads=["vtok", "attb"], writes=[pk])
                    T.op("pe", lambda e, h=h, c=c, p=p: e.matmul(p[0:64, c * 64:(c + 1) * 64], lhsT=Ssc[0:64, h, c, :], rhs=qtl[0:64, h, c * 64:(c + 1) * 64],
                                                              start=False, stop=True),
                         reads=[("Ssc", c), "qtl"], writes=[pk])
                T.op("act", lambda e, p=p, h=h: e.activation(out=osq[0:64, h, :], in_=p[0:64, 0:TS], func=AF.Square), reads=[pk], writes=[("osq", h)])
                T.op("pe", lambda e, h=h: e.matmul(ps[5][0:64, 0:TS], lhsT=ones[0:64, 0:64], rhs=osq[0:64, h, :], start=True, stop=True),
                     reads=["ones", ("osq", h), "attb"], writes=["ps5"])
                g0, g1 = gen[4], gen[5]
                g0f = g0.rearrange("p a b -> p (a b)"); g1f = g1.rearrange("p a b -> p (a b)")
                self.rstd_from(g1f[0:64, 0:TS], ps[5][0:64, 0:TS], g0f[0:64, 0:TS], 1.0 / 64, EPS, ["ps5"], "g4", "g5")
                T.op("dve", lambda e, p=p, h=h, g0f=g0f, g1f=g1f: e.scalar_tensor_tensor(out=g0f[0:64, 0:TS], in0=p[0:64, 0:TS], scalar=go[:, h:h + 1], in1=g1f[0:64, 0:TS],
                                                                                     op0=ALU.mult, op1=ALU.mult),
                     reads=[pk, "g5", "vec"], writes=["g4"])
                T.op("dve", lambda e, h=h, g0f=g0f: e.tensor_tensor(out=yho[0:64, h, :], in0=g0f[0:64, 0:TS], in1=L2[0:64, h, :], op=ALU.mult),
                     reads=["g4", "L2"], writes=["yho"])
            T.dma("sp", "yhst", lambda e, b=b, s0=s0: e.dma_start(out=self.yh_s[b, :, :, s0:s0 + TS].rearrange("h p n -> p h n"), in_=yho[0:64]),
                  reads=["yho"], writes=[("yh_s", b)])

            T.mute = 'C' not in A1PARTS
            for j in range(2):
                inproj(O_CB + j * 128, 128, cbt[:, j, :], ("cb", j))
                inproj(O_CC + j * 128, 128, cct[:, j, :], ("cc", j))
                inproj(O_CX + j * 128, 128, cxt[:, j, :], ("cx", j))
            ck = [("cc", 0), ("cc", 1), ("cx", 0), ("cx", 1)]
            T.op("dve", lambda e: e.tensor_tensor(out=ubuf[:, :, 2:TS + 2], in0=cct, in1=cxt, op=ALU.mult), reads=ck + ["ucarry"], writes=["ubuf"])
            for j in range(2):
                T.op("dve", lambda e, j=j: e.tensor_scalar(out=yacc[:, j, :], in0=ubuf[:, j, 2:TS + 2], scalar1=cw[:, j * 3 + 2:j * 3 + 3], scalar2=None, op0=ALU.mult),
                     reads=["ubuf", "vec"], writes=[("yacc", j)])
                T.op("dve", lambda e, j=j: e.scalar_tensor_tensor(out=yacc[:, j, :], in0=ubuf[:, j, 1:TS + 1], scalar=cw[:, j * 3 + 1:j * 3 + 2], in1=yacc[:, j, :],
                                                                  op0=ALU.mult, op1=ALU.add),
                     reads=["ubuf", "vec", ("yacc", j)], writes=[("yacc", j)])
                T.op("dve", lambda e, j=j: e.scalar_tensor_tensor(out=yacc[:, j, :], in0=ubuf[:, j, 0:TS], scalar=cw[:, j * 3:j * 3 + 1], in1=yacc[:, j, :],
                                                                  op0=ALU.mult, op1=ALU.add),
                     reads=["ubuf", "vec", ("yacc", j)], writes=[("yacc", j)])
            T.op("dve", lambda e: e.tensor_tensor(out=yco, in0=yacc, in1=cbt, op=ALU.mult),
                 reads=[("yacc", 0), ("yacc", 1), ("cb", 0), ("cb", 1)], writes=["yco"])
            T.op("dve", lambda e: e.tensor_copy(out=ubuf[:, :, 0:2], in_=ubuf[:, :, TS:TS + 2]), reads=["ubuf"], writes=["ucarry", "ubuf"])
            T.dma("sp", "ycst", lambda e, b=b, s0=s0: e.dma_start(out=self.yc_s[b, :, s0:s0 + TS].rearrange("(j p) n -> p j n", p=128), in_=yco),
                  reads=["yco"], writes=[("yc_s", b)])
            T.mute = False

    def phase_A2(self, l, src):
        T, A, nc, ps = self.T, self.A, self.nc, self.ps
        NB, S = self.NB, self.S
        TS = 512
        NKB = S // 128
        A.reset()
        al = A.alloc
        kc_ = al([8, S], BF16)
        vc_ = al([NKB, VROW], BF16)
        wom = al([8, D], BF16)
        woh = al([4, D], BF16)
        woc = al([2, D], BF16)
        cm = al([4, 512], BF16)
        ident = al([128], BF16)
        cst = al([128 + 64 + 256], F32)
        qt = [al([8, TS], BF16) for _ in range(2)]
        yh = al([4, TS], BF16)
        yc = al([2, TS], BF16)
        ym = al([8, TS], BF16)
        PT = [al([TS], BF16) for _ in range(4)]
        rc2 = [al([TS], F32) for _ in range(2)]
        xr = [al([TS], F32) for _ in range(3)]
        yT = self.yT
        T.dma("sp", "cst", lambda e: e.dma_start(out=cst, in_=self.consts), writes=["cst"])
        T.op("dve", lambda e: e.tensor_copy(out=ident, in_=cst[:, 0:128]), reads=["cst"], writes=["ident"])
        T.dma("pool", "cm", lambda e: e.dma_start(out=cm.rearrange("p a b -> p (a b)"), in_=self.cmask), writes=["cm"])
        self.wload(wom[0:64].rearrange("p a b -> p (a b)"), self.w_om[l], "wom", 4096, 8 * D)
        self.wload(woh[0:64].rearrange("p a b -> p (a b)"), self.w_oh[l], "woh", 4096, 4 * D)
        self.wload(woc.rearrange("p a b -> p (a b)"), self.w_oc[l], "woc", 2048, 2 * D)
        scale = 1.0 / math.sqrt(96.0)
        tps = S // TS
        sci = 0
        xri = 0
        for b in range(NB):
            for h in range(8):
                T.dma("sp", "kcl", lambda e, b=b, h=h: e.dma_start(out=kc_[0:96, h, :], in_=self.k_s[b, h]), reads=[("k_s", b)], writes=["kc"])
            KS = min(8, NKB)
            for k0 in range(0, NKB, KS):
                T.dma("sp", "vcl", lambda e, b=b, k0=k0: e.dma_start(out=vc_[:, k0:k0 + KS, :], in_=self.v_s[b, k0 * 128:(k0 + KS) * 128, :].rearrange("(k p) c -> p k c", p=128)),
                      reads=[("v_s", b)], writes=["vc"])
            vc4 = vc_.rearrange("p k (h w) -> p k h w", w=128)
            for i in range(tps):
                s0 = i * TS
                t = b * tps + i
                c0 = b * S + s0
                q = qt[t % 2]; qk = f"qt{t % 2}"
                T.dma("sp", qk, lambda e, q=q, b=b, s0=s0: e.dma_start(out=q[0:96], in_=self.q_s[b, :, :, s0:s0 + TS].rearrange("h p n -> p h n")),
                      reads=[("q_s", b)], writes=[qk])
                T.dma("sp", "yhl", lambda e, b=b, s0=s0: e.dma_start(out=yh[0:64], in_=self.yh_s[b, :, :, s0:s0 + TS].rearrange("h p n -> p h n")),
                      reads=[("yh_s", b)], writes=["yh"])
                T.dma("sp", "ycl", lambda e, b=b, s0=s0: e.dma_start(out=yc, in_=self.yc_s[b, :, s0:s0 + TS].rearrange("(j p) n -> p j n", p=128)),
                      reads=[("yc_s", b)], writes=["yc"])
                nkb = 4 * i + 4
                LA = 2
                ND = 0
                steps = [(h, kb) for h in range(8) for kb in range(nkb)]
                slots = {}
                pending_norm = []

                def emit_qk(idx):
                    nonlocal sci
                    h, kb = steps[idx]
                    sb_ = sci % 4; sci += 1
                    pt = PT[sb_]; ptk = f"PT{sb_}"
                    slots[idx] = (pt, ptk)
                    dj = kb - 4 * i
                    T.op("pe", lambda e, h=h, kb=kb, sb_=sb_, q=q, dj=dj: e.matmul(ps[sb_][:], lhsT=kc_[0:96, h, kb * 128:(kb + 1) * 128], rhs=q[0:96, h, :],
                                                                                start=True, stop=(dj < 0)),
                         reads=["kc", qk], writes=[f"ps{sb_}"])
                    if dj >= 0:
                        T.op("pe", lambda e, sb_=sb_, dj=dj: e.matmul(ps[sb_][:], lhsT=ident, rhs=cm[:, dj, :], start=False, stop=True),
                             reads=["ident", "cm"], writes=[f"ps{sb_}"])
                    T.op("act", lambda e, pt=pt, sb_=sb_: e.activation(out=pt, in_=ps[sb_][:], func=AF.Exp, scale=scale),
                         reads=[f"ps{sb_}"], writes=[ptk])

                def emit_pv(idx):
                    h, kb = steps[idx]
                    pt, ptk = slots.pop(idx)
                    po = ps[4 + h % 2]; pok = f"ps{4 + h % 2}"
                    T.op("pe", lambda e, h=h, kb=kb, pt=pt, po=po, nkb=nkb, vc4=vc4: e.matmul(po[:], lhsT=vc4[:, kb, h, :], rhs=pt, start=(kb == 0), stop=(kb == nkb - 1)),
                         reads=["vc", ptk], writes=[pok])
                    if kb == nkb - 1:
                        return h
                    return None

                def emit_norm(h):
                    po = ps[4 + h % 2]; pok = f"ps{4 + h % 2}"
                    rch = rc2[h % 2]; rck = f"rc{h % 2}"
                    T.op("dve", lambda e, po=po, rch=rch: e.reciprocal(out=rch[64:128], in_=po[64:128, :]), reads=[pok], writes=[rck])
                    T.op("dve", lambda e, po=po, h=h, rch=rch: e.tensor_tensor(out=ym[0:64, h, :], in0=po[0:64, :], in1=rch[64:128], op=ALU.mult),
                         reads=[pok, rck], writes=[("ym", h)])

                nst = len(steps)
                for idx in range(nst + LA):
                    if idx < nst:
                        emit_qk(idx)
                    if idx - LA >= 0:
                        hdone = emit_pv(idx - LA)
                        if hdone is not None:
                            pending_norm.append((idx + ND, hdone))
                    while pending_norm and pending_norm[0][0] <= idx:
                        emit_norm(pending_norm.pop(0)[1])
                for _, hh_ in pending_norm:
                    emit_norm(hh_)
                for oc in range(8):
                    p = ps[6 + oc % 2]; pk = f"ps{6 + oc % 2}"
                    x_ = xr[xri % 3]; xk = f"xrA{xri % 3}"; xri += 1
                    T.dma("sp", xk, lambda e, x_=x_, oc=oc, c0=c0: e.dma_start(out=x_, in_=src[oc * 128:(oc + 1) * 128, c0:c0 + TS]),
                          reads=[("y", c0 // 512)], writes=[xk])
                    for h in range(8):
                        T.op("pe", lambda e, h=h, oc=oc, p=p: e.matmul(p[:], lhsT=wom[0:64, h, oc * 128:(oc + 1) * 128], rhs=ym[0:64, h, :], start=(h == 0), stop=False),
                             reads=["wom", ("ym", h)], writes=[pk])
                    for h in range(4):
                        T.op("pe", lambda e, h=h, oc=oc, p=p: e.matmul(p[:], lhsT=woh[0:64, h, oc * 128:(oc + 1) * 128], rhs=yh[0:64, h, :], start=False, stop=False),
                             reads=["woh", "yh"], writes=[pk])
                    for j in range(2):
                        T.op("pe", lambda e, j=j, oc=oc, p=p: e.matmul(p[:], lhsT=woc[:, j, oc * 128:(oc + 1) * 128], rhs=yc[:, j, :], start=False, stop=(j == 1)),
                             reads=["woc", "yc"], writes=[pk])
                    T.op("dve", lambda e, p=p, x_=x_: e.tensor_tensor(out=x_, in0=p[:], in1=x_, op=ALU.add), reads=[pk, xk], writes=[xk])
                    T.dma("sp", xk, lambda e, x_=x_, oc=oc, c0=c0: e.dma_start(out=yT[oc * 128:(oc + 1) * 128, c0:c0 + TS], in_=x_),
                          reads=[xk], writes=[("y", c0 // 512)])


def _kpn(w, p=128):
    K, N = w.shape
    return np.ascontiguousarray(w.reshape(K // p, p, N).transpose(1, 0, 2).reshape(p, -1))


def host_consts(S, positions):
    ident = np.eye(128, dtype=np.float32)
    tri = np.zeros((128, 64), np.float32)
    tri[0:64] = (np.arange(64)[:, None] <= np.arange(64)[None, :]).astype(np.float32)
    m01 = np.ones((128, 256), np.float32)
    m01[:, ::64] = 0.0
    consts = np.concatenate([ident, tri, m01], axis=1)
    cmask = np.zeros((128, 4, 512), np.float32)
    p = np.arange(128)[:, None]
    n = np.arange(512)[None, :]
    for j in range(4):
        cmask[:, j, :] = np.where(128 * j + p <= n, 0.0, NEG)
    inv_freq = (10000.0 ** (-np.arange(0, 32, 2, dtype=np.float32) / 32)).astype(np.float32)
    ang = positions.astype(np.float32)[:, None] * inv_freq[None, :]
    cos = np.cos(ang).astype(np.float32).T
    sin = np.sin(ang).astype(np.float32).T
    cosT = np.ones((96, S), np.float32)
    sinS = np.zeros((96, S), np.float32)
    cosT[64:80] = cos
    cosT[80:96] = cos
    sinS[64:80] = -sin
    sinS[80:96] = sin
    return consts, cmask.reshape(128, -1), cosT, sinS


def host_weights(inp, L):
    out = {}
    perm = np.concatenate([np.arange(64), np.arange(80, 96), np.arange(64, 80)])
    w_in = np.asarray(inp["w_in"])
    kr = w_in[:, :, 640:672]
    krsw = np.concatenate([kr[:, :, 16:32], kr[:, :, 0:16]], axis=2)
    w_in2 = np.concatenate([w_in[:, :, :672], krsw, w_in[:, :, 672:]], axis=2)
    out["w_in"] = np.stack([_kpn(w_in2[l]) for l in range(L)])
    w_uq = np.asarray(inp["w_uq"])
    out["w_uq"] = np.stack([_kpn(w_uq[l]) for l in range(L)])
    w_uqsw = w_uq.reshape(L, 384, 8, 96)[:, :, :, perm].reshape(L, 384, 768)
    out["w_uqsw"] = np.stack([_kpn(w_uqsw[l]) for l in range(L)])
    w_ukv = np.asarray(inp["w_ukv"]).reshape(L, 256, 8, 128)
    w_uk = w_ukv[:, :, :, :64].reshape(L, 256, 512)
    w_uv = w_ukv[:, :, :, 64:].reshape(L, 256, 512)
    out["w_uk"] = np.stack([_kpn(w_uk[l]) for l in range(L)])
    out["w_uv"] = np.stack([_kpn(w_uv[l]) for l in range(L)])
    w_out = np.asarray(inp["w_out"])
    out["w_om"] = np.stack([_kpn(w_out[l, 0:512], 64) for l in range(L)])
    out["w_oh"] = np.stack([_kpn(w_out[l, 512:768], 64) for l in range(L)])
    out["w_oc"] = np.stack([_kpn(w_out[l, 768:1024]) for l in range(L)])
    for nm, key in (("w_xq", "w_xq"), ("w_xkv", "w_xkv"), ("w_xo", "w_xo"), ("w_up", "w_up"), ("w_dn", "w_down")):
        w = np.asarray(inp[key])
        out[nm] = np.stack([_kpn(w[l]) for l in range(L)])
    vecs = np.zeros((L, 128, NV), np.float32)

    def col(v):
        v = np.asarray(v)
        return v.reshape(-1, 128).T
    for l in range(L):
        vecs[l, :, V_GMIX:V_GMIX + 8] = col(inp["mix_norm_g"][l])
        vecs[l, :, V_GQ1:V_GQ1 + 3] = col(inp["mla_q_norm_g"][l])
        vecs[l, :, V_GKV:V_GKV + 2] = col(inp["mla_kv_norm_g"][l])
        gq = np.asarray(inp["mla_qn_g"][l]); gk = np.asarray(inp["mla_kn_g"][l])
        vecs[l, 0:96, V_GQN] = gq
        vecs[l, 0:96, V_GQNSW] = gq[perm]
        vecs[l, 0:96, V_GKN] = gk
        vecs[l, 0:96, V_GKNSW] = gk[perm]
        vecs[l, 0:64, V_GO:V_GO + 4] = np.asarray(inp["hgrn_o_norm_g"][l]).reshape(4, 64).T
        cwl = np.asarray(inp["conv_w"][l])
        vecs[l, :, V_CW:V_CW + 6] = cwl.reshape(3, 2, 128).transpose(2, 1, 0).reshape(128, 6)
        vecs[l, :, V_GXA:V_GXA + 8] = col(inp["xattn_norm_g"][l])
        vecs[l, :, V_GXQ] = np.asarray(inp["xq_norm_g"][l])
        vecs[l, :, V_GXK] = np.asarray(inp["xk_norm_g"][l])
        vecs[l, :, V_GMLP:V_GMLP + 8] = col(inp["mlp_norm_g"][l])
        vecs[l, :, V_GMEM:V_GMEM + 8] = col(inp["mem_norm_g"][l])
    out["vecs"] = vecs
    lb = np.asarray(inp["hgrn_lb_logits"])
    lbl = np.zeros((64, 4, 4), np.float32)
    lbl[:, :, :L] = lb.reshape(L, 4, 64).transpose(2, 1, 0)
    if L < 4:
        lbl[:, :, L:] = -1e4
    out["lbl"] = lbl.reshape(64, 16)
    return out


_CACHE = {}


def run(inputs, NB, S, L, ncores, phases="A1,A2,B,C", trace=False):
    x = np.asarray(inputs["x"], dtype=np.float32)
    mem = np.asarray(inputs["mem"], dtype=np.float32)
    MEM = mem.shape[1]
    key = (NB, S, L, MEM, phases)
    if key not in _CACHE:
        bld = Builder(NB, S, L, MEM, phases)
        bld.build()
        _CACHE[key] = bld
    bld = _CACHE[key]
    hw = host_weights(inputs, L)
    consts, cmask, cosT, sinS = host_consts(S, np.asarray(inputs["positions"]))
    in_maps = []
    for c in range(ncores):
        xs = x[c * NB:(c + 1) * NB].reshape(NB * S, D)
        m = dict(hw)
        m["xT"] = np.ascontiguousarray(xs.T)
        m["memT"] = np.ascontiguousarray(mem[c * NB:(c + 1) * NB].transpose(0, 2, 1))
        m["cosT"] = cosT
        m["sinS"] = sinS
        m["cmask"] = cmask
        m["consts"] = consts
        in_maps.append(m)
    res = run_bass_kernel_spmd(bld.nc, in_maps, core_ids=list(range(ncores)), trace=trace)
    outs = [np.ascontiguousarray(r["yT"].T).reshape(NB, S, D) for r in res.results]
    return np.concatenate(outs, axis=0), res


def kernel(**inputs):
    out, _ = run(inputs, NB=2, S=4096, L=4, ncores=8)
    return out.astype(np.float32)
```

```python
import contextlib
import math
A1PARTS = 'QKHC'
QSTAGE = 9
import numpy as np
import concourse.bass as bass
import concourse.mybir as mybir
from concourse.bass_utils import run_bass_kernel_spmd

F32 = mybir.dt.float32
BF16 = mybir.dt.bfloat16
AF = mybir.ActivationFunctionType
ALU = mybir.AluOpType
EPS = 1e-6

D = 1024
DFF = 4096
NIN = 2496
O_CQ, O_CKV, O_KR, O_HQ, O_HF, O_HI, O_HG, O_CB, O_CC, O_CX = 0, 384, 640, 704, 960, 1216, 1472, 1728, 1984, 2240
NV = 56
V_GMIX, V_GQ1, V_GKV, V_GQN, V_GQNSW, V_GKN, V_GKNSW, V_GO, V_CW, V_GXA, V_GXQ, V_GXK, V_GMLP, V_GMEM = \
    0, 8, 11, 13, 14, 15, 16, 17, 21, 27, 35, 36, 37, 45
VROW = 8 * 128
NEG = -30000.0

ENG_NAMES = ("pe", "act", "dve", "pool", "sp")


class Op:
    __slots__ = ("eng", "fn", "deps", "odeps", "cost", "inc_idx", "dsem", "dval", "needed")

    def __init__(self, eng, fn):
        self.eng = eng
        self.fn = fn
        self.deps = []
        self.odeps = []
        self.cost = None
        self.inc_idx = None
        self.dsem = None
        self.dval = None
        self.needed = False


class Tracker:
    def __init__(self, nc, es):
        self.nc = nc
        self.es = es
        self.ops = []
        self.last_w = {}
        self.readers = {}
        self.dsem_cnt = {}
        self.last_dma = {}
        self.default_cost = {"pe": 150, "act": 600, "dve": 500, "pool": 1000, "sp": 100}
        self.seg_start = 0
        self.engobj = {"pe": nc.tensor, "act": nc.scalar, "dve": nc.vector, "pool": nc.gpsimd, "sp": nc.sync}

    mute = False
    do_schedule = True

    def op(self, eng, fn, reads=(), writes=(), cost=None):
        if self.mute:
            return None
        o = Op(eng, fn)
        o.cost = cost if cost is not None else self.default_cost.get(eng, 500)
        self._deps(o, reads, writes)
        self.ops.append(o)
        return o

    def dma(self, queue, dsem, fn, reads=(), writes=()):
        if self.mute:
            return None
        o = Op(queue, fn)
        o.dsem = dsem
        self.dsem_cnt[dsem] = self.dsem_cnt.get(dsem, 0) + 16
        o.dval = self.dsem_cnt[dsem]
        self._deps(o, reads, writes)
        prev = self.last_dma.get(dsem)
        if prev is not None:
            o.odeps.append(prev)
        self.last_dma[dsem] = o
        self.ops.append(o)
        return o

    def _deps(self, o, reads, writes):
        deps = []
        for k in reads:
            w = self.last_w.get(k)
            if w is not None:
                deps.append((w, False))
            if isinstance(k, str) and k[:2] == "ps" and k[2:].isdigit():
                for r in self.readers.get(k, ()):
                    if r.eng != o.eng:
                        deps.append((r, False))
        for k in writes:
            w = self.last_w.get(k)
            if w is not None:
                deps.append((w, True))
            for r in self.readers.get(k, ()):
                deps.append((r, False))
        seen = set()
        for d, waw in deps:
            if id(d) in seen or d is o:
                continue
            if d.dsem is None and o.dsem is None and d.eng == "pe" and o.eng == "pe":
                o.odeps.append(d)
                continue
            if waw and d.dsem is not None and o.dsem is not None and d.dsem == o.dsem and d.eng == o.eng:
                o.odeps.append(d)
                continue
            seen.add(id(d))
            o.deps.append(d)
            d.needed = True
        for k in writes:
            self.last_w[k] = o
            self.readers[k] = []
        for k in reads:
            lst = self.readers.setdefault(k, [])
            if o.dsem is None:
                keep = []
                for r in lst:
                    if r.dsem is None and r.eng == o.eng:
                        if r is not o:
                            o.odeps.append(r)
                    else:
                        keep.append(r)
                lst[:] = keep
            lst.append(o)


    def schedule_segment(self):
        import heapq
        seg = self.ops[self.seg_start:]
        n = len(seg)
        if n == 0:
            return
        pos = {id(o): i for i, o in enumerate(seg)}
        succ = [[] for _ in range(n)]
        indeg = [0] * n
        for i, o in enumerate(seg):
            for d in list(o.deps) + list(o.odeps):
                j = pos.get(id(d))
                if j is not None:
                    succ[j].append(i)
                    indeg[i] += 1
        ready_t = [0.0] * n
        fin = [0.0] * n
        engs = ENG_NAMES
        avail = {e: [] for e in engs}
        future = {e: [] for e in engs}
        free_t = {e: 0.0 for e in engs}
        for i, o in enumerate(seg):
            if indeg[i] == 0:
                heapq.heappush(future[o.eng], (0.0, i))
        order = []
        LAT = 150.0
        done = 0
        while done < n:
            best = None
            for e in engs:
                fu, av = future[e], avail[e]
                while fu and fu[0][0] <= free_t[e]:
                    rt, i = heapq.heappop(fu)
                    heapq.heappush(av, i)
                if av:
                    cand = (free_t[e], av[0], e, True)
                elif fu:
                    cand = (fu[0][0], fu[0][1], e, False)
                else:
                    continue
                if best is None or cand[:2] < best[:2]:
                    best = cand
            st, i, e, from_av = best
            if from_av:
                heapq.heappop(avail[e])
            else:
                heapq.heappop(future[e])
            o = seg[i]
            if o.dsem is not None:
                free_t[e] = st + 100.0
                fin[i] = st + 2500.0
            else:
                c = float(o.cost or 500)
                free_t[e] = st + c
                fin[i] = st + c
            order.append((st, i))
            done += 1
            for j in succ[i]:
                rt = fin[i] + (LAT if seg[j].eng != e or o.dsem is not None else 30.0)
                if rt > ready_t[j]:
                    ready_t[j] = rt
                indeg[j] -= 1
                if indeg[j] == 0:
                    heapq.heappush(future[seg[j].eng], (ready_t[j], j))
        order.sort()
        self.ops[self.seg_start:] = [seg[i] for _, i in order]

    def _lasts(self):
        last = {}
        for o in self.ops:
            if o.fn is None:
                continue
            last[(o.eng, o.dsem)] = o
        return list(last.values())

    def full_barrier(self):
        if self.do_schedule:
            self.schedule_segment()
        lasts = self._lasts()
        for e in ENG_NAMES:
            o = Op(e, None)
            for d in lasts:
                o.deps.append(d)
                d.needed = True
            self.ops.append(o)
        self.last_w.clear()
        self.readers.clear()
        self.seg_start = len(self.ops)

    def final_wait(self, eng="sp"):
        if self.do_schedule:
            self.schedule_segment()
        o = Op(eng, None)
        for d in self._lasts():
            o.deps.append(d)
            d.needed = True
        self.ops.append(o)

    def emit(self):
        nc = self.nc
        cnt = {e: 0 for e in ENG_NAMES}
        for o in self.ops:
            if o.dsem is None and o.needed and o.fn is not None:
                cnt[o.eng] += 1
                o.inc_idx = cnt[o.eng]
        sems = {}
        for e in ENG_NAMES:
            if cnt[e] > 0:
                sems[e] = self.es.enter_context(nc.semaphore("s_" + e))
        dsems = {}
        for name in self.dsem_cnt:
            dsems[name] = self.es.enter_context(nc.semaphore("d_" + name))
        waited = {e: {} for e in ENG_NAMES}
        n_wait = 0
        for o in self.ops:
            eng = self.engobj[o.eng]
            wt = waited[o.eng]
            need = {}
            for d in o.deps:
                if d.dsem is not None:
                    key = ("d", d.dsem)
                    val = d.dval
                else:
                    key = ("c", d.eng)
                    val = d.inc_idx
                if wt.get(key, 0) >= val:
                    continue
                if need.get(key, 0) < val:
                    need[key] = val
            for key, val in need.items():
                sem = dsems[key[1]] if key[0] == "d" else sems[key[1]]
                eng.wait_ge(sem, val)
                wt[key] = val
                n_wait += 1
            if o.fn is None:
                continue
            ins = o.fn(eng)
            if o.dsem is not None:
                ins.then_inc(dsems[o.dsem], 16)
            elif o.inc_idx is not None:
                ins.then_inc(sems[o.eng], 1)
        return dict(n_ops=len(self.ops), n_wait=n_wait, incs=cnt, n_dsems=len(dsems))


class Arena:
    def __init__(self, nc, es, nbytes):
        self.cap = nbytes
        self.t = es.enter_context(nc.sbuf_tensor("arena", [128, nbytes // 4], F32))
        self.off = 0

    def reset(self, off=0):
        self.off = off

    def alloc(self, shape, dt):
        n = 1
        for s in shape:
            n *= s
        size = n * (4 if dt == F32 else 2)
        size = (size + 3) // 4 * 4
        off = (self.off + 63) // 64 * 64
        assert off + size <= self.cap, f"arena overflow {off + size} > {self.cap}"
        ap = self.t[:, off // 4:(off + size) // 4]
        if dt != F32:
            ap = ap.bitcast(dt)
            if ap.shape[-1] != n:
                ap = ap[:, 0:n]
        if len(shape) == 2:
            ap = ap.rearrange("p (a b) -> p a b", b=shape[1])
        elif len(shape) == 3:
            ap = ap.rearrange("p (a b c) -> p a b c", b=shape[1], c=shape[2])
        self.off = off + size
        return ap


class Builder:
    def __init__(self, NB, S, L, MEM=256, phases="A1,A2,B,C"):
        self.NB, self.S, self.L, self.MEM = NB, S, L, MEM
        self.NT = NB * S
        self.phases = phases.split(",")
        nc = bass.Bass("TRN2", target_bir_lowering=False)
        self.nc = nc
        NT = self.NT
        dt = nc.dram_tensor
        self.xT = dt("xT", [D, NT], F32, kind="ExternalInput").ap()
        self.yT = dt("yT", [D, NT], F32, kind="ExternalOutput").ap()
        self.memT = dt("memT", [NB, D, MEM], F32, kind="ExternalInput").ap()
        self.w_in = dt("w_in", [L, 128, 8 * NIN], F32, kind="ExternalInput").ap()
        self.w_uq = dt("w_uq", [L, 128, 3 * 768], F32, kind="ExternalInput").ap()
        self.w_uqsw = dt("w_uqsw", [L, 128, 3 * 768], F32, kind="ExternalInput").ap()
        self.w_uk = dt("w_uk", [L, 128, 2 * 512], F32, kind="ExternalInput").ap()
        self.w_uv = dt("w_uv", [L, 128, 2 * 512], F32, kind="ExternalInput").ap()
        self.w_om = dt("w_om", [L, 128, 4 * D], F32, kind="ExternalInput").ap()
        self.w_oh = dt("w_oh", [L, 128, 2 * D], F32, kind="ExternalInput").ap()
        self.w_oc = dt("w_oc", [L, 128, 2 * D], F32, kind="ExternalInput").ap()
        self.w_xq = dt("w_xq", [L, 128, 8 * 512], F32, kind="ExternalInput").ap()
        self.w_xkv = dt("w_xkv", [L, 128, 8 * 1024], F32, kind="ExternalInput").ap()
        self.w_xo = dt("w_xo", [L, 128, 4 * D], F32, kind="ExternalInput").ap()
        self.w_up = dt("w_up", [L, 128, 8 * DFF], F32, kind="ExternalInput").ap()
        self.w_dn = dt("w_dn", [L, 128, 32 * D], F32, kind="ExternalInput").ap()
        self.vecs = dt("vecs", [L, 128, NV], F32, kind="ExternalInput").ap()
        self.lbl = dt("lbl", [64, 4 * 4], F32, kind="ExternalInput").ap()
        self.cosT = dt("cosT", [96, S], F32, kind="ExternalInput").ap()
        self.sinS = dt("sinS", [96, S], F32, kind="ExternalInput").ap()
        self.cmask = dt("cmask", [128, 4 * 512], F32, kind="ExternalInput").ap()
        self.consts = dt("consts", [128, 128 + 64 + 256], F32, kind="ExternalInput").ap()
        self.q_s = dt("q_s", [NB, 8, 96, S], BF16, kind="Internal").ap()
        self.k_s = dt("k_s", [NB, 8, 96, S], BF16, kind="Internal").ap()
        self.v_s = dt("v_s", [NB, S, VROW], BF16, kind="Internal").ap()
        self.yh_s = dt("yh_s", [NB, 4, 64, S], BF16, kind="Internal").ap()
        self.yc_s = dt("yc_s", [NB, 256, S], BF16, kind="Internal").ap()

    def build(self):
        nc = self.nc
        with contextlib.ExitStack() as es:
            self.es = es
            self.T = Tracker(nc, es)
            self.A = Arena(nc, es, 212480)
            self.ps = [es.enter_context(nc.psum_tensor(f"psb{i}", [128, 512], F32)) for i in range(8)]
            T = self.T
            for l in range(self.L):
                src = self.xT if l == 0 else self.yT
                if "A1" in self.phases:
                    self.phase_A1(l, src)
                    T.full_barrier()
                if "A2" in self.phases:
                    self.phase_A2(l, src)
                    T.full_barrier()
                elif l == 0:
                    self.copy_x()
                    T.full_barrier()
                if "B" in self.phases:
                    self.phase_B(l)
                    T.full_barrier()
                if "C" in self.phases:
                    self.phase_C(l)
                    T.full_barrier()
            T.final_wait("sp")
            self.stats = T.emit()
        return nc

    def copy_x(self):
        T = self.T
        for t in range(self.NT // 512):
            T.dma("sp", "cp", lambda e, t=t: e.dma_start(out=self.yT[:, t * 512:(t + 1) * 512], in_=self.xT[:, t * 512:(t + 1) * 512]),
                  writes=[("y", t)])

    def wload(self, dst, src, key, piece_cols, ncols):
        T = self.T
        for c0 in range(0, ncols, piece_cols):
            c1 = min(ncols, c0 + piece_cols)
            T.dma("pool", key, lambda e, c0=c0, c1=c1: e.dma_start(out=dst[:, c0:c1], in_=src[:, c0:c1]), writes=[key])

    def rstd_from(self, out_ap, in_ap, tmp_ap, scale, bias_ap_or_f, keys_r, key_tmp, key_out):
        T = self.T
        T.op("act", lambda e: e.activation(out=tmp_ap, in_=in_ap, func=AF.Ln, scale=scale, bias=bias_ap_or_f),
             reads=keys_r, writes=[key_tmp])
        T.op("act", lambda e: e.activation(out=out_ap, in_=tmp_ap, func=AF.Exp, scale=-0.5), reads=[key_tmp], writes=[key_out])

    def phase_C(self, l):
        T, A, nc, ps = self.T, self.A, self.nc, self.ps
        T.default_cost = {"pe": 215, "act": 520, "dve": 560, "pool": 1000, "sp": 100}
        NT = self.NT
        TS = 512
        KC, FC = 8, 32
        A.reset()
        wup = A.alloc([KC, DFF], BF16)
        wdn = A.alloc([FC, D], BF16)
        vec = A.alloc([NV], F32)
        ones = A.alloc([128], BF16)
        xin = A.alloc([KC, TS], F32)
        xb = A.alloc([KC, TS], BF16)
        sq = A.alloc([KC, TS], BF16)
        a = A.alloc([FC, TS], BF16)
        lnt = A.alloc([TS], F32)
        rstd = A.alloc([TS], F32)
        t1 = [A.alloc([TS], F32) for _ in range(2)]
        xr = [A.alloc([TS], F32) for _ in range(3)]
        yT = self.yT
        T.op("pool", lambda e: e.memset(ones, 1.0), writes=["ones"])
        T.dma("sp", "vec", lambda e: e.dma_start(out=vec, in_=self.vecs[l]), writes=["vec"])
        wup_d = self.w_up[l].rearrange("p (k n) -> p k n", n=DFF)
        for j in range(8):
            T.dma("pool", f"wup{j}", lambda e, j=j: e.dma_start(out=wup[:, :, j * 512:(j + 1) * 512], in_=wup_d[:, :, j * 512:(j + 1) * 512]),
                  writes=[("wup", j)])
        wdn_d = self.w_dn[l].rearrange("p (k n) -> p k n", n=D)
        for j in range(8):
            T.dma("pool", f"wdn{j}", lambda e, j=j: e.dma_start(out=wdn[:, 4 * j:4 * j + 4, :], in_=wdn_d[:, 4 * j:4 * j + 4, :]),
                  writes=[("wdn", j)])
        g = vec[:, V_GMLP:V_GMLP + 8]
        yT_t = yT.rearrange("(kc p) n -> p kc n", p=128)
        ntile = NT // TS
        T.dma("sp", "xin", lambda e: e.dma_start(out=xin, in_=yT_t[:, :, 0:TS]), reads=[("y", 0)], writes=["xin"])
        psi = 0
        xri = 0
        for t in range(ntile):
            c0 = t * TS
            T.op("dve", lambda e: e.tensor_tensor(out=xb, in0=xin, in1=g.unsqueeze(2).to_broadcast([128, KC, TS]), op=ALU.mult),
                 reads=["xin", "vec"], writes=["xb"])
            T.op("act", lambda e: e.activation(out=sq, in_=xin, func=AF.Square), reads=["xin"], writes=["sq"])
            if t + 1 < ntile:
                T.dma("sp", "xin", lambda e, c1=c0 + TS: e.dma_start(out=xin, in_=yT_t[:, :, c1:c1 + TS]),
                      reads=[("y", t + 1)], writes=["xin"])
            pss = ps[7]
            for kc in range(KC):
                T.op("pe", lambda e, kc=kc: e.matmul(pss[:], lhsT=ones, rhs=sq[:, kc, :], start=(kc == 0), stop=(kc == KC - 1)),
                     reads=["ones", "sq"], writes=["ps7"])
            self.rstd_from(rstd, pss[:], lnt, 1.0 / D, EPS, ["ps7"], "lnt", "rstd")
            for oc in range(FC):
                b_ = psi % 4; p = ps[b_]; pk = f"ps{b_}"; tt = t1[psi % 2]; tk = f"t1_{psi % 2}"; psi += 1
                for kc in range(KC):
                    T.op("pe", lambda e, kc=kc, oc=oc, p=p: e.matmul(p[:], lhsT=wup[:, kc, oc * 128:(oc + 1) * 128], rhs=xb[:, kc, :],
                                                                  start=(kc == 0), stop=(kc == KC - 1)),
                         reads=[("wup", oc // 4), "xb"], writes=[pk])
                T.op("dve", lambda e, p=p, tt=tt: e.scalar_tensor_tensor(out=tt, in0=p[:], scalar=0.0, in1=rstd, op0=ALU.max, op1=ALU.mult),
                     reads=[pk, "rstd"], writes=[tk])
                T.op("act", lambda e, tt=tt, oc=oc: e.activation(out=a[:, oc, :], in_=tt, func=AF.Square), reads=[tk], writes=[("a", oc)])
            for oc in range(8):
                b_ = psi % 4; p = ps[b_]; pk = f"ps{b_}"; psi += 1
                x_ = xr[xri % 3]; xk = f"xr_{xri % 3}"; xri += 1
                T.dma("sp", xk, lambda e, x_=x_, oc=oc, c0=c0: e.dma_start(out=x_, in_=yT[oc * 128:(oc + 1) * 128, c0:c0 + TS]),
                      reads=[("y", t)], writes=[xk])
                for kc in range(FC):
                    T.op("pe", lambda e, kc=kc, oc=oc, p=p: e.matmul(p[:], lhsT=wdn[:, kc, oc * 128:(oc + 1) * 128], rhs=a[:, kc, :],
                                                                  start=(kc == 0), stop=(kc == FC - 1)),
                         reads=[("wdn", kc // 4), ("a", kc)], writes=[pk])
                T.op("dve", lambda e, p=p, x_=x_: e.tensor_tensor(out=x_, in0=p[:], in1=x_, op=ALU.add), reads=[pk, xk], writes=[xk])
                T.dma("sp", xk, lambda e, x_=x_, oc=oc, c0=c0: e.dma_start(out=yT[oc * 128:(oc + 1) * 128, c0:c0 + TS], in_=x_),
                      reads=[xk], writes=[("y", t)])

    def phase_B(self, l):
        T, A, nc, ps = self.T, self.A, self.nc, self.ps
        NB, S, MEM = self.NB, self.S, self.MEM
        T.default_cost = {"pe": 260, "act": 550, "dve": 600, "pool": 1000, "sp": 100}
        TS = 512
        KC = 8
        MB = MEM // 128
        A.reset()
        wq = A.alloc([KC, 512], BF16)
        wkv = A.alloc([KC, 1024], BF16)
        wo = A.alloc([4, D], BF16)
        vec = A.alloc([NV], F32)
        ones = A.alloc([128], BF16)
        kT = A.alloc([NB, 4, MEM], BF16)
        vv = A.alloc([NB, MB, 512], BF16)
        memf = A.alloc([KC, MEM], F32)
        memb = A.alloc([KC, MEM], BF16)
        msq = A.alloc([KC, MEM], BF16)
        rsm = A.alloc([MEM], F32)
        tmpm = A.alloc([MEM], F32)
        kraw = A.alloc([MEM], F32)
        ksq = A.alloc([MEM], BF16)
        rk = A.alloc([MEM], F32)
        rtok = A.alloc([2 * MB], F32)
        ttok = A.alloc([2 * MB], F32)
        xin = [A.alloc([KC, TS], F32) for _ in range(2)]
        xb = A.alloc([KC, TS], BF16)
        sq = A.alloc([KC, TS], BF16)
        epsx = A.alloc([TS], F32)
        psq4 = A.alloc([4, TS], BF16)
        tq4 = A.alloc([4, TS], F32)
        lq4 = A.alloc([4, TS], F32)
        rq4 = A.alloc([4, TS], F32)
        qT = A.alloc([4, TS], BF16)
        PT = [A.alloc([TS], BF16) for _ in range(8)]
        rec2 = [A.alloc([TS], F32) for _ in range(2)]
        ob = A.alloc([4, TS], BF16)
        xo = [A.alloc([TS], F32) for _ in range(3)]
        yT = self.yT
        T.op("pool", lambda e: e.memset(ones, 1.0), writes=["ones"])
        T.dma("sp", "vec", lambda e: e.dma_start(out=vec, in_=self.vecs[l]), writes=["vec"])
        self.wload(wq.rearrange("p a b -> p (a b)"), self.w_xq[l], "wq", 4096, KC * 512)
        self.wload(wkv.rearrange("p a b -> p (a b)"), self.w_xkv[l], "wkv", 4096, KC * 1024)
        self.wload(wo.rearrange("p a b -> p (a b)"), self.w_xo[l], "wo", 4096, 4 * D)
        gmem = vec[:, V_GMEM:V_GMEM + 8]
        gxa = vec[:, V_GXA:V_GXA + 8]
        gxq = vec[:, V_GXQ:V_GXQ + 1]
        gxk = vec[:, V_GXK:V_GXK + 1]
        for b in range(NB):
            T.dma("sp", "memf", lambda e, b=b: e.dma_start(out=memf, in_=self.memT[b].rearrange("(kc p) m -> p kc m", p=128)),
                  writes=["memf"])
            T.op("dve", lambda e: e.tensor_tensor(out=memb, in0=memf, in1=gmem.unsqueeze(2).to_broadcast([128, KC, MEM]), op=ALU.mult),
                 reads=["memf", "vec"], writes=["memb"])
            T.op("act", lambda e: e.activation(out=msq, in_=memf, func=AF.Square), reads=["memf"], writes=["msq"])
            for kc in range(KC):
                T.op("pe", lambda e, kc=kc: e.matmul(ps[7][:, 0:MEM], lhsT=ones, rhs=msq[:, kc, :], start=(kc == 0), stop=(kc == KC - 1)),
                     reads=["ones", "msq"], writes=["ps7"])
            self.rstd_from(rsm, ps[7][:, 0:MEM], tmpm, 1.0 / D, EPS, ["ps7"], "tmpm", "rsm")
            for mb in range(MB):
                for kc in range(KC):
                    T.op("pe", lambda e, kc=kc, mb=mb: e.matmul(ps[6][:, 2 * mb:2 * mb + 2], lhsT=msq[:, kc, mb * 128:(mb + 1) * 128], rhs=ones[:, 0:2],
                                                               start=(kc == 0), stop=(kc == KC - 1)),
                         reads=["ones", "msq"], writes=["ps6"])
            self.rstd_from(rtok, ps[6][:, 0:2 * MB], ttok, 1.0 / D, EPS, ["ps6"], "ttok", "rtok")
            for h in range(4):
                p = ps[h % 2]; pk = f"ps{h % 2}"
                for kc in range(KC):
                    T.op("pe", lambda e, kc=kc, h=h, p=p: e.matmul(p[:, 0:MEM], lhsT=wkv[:, kc, h * 128:(h + 1) * 128], rhs=memb[:, kc, :],
                                                                start=(kc == 0), stop=(kc == KC - 1)),
                         reads=["wkv", "memb"], writes=[pk])
                T.op("dve", lambda e, p=p: e.tensor_tensor(out=kraw, in0=p[:, 0:MEM], in1=rsm, op=ALU.mult), reads=[pk, "rsm"], writes=["kraw"])
                T.op("act", lambda e: e.activation(out=ksq, in_=kraw, func=AF.Square), reads=["kraw"], writes=["ksq"])
                T.op("pe", lambda e: e.matmul(ps[5][:, 0:MEM], lhsT=ones, rhs=ksq, start=True, stop=True), reads=["ones", "ksq"], writes=["ps5"])
                self.rstd_from(rk, ps[5][:, 0:MEM], tmpm, 1.0 / 128, EPS, ["ps5"], "tmpm", "rk")
                T.op("dve", lambda e, b=b, h=h: e.scalar_tensor_tensor(out=kT[:, b, h, :], in0=kraw, scalar=gxk, in1=rk, op0=ALU.mult, op1=ALU.mult),
                     reads=["kraw", "rk", "vec"], writes=["kT"])
            for mb in range(MB):
                p = ps[2 + mb % 2]; pk = f"ps{2 + mb % 2}"
                for kc in range(KC):
                    T.op("pe", lambda e, kc=kc, mb=mb, p=p: e.matmul(p[:], lhsT=memb[:, kc, mb * 128:(mb + 1) * 128], rhs=wkv[:, kc, 512:1024],
                                                                  start=(kc == 0), stop=(kc == KC - 1)),
                         reads=["wkv", "memb"], writes=[pk])
                T.op("dve", lambda e, p=p, b=b, mb=mb: e.tensor_scalar(out=vv[:, b, mb, :], in0=p[:], scalar1=rtok[:, 2 * mb:2 * mb + 1], scalar2=None, op0=ALU.mult),
                     reads=[pk, "rtok"], writes=["vv"])
        yT_t = yT.rearrange("(kc p) n -> p kc n", p=128)
        ntile = self.NT // TS
        T.dma("sp", "xinB0", lambda e: e.dma_start(out=xin[0], in_=yT_t[:, :, 0:TS]), reads=[("y", 0)], writes=["xinB0"])
        sc_i = 0
        xoi = 0
        for t in range(ntile):
            c0 = t * TS
            b = c0 // S
            xi = xin[t % 2]; xik = f"xinB{t % 2}"
            if t + 1 < ntile:
                T.dma("sp", f"xinB{(t + 1) % 2}", lambda e, c1=c0 + TS, x2=xin[(t + 1) % 2]: e.dma_start(out=x2, in_=yT_t[:, :, c1:c1 + TS]),
                      reads=[("y", t + 1)], writes=[f"xinB{(t + 1) % 2}"])
            T.op("dve", lambda e, xi=xi: e.tensor_tensor(out=xb, in0=xi, in1=gxa.unsqueeze(2).to_broadcast([128, KC, TS]), op=ALU.mult),
                 reads=[xik, "vec"], writes=["xb"])
            T.op("act", lambda e, xi=xi: e.activation(out=sq, in_=xi, func=AF.Square), reads=[xik], writes=["sq"])
            for kc in range(KC):
                T.op("pe", lambda e, kc=kc: e.matmul(ps[7][:], lhsT=ones, rhs=sq[:, kc, :], start=(kc == 0), stop=(kc == KC - 1)),
                     reads=["ones", "sq"], writes=["ps7"])
            T.op("dve", lambda e: e.tensor_scalar(out=epsx, in0=ps[7][:], scalar1=EPS / D, scalar2=EPS * EPS, op0=ALU.mult, op1=ALU.add),
                 reads=["ps7"], writes=["epsx"])
            for h in range(4):
                p = ps[h]; pk = f"ps{h}"
                for kc in range(KC):
                    T.op("pe", lambda e, kc=kc, h=h, p=p: e.matmul(p[:], lhsT=wq[:, kc, h * 128:(h + 1) * 128], rhs=xb[:, kc, :],
                                                                start=(kc == 0), stop=(kc == KC - 1)),
                         reads=["wq", "xb"], writes=[pk])
                T.op("act", lambda e, p=p, h=h: e.activation(out=psq4[:, h, :], in_=p[:], func=AF.Square), reads=[pk], writes=[("psq", h)])
            for h in range(4):
                T.op("pe", lambda e, h=h: e.matmul(ps[4 + h][:], lhsT=ones, rhs=psq4[:, h, :], start=True, stop=True),
                     reads=["ones", ("psq", h)], writes=[f"ps{4 + h}"])
            for h in range(4):
                T.op("dve", lambda e, h=h: e.scalar_tensor_tensor(out=tq4[:, h, :], in0=ps[4 + h][:], scalar=1.0 / 128, in1=epsx, op0=ALU.mult, op1=ALU.add),
                     reads=[f"ps{4 + h}", "epsx"], writes=[("tq", h)])
            T.op("act", lambda e: e.activation(out=lq4, in_=tq4, func=AF.Ln), reads=[("tq", h) for h in range(4)], writes=["lq4"])
            T.op("act", lambda e: e.activation(out=rq4, in_=lq4, func=AF.Exp, scale=-0.5), reads=["lq4"], writes=["rq4"])
            for h in range(4):
                T.op("dve", lambda e, h=h: e.scalar_tensor_tensor(out=qT[:, h, :], in0=ps[h][:], scalar=gxq, in1=rq4[:, h, :], op0=ALU.mult, op1=ALU.mult),
                     reads=[f"ps{h}", "rq4", "vec"], writes=[("qT", h)])
            for h in range(4):
                for mb in range(MB):
                    sb_ = (h * MB + mb) % 4
                    pt = PT[(h * MB + mb) % 8]; ptk = f"PT{(h * MB + mb) % 8}"
                    T.op("pe", lambda e, h=h, mb=mb, sb_=sb_, b=b: e.matmul(ps[sb_][:], lhsT=kT[:, b, h, mb * 128:(mb + 1) * 128], rhs=qT[:, h, :],
                                                                         start=True, stop=True),
                         reads=["kT", ("qT", h)], writes=[f"ps{sb_}"])
                    T.op("act", lambda e, pt=pt, sb_=sb_: e.activation(out=pt, in_=ps[sb_][:], func=AF.Exp, scale=1.0 / math.sqrt(128.0)),
                         reads=[f"ps{sb_}"], writes=[ptk])
            for h in range(4):
                po = ps[4 + h % 2]; pok = f"ps{4 + h % 2}"
                pz = ps[6 + h % 2]; pzk = f"ps{6 + h % 2}"
                for mb in range(MB):
                    pt = PT[(h * MB + mb) % 8]; ptk = f"PT{(h * MB + mb) % 8}"
                    T.op("pe", lambda e, h=h, mb=mb, pt=pt, b=b, po=po: e.matmul(po[:], lhsT=vv[:, b, mb, h * 128:(h + 1) * 128], rhs=pt,
                                                                              start=(mb == 0), stop=(mb == MB - 1)),
                         reads=["vv", ptk], writes=[pok])
                for mb in range(MB):
                    pt = PT[(h * MB + mb) % 8]; ptk = f"PT{(h * MB + mb) % 8}"
                    T.op("pe", lambda e, mb=mb, pt=pt, pz=pz: e.matmul(pz[:], lhsT=ones, rhs=pt, start=(mb == 0), stop=(mb == MB - 1)),
                         reads=["ones", ptk], writes=[pzk])
                rc_ = rec2[h % 2]; rck = f"rec{h % 2}"
                T.op("dve", lambda e, pz=pz, rc_=rc_: e.reciprocal(out=rc_, in_=pz[:]), reads=[pzk], writes=[rck])
                T.op("dve", lambda e, h=h, po=po, rc_=rc_: e.tensor_tensor(out=ob[:, h, :], in0=po[:], in1=rc_, op=ALU.mult), reads=[pok, rck], writes=[("ob", h)])
            for oc in range(8):
                p = ps[oc % 2]; pk = f"ps{oc % 2}"
                x_ = xo[xoi % 3]; xk = f"xoB{xoi % 3}"; xoi += 1
                for h in range(4):
                    T.op("pe", lambda e, h=h, oc=oc, p=p: e.matmul(p[:], lhsT=wo[:, h, oc * 128:(oc + 1) * 128], rhs=ob[:, h, :],
                                                                start=(h == 0), stop=(h == 3)),
                         reads=["wo", ("ob", h)], writes=[pk])
                T.op("dve", lambda e, p=p, x_=x_, xi=xi, oc=oc: e.tensor_tensor(out=x_, in0=p[:], in1=xi[:, oc, :], op=ALU.add),
                     reads=[pk, xik], writes=[xk])
                T.dma("sp", xk, lambda e, x_=x_, oc=oc, c0=c0: e.dma_start(out=yT[oc * 128:(oc + 1) * 128, c0:c0 + TS], in_=x_),
                      reads=[xk], writes=[("y", t)])

    def phase_A1(self, l, src):
        T, A, nc, ps = self.T, self.A, self.nc, self.ps
        NB, S = self.NB, self.S
        T.default_cost = {"pe": 150, "act": 600, "dve": 500, "pool": 1000, "sp": 100}
        TS = 256
        KC = 8
        A.reset()
        al = A.alloc
        win = al([KC, NIN], BF16)
        wuq = al([3, 768], BF16)
        wuqs = al([3, 768], BF16)
        wuk = al([2, 512], BF16)
        wuv = al([2, 512], BF16)
        vec = al([NV], F32)
        ones = al([128], BF16)
        cst = al([128 + 64 + 256], F32)
        ident = al([128], BF16)
        tri = al([4, 4, 64], F32)
        m01 = al([4, TS], F32)
        lbt = al([4, 4], F32)
        lbe = al([4, 4], F32)
        lbs = al([4], F32)
        lbv = al([4], F32)
        xin = [al([KC, TS], F32) for _ in range(2)]
        xb = al([KC, TS], BF16)
        sq = al([KC, TS], BF16)
        rs0 = al([TS], F32)
        lt0 = al([TS], F32)
        rtok = al([8], F32)
        ttok = al([8], F32)
        cosb = [al([TS], F32) for _ in range(2)]
        sinb = [al([TS], F32) for _ in range(2)]
        cq = al([3, TS], F32)
        cqb = al([3, TS], BF16)
        cqs = al([3, TS], BF16)
        epsq = al([TS], F32)
        ckv = al([2, TS], F32)
        ckvs = al([2, TS], BF16)
        rskv = al([TS], F32)
        ckvb = al([2, TS], BF16)
        krt = al([TS], F32)
        kcat = al([8, TS], F32)
        ksw = al([TS], F32)
        kbt = al([TS], F32)
        ksq = al([8, TS], BF16)
        gen = [al([2, TS], F32) for _ in range(6)]
        qTo = al([8, TS], BF16)
        kTo = al([8, TS], BF16)
        vto = al([2, 8, 128], BF16)
        hq = al([4, TS], F32)
        hf = al([4, TS], F32)
        hg = al([4, TS], F32)
        hgen = [al([4, TS], F32) for _ in range(5)]
        qtl = al([4, TS], BF16)
        ktl = al([4, TS], BF16)
        ktok = al([4, 4, 64], BF16)
        vtok = al([4, 256], BF16)
        attb = al([4, 4, 64], BF16)
        sc8 = [al([4, 4], F32) for _ in range(6)]
        Sst = al([4, 64], F32)
        Ssc = al([4, 4, 64], BF16)
        tmpU = al([4, 64], F32)
        osq = al([4, TS], BF16)
        yho = al([4, TS], BF16)
        cbt = al([2, TS], F32)
        cct = al([2, TS], F32)
        cxt = al([2, TS], F32)
        ubuf = al([2, TS + 2], F32)
        yacc = al([2, TS], F32)
        yco = al([2, TS], BF16)

        T.op("pool", lambda e: e.memset(ones, 1.0), writes=["ones"])
        T.dma("sp", "vec", lambda e: e.dma_start(out=vec, in_=self.vecs[l]), writes=["vec"])
        T.dma("sp", "cst", lambda e: e.dma_start(out=cst, in_=self.consts), writes=["cst"])
        T.dma("sp", "lbt", lambda e: e.dma_start(out=lbt[0:64].rearrange("p a b -> p (a b)"), in_=self.lbl), writes=["lbt"])
        T.op("dve", lambda e: e.tensor_copy(out=ident, in_=cst[:, 0:128]), reads=["cst"], writes=["ident"])
        T.op("dve", lambda e: e.tensor_copy(out=tri[0:64], in_=cst[0:64, 128:192].unsqueeze(1).unsqueeze(1).to_broadcast([64, 4, 4, 64])),
             reads=["cst"], writes=["tri"])
        T.op("dve", lambda e: e.tensor_copy(out=m01[0:64], in_=cst[0:64, 192:448].unsqueeze(1).to_broadcast([64, 4, TS])),
             reads=["cst"], writes=["m01"])
        T.op("pool", lambda e: e.memset(ksw, 0.0), writes=["ksw"])
        T.op("pool", lambda e: e.memset(vto, 1.0), writes=["vto"])
        T.op("act", lambda e: e.activation(out=lbe[0:64], in_=lbt[0:64], func=AF.Exp), reads=["lbt"], writes=["lbe"])
        T.op("dve", lambda e: e.tensor_reduce(out=lbs[0:64], in_=lbe[0:64], axis=mybir.AxisListType.X, op=ALU.add), reads=["lbe"], writes=["lbs"])
        T.op("dve", lambda e: e.reciprocal(out=lbs[0:64], in_=lbs[0:64]), reads=["lbs"], writes=["lbs"])
        if l == 0:
            T.op("pool", lambda e: e.memset(lbv, 0.0), writes=["lbv"])
        else:
            T.op("dve", lambda e: e.tensor_reduce(out=lbv[0:64], in_=lbe[0:64, :, 1:l + 1], axis=mybir.AxisListType.X, op=ALU.add),
                 reads=["lbe"], writes=["lbv"])
            T.op("dve", lambda e: e.tensor_tensor(out=lbv[0:64], in0=lbv[0:64], in1=lbs[0:64], op=ALU.mult), reads=["lbv", "lbs"], writes=["lbv"])
        self.wload(win.rearrange("p a b -> p (a b)"), self.w_in[l], "win", NIN, KC * NIN)
        self.wload(wuq.rearrange("p a b -> p (a b)"), self.w_uq[l], "wuq", 2304, 2304)
        self.wload(wuqs.rearrange("p a b -> p (a b)"), self.w_uqsw[l], "wuqs", 2304, 2304)
        self.wload(wuk.rearrange("p a b -> p (a b)"), self.w_uk[l], "wuk", 1024, 1024)
        self.wload(wuv.rearrange("p a b -> p (a b)"), self.w_uv[l], "wuv", 1024, 1024)

        gmix = vec[:, V_GMIX:V_GMIX + 8]
        gq1 = vec[:, V_GQ1:V_GQ1 + 3]
        gkv = vec[:, V_GKV:V_GKV + 2]
        gqn = vec[0:96, V_GQN:V_GQN + 1]
        gqns = vec[0:96, V_GQNSW:V_GQNSW + 1]
        gkn = vec[0:96, V_GKN:V_GKN + 1]
        gkns = vec[0:96, V_GKNSW:V_GKNSW + 1]
        go = vec[0:64, V_GO:V_GO + 4]
        cw = vec[:, V_CW:V_CW + 6]
        src_t = src.rearrange("(kc p) n -> p kc n", p=128)
        ntile = self.NT // TS
        tps = S // TS
        T.dma("sp", "xinA0", lambda e: e.dma_start(out=xin[0], in_=src_t[:, :, 0:TS]), reads=[("y", 0)], writes=["xinA0"])
        pcnt = [0]

        def pbank():
            b_ = pcnt[0] % 4
            pcnt[0] += 1
            return ps[b_], f"ps{b_}"

        for t in range(ntile):
            c0 = t * TS
            b = c0 // S
            s0 = c0 - b * S
            xi = xin[t % 2]; xik = f"xinA{t % 2}"
            cs = cosb[t % 2]; csk = f"cos{t % 2}"; sn = sinb[t % 2]; snk = f"sin{t % 2}"
            T.dma("sp", csk, lambda e, cs=cs, s0=s0: e.dma_start(out=cs[0:96], in_=self.cosT[:, s0:s0 + TS]), writes=[csk])
            T.dma("sp", snk, lambda e, sn=sn, s0=s0: e.dma_start(out=sn[0:96], in_=self.sinS[:, s0:s0 + TS]), writes=[snk])
            if t + 1 < ntile:
                T.dma("sp", f"xinA{(t + 1) % 2}", lambda e, c1=c0 + TS, x2=xin[(t + 1) % 2]: e.dma_start(out=x2, in_=src_t[:, :, c1:c1 + TS]),
                      reads=[("y", (t + 1) // 2)], writes=[f"xinA{(t + 1) % 2}"])
            T.default_cost = {"pe": 150, "act": 600, "dve": 500, "pool": 1000, "sp": 100}
            if s0 == 0:
                T.op("pool", lambda e: e.memset(Sst, 0.0), writes=["Sst"])
                T.op("pool", lambda e: e.memset(ubuf, 0.0), writes=["ubuf"])
            T.op("dve", lambda e, xi=xi: e.tensor_tensor(out=xb, in0=xi, in1=gmix.unsqueeze(2).to_broadcast([128, KC, TS]), op=ALU.mult),
                 reads=[xik, "vec"], writes=["xb"])
            T.op("act", lambda e, xi=xi: e.activation(out=sq, in_=xi, func=AF.Square), reads=[xik], writes=["sq"])
            for kc in range(KC):
                T.op("pe", lambda e, kc=kc: e.matmul(ps[7][:, 0:TS], lhsT=ones, rhs=sq[:, kc, :], start=(kc == 0), stop=(kc == KC - 1)),
                     reads=["ones", "sq"], writes=["ps7"])
            self.rstd_from(rs0, ps[7][:, 0:TS], lt0, 1.0 / D, EPS, ["ps7"], "lt0", "rs0")
            for blk in range(2):
                for kc in range(KC):
                    T.op("pe", lambda e, kc=kc, blk=blk: e.matmul(ps[6][:, 2 * blk:2 * blk + 2], lhsT=sq[:, kc, blk * 128:(blk + 1) * 128], rhs=ones[:, 0:2],
                                                                 start=(kc == 0), stop=(kc == KC - 1)),
                         reads=["ones", "sq"], writes=["ps6"])
            self.rstd_from(rtok[:, 0:4], ps[6][:, 0:4], ttok[:, 0:4], 1.0 / D, EPS, ["ps6"], "ttok", "rtok")

            def inproj(col0, M, dst, dkey, extra_reads=()):
                p, pk = pbank()
                for kc in range(KC):
                    T.op("pe", lambda e, kc=kc, p=p: e.matmul(p[0:M, 0:TS], lhsT=win[:, kc, col0:col0 + M], rhs=xb[:, kc, :],
                                                             start=(kc == 0), stop=(kc == KC - 1)),
                         reads=["win", "xb"], writes=[pk])
                T.op("dve", lambda e, p=p: e.tensor_tensor(out=dst, in0=p[0:M, 0:TS], in1=rs0[0:M], op=ALU.mult),
                     reads=[pk, "rs0"], writes=[dkey])

            T.mute = 'Q' not in A1PARTS
            for j in range(3):
                inproj(O_CQ + j * 128, 128, cq[:, j, :], ("cq", j))
            T.op("act", lambda e: e.activation(out=cqs, in_=cq, func=AF.Square), reads=[("cq", 0), ("cq", 1), ("cq", 2)], writes=["cqs"])
            T.op("dve", lambda e: e.tensor_tensor(out=cqb, in0=cq, in1=gq1.unsqueeze(2).to_broadcast([128, 3, TS]), op=ALU.mult),
                 reads=[("cq", 0), ("cq", 1), ("cq", 2), "vec"], writes=["cqb"])
            for j in range(3):
                T.op("pe", lambda e, j=j: e.matmul(ps[7][:, 0:TS], lhsT=ones, rhs=cqs[:, j, :], start=(j == 0), stop=(j == 2)),
                     reads=["ones", "cqs"], writes=["ps7"])
            self.rstd_from(epsq, ps[7][:, 0:TS], lt0, 1.0 / 384, EPS, ["ps7"], "lt0", "epsq")
            T.op("dve", lambda e: e.tensor_tensor(out=cqb, in0=cqb, in1=epsq.unsqueeze(1).to_broadcast([128, 3, TS]), op=ALU.mult),
                 reads=["cqb", "epsq"], writes=["cqb"])
            for hp in range(4):
                pa, pak = pbank()
                pb, pbk = pbank()
                for hh in range(2):
                    h = hp * 2 + hh
                    for j in range(3):
                        T.op("pe", lambda e, j=j, h=h, hh=hh, pa=pa: e.matmul(pa[0:96, hh * TS:(hh + 1) * TS], lhsT=wuq[:, j, h * 96:(h + 1) * 96], rhs=cqb[:, j, :],
                                                                            start=(j == 0), stop=(j == 2)),
                             reads=["wuq", "cqb"], writes=[pak])
                for hh in range(2):
                    h = hp * 2 + hh
                    for j in range(3):
                        T.op("pe", lambda e, j=j, h=h, hh=hh, pb=pb: e.matmul(pb[0:96, hh * TS:(hh + 1) * TS], lhsT=wuqs[:, j, h * 96:(h + 1) * 96], rhs=cqb[:, j, :],
                                                                            start=(j == 0), stop=(j == 2)),
                             reads=["wuqs", "cqb"], writes=[pbk])
                if QSTAGE < 1:
                    continue
                g0, g1, g2, g3 = gen[0], gen[1], gen[2], gen[3]
                g0f = g0.rearrange("p a b -> p (a b)"); g1f = g1.rearrange("p a b -> p (a b)")
                g2f = g2.rearrange("p a b -> p (a b)"); g3f = g3.rearrange("p a b -> p (a b)")
                sqp = ksq[0:96, 0:2, :].rearrange("p a b -> p (a b)")
                T.op("act", lambda e, pa=pa, sqp=sqp: e.activation(out=sqp, in_=pa[0:96, :], func=AF.Square), reads=[pak], writes=["ksq"])
                T.op("pe", lambda e, sqp=sqp: e.matmul(ps[5][0:96, :], lhsT=ones[0:96, 0:96], rhs=sqp, start=True, stop=True),
                     reads=["ones", "ksq"], writes=["ps5"])
                self.rstd_from(g1f[0:96], ps[5][0:96, :], g0f[0:96], 1.0 / 96, EPS, ["ps5"], "g0", "g1")
                if QSTAGE < 2:
                    continue
                csb = cs[0:96].unsqueeze(1).to_broadcast([96, 2, TS])
                snb = sn[0:96].unsqueeze(1).to_broadcast([96, 2, TS])
                for hh in range(2):
                    T.op("dve", lambda e, pa=pa, g2=g2, hh=hh, cs=cs: e.scalar_tensor_tensor(out=g2[0:96, hh, :], in0=pa[0:96, hh * TS:(hh + 1) * TS], scalar=gqn, in1=cs[0:96],
                                                                                       op0=ALU.mult, op1=ALU.mult),
                         reads=[pak, csk, "vec"], writes=["g2"])
                    T.op("dve", lambda e, pb=pb, g3=g3, hh=hh, sn=sn: e.scalar_tensor_tensor(out=g3[0:96, hh, :], in0=pb[0:96, hh * TS:(hh + 1) * TS], scalar=gqns, in1=sn[0:96],
                                                                                       op0=ALU.mult, op1=ALU.mult),
                         reads=[pbk, snk, "vec"], writes=["g3"])
                T.op("dve", lambda e, g2=g2, g3=g3: e.tensor_tensor(out=g2[0:96], in0=g2[0:96], in1=g3[0:96], op=ALU.add), reads=["g2", "g3"], writes=["g2"])
                T.op("dve", lambda e, g2=g2, g1=g1, hp=hp: e.tensor_tensor(out=qTo[0:96, hp * 2:hp * 2 + 2, :], in0=g2[0:96], in1=g1[0:96], op=ALU.mult),
                     reads=["g2", "g1"], writes=["qTo"])
            T.dma("sp", "qst", lambda e, b=b, s0=s0: e.dma_start(out=self.q_s[b, :, :, s0:s0 + TS].rearrange("h p n -> p h n"), in_=qTo[0:96]),
                  reads=["qTo"], writes=[("q_s", b)])

            T.mute = 'K' not in A1PARTS
            for j in range(2):
                inproj(O_CKV + j * 128, 128, ckv[:, j, :], ("ckv", j))
            T.op("act", lambda e: e.activation(out=ckvs, in_=ckv, func=AF.Square), reads=[("ckv", 0), ("ckv", 1)], writes=["ckvs"])
            for j in range(2):
                T.op("pe", lambda e, j=j: e.matmul(ps[7][:, 0:TS], lhsT=ones, rhs=ckvs[:, j, :], start=(j == 0), stop=(j == 1)),
                     reads=["ones", "ckvs"], writes=["ps7"])
            self.rstd_from(rskv, ps[7][:, 0:TS], lt0, 1.0 / 256, EPS, ["ps7"], "lt0", "rskv")
            T.op("dve", lambda e: e.tensor_tensor(out=ckv, in0=ckv, in1=rskv.unsqueeze(1).to_broadcast([128, 2, TS]), op=ALU.mult),
                 reads=[("ckv", 0), ("ckv", 1), "rskv"], writes=[("ckv", 0), ("ckv", 1)])
            T.op("dve", lambda e: e.tensor_tensor(out=ckvb, in0=ckv, in1=gkv.unsqueeze(2).to_broadcast([128, 2, TS]), op=ALU.mult),
                 reads=[("ckv", 0), ("ckv", 1), "vec"], writes=["ckvb"])
            inproj(O_KR, 64, krt[0:64], "krt")
            T.op("act", lambda e: e.activation(out=kcat[64:96], in_=krt[0:32].unsqueeze(1).to_broadcast([32, 8, TS]), func=AF.Copy),
                 reads=["krt"], writes=["kcat_r"])
            T.op("act", lambda e: e.activation(out=ksw[64:96], in_=krt[32:64], func=AF.Copy), reads=["krt"], writes=["ksw"])
            for hp in range(4):
                p, pk = pbank()
                for hh in range(2):
                    h = hp * 2 + hh
                    for j in range(2):
                        T.op("pe", lambda e, j=j, h=h, hh=hh, p=p: e.matmul(p[0:64, hh * TS:(hh + 1) * TS], lhsT=wuk[:, j, h * 64:(h + 1) * 64], rhs=ckvb[:, j, :],
                                                                          start=(j == 0), stop=(j == 1)),
                             reads=["wuk", "ckvb"], writes=[pk])
                T.op("act", lambda e, p=p, hp=hp: e.activation(out=kcat[0:64, hp * 2:hp * 2 + 2, :], in_=p[0:64, :].rearrange("p (a b) -> p a b", a=2), func=AF.Copy),
                     reads=[pk], writes=[("kcat_n", hp)])
            kcat_keys = ["kcat_r"] + [("kcat_n", hp) for hp in range(4)]
            T.op("act", lambda e: e.activation(out=ksq[0:96], in_=kcat[0:96], func=AF.Square), reads=kcat_keys, writes=["ksq"])
            T.op("dve", lambda e, sn=sn: e.scalar_tensor_tensor(out=kbt[0:96], in0=ksw[0:96], scalar=gkns, in1=sn[0:96], op0=ALU.mult, op1=ALU.mult),
                 reads=["ksw", snk, "vec"], writes=["kbt"])
            for hp in range(4):
                g0, g1, g2 = gen[0], gen[1], gen[2]
                g0f = g0.rearrange("p a b -> p (a b)"); g1f = g1.rearrange("p a b -> p (a b)")
                T.op("pe", lambda e, hp=hp: e.matmul(ps[5][0:96, :], lhsT=ones[0:96, 0:96], rhs=ksq[0:96, hp * 2:hp * 2 + 2, :].rearrange("p a b -> p (a b)"),
                                                    start=True, stop=True),
                     reads=["ones", "ksq"], writes=["ps5"])
                self.rstd_from(g1f[0:96], ps[5][0:96, :], g0f[0:96], 1.0 / 96, EPS, ["ps5"], "g0", "g1")
                csb = cs[0:96].unsqueeze(1).to_broadcast([96, 2, TS])
                T.op("dve", lambda e, g2=g2, hp=hp, csb=csb: e.scalar_tensor_tensor(out=g2[0:96], in0=kcat[0:96, hp * 2:hp * 2 + 2, :], scalar=gkn, in1=csb,
                                                                                op0=ALU.mult, op1=ALU.mult),
                     reads=kcat_keys + [csk, "vec"], writes=["g2"])
                T.op("dve", lambda e, g2=g2: e.tensor_tensor(out=g2[0:96], in0=g2[0:96], in1=kbt[0:96].unsqueeze(1).to_broadcast([96, 2, TS]), op=ALU.add),
                     reads=["g2", "kbt"], writes=["g2"])
                T.op("dve", lambda e, g2=g2, g1=g1, hp=hp: e.tensor_tensor(out=kTo[0:96, hp * 2:hp * 2 + 2, :], in0=g2[0:96], in1=g1[0:96], op=ALU.mult),
                     reads=["g2", "g1"], writes=["kTo"])
            T.dma("sp", "kst", lambda e, b=b, s0=s0: e.dma_start(out=self.k_s[b, :, :, s0:s0 + TS].rearrange("h p n -> p h n"), in_=kTo[0:96]),
                  reads=["kTo"], writes=[("k_s", b)])
            for blk in range(2):
                p, pk = pbank()
                for j in range(2):
                    T.op("pe", lambda e, j=j, blk=blk, p=p: e.matmul(p[:], lhsT=ckvb[:, j, blk * 128:(blk + 1) * 128], rhs=wuv[:, j, :],
                                                                  start=(j == 0), stop=(j == 1)),
                         reads=["wuv", "ckvb"], writes=[pk])
                p3 = p[:].rearrange("p (h d) -> p h d", h=8)
                T.op("act", lambda e, p3=p3, blk=blk: e.activation(out=vto[:, blk, 0:8:2, 0:64], in_=p3[:, 0:8:2, :], func=AF.Copy),
                     reads=[pk], writes=["vto"])
                T.op("act", lambda e, p3=p3, blk=blk: e.activation(out=vto[:, blk, 1:8:2, 64:128], in_=p3[:, 1:8:2, :], func=AF.Copy),
                     reads=[pk], writes=["vto"])
            T.dma("sp", "vst", lambda e, b=b, s0=s0: e.dma_start(out=self.v_s[b, s0:s0 + TS, :].rearrange("(k p) c -> p k c", p=128),
                                                               in_=vto.rearrange("p k h w -> p k (h w)")),
                  reads=["vto"], writes=[("v_s", b)])

            T.mute = 'H' not in A1PARTS
            for (col, dstt, nm) in ((O_HQ, hq, "hq"), (O_HF, hf, "hf"), (O_HG, hg, "hg")):
                for j in range(2):
                    p, pk = pbank()
                    for kc in range(KC):
                        T.op("pe", lambda e, kc=kc, p=p, col=col, j=j: e.matmul(p[:, 0:TS], lhsT=win[:, kc, col + j * 128:col + (j + 1) * 128], rhs=xb[:, kc, :],
                                                                            start=(kc == 0), stop=(kc == KC - 1)),
                             reads=["win", "xb"], writes=[pk])
                    T.op("dve", lambda e, p=p, dstt=dstt, j=j: e.tensor_tensor(out=dstt[0:64, 2 * j, :], in0=p[0:64, 0:TS], in1=rs0[0:64], op=ALU.mult),
                         reads=[pk, "rs0"], writes=[(nm, 2 * j)])
                    T.op("dve", lambda e, p=p, dstt=dstt, j=j: e.tensor_tensor(out=dstt[0:64, 2 * j + 1, :], in0=p[64:128, 0:TS], in1=rs0[64:128], op=ALU.mult),
                         reads=[pk, "rs0"], writes=[(nm, 2 * j + 1)])
            hqk = [("hq", h) for h in range(4)]; hfk = [("hf", h) for h in range(4)]; hgk = [("hg", h) for h in range(4)]
            for blk in range(2):
                p, pk = pbank()
                for kc in range(KC):
                    T.op("pe", lambda e, kc=kc, blk=blk, p=p: e.matmul(p[:, 0:256], lhsT=xb[:, kc, blk * 128:(blk + 1) * 128], rhs=win[:, kc, O_HI:O_HI + 256],
                                                                    start=(kc == 0), stop=(kc == KC - 1)),
                         reads=["win", "xb"], writes=[pk])
                T.op("dve", lambda e, p=p, blk=blk: e.tensor_scalar(out=vtok[0:64, 2 * blk, :], in0=p[0:64, 0:256], scalar1=rtok[0:64, 2 * blk:2 * blk + 1], scalar2=None, op0=ALU.mult),
                     reads=[pk, "rtok"], writes=["vtok"])
                T.op("dve", lambda e, p=p, blk=blk: e.tensor_scalar(out=vtok[0:64, 2 * blk + 1, :], in0=p[64:128, 0:256], scalar1=rtok[64:128, 2 * blk:2 * blk + 1], scalar2=None, op0=ALU.mult),
                     reads=[pk, "rtok"], writes=["vtok"])
            T.default_cost = {"pe": 120, "act": 1900, "dve": 1600, "pool": 1000, "sp": 100}
            E, L1, L2, Bc, Wk = hgen
            H = slice(0, 64)
            lbb = lbv[0:64].unsqueeze(2).to_broadcast([64, 4, TS])
            T.op("act", lambda e: e.activation(out=E[H], in_=hf[H], func=AF.Exp, scale=-1.0), reads=hfk, writes=["E"])
            T.op("act", lambda e: e.activation(out=L1[H], in_=E[H], func=AF.Ln, scale=1.0, bias=1.0), reads=["E"], writes=["L1"])
            T.op("dve", lambda e: e.tensor_tensor(out=E[H], in0=E[H], in1=lbb, op=ALU.mult), reads=["E", "lbv"], writes=["E"])
            T.op("act", lambda e: e.activation(out=L2[H], in_=E[H], func=AF.Ln, scale=1.0, bias=1.0), reads=["E"], writes=["L2"])
            T.op("dve", lambda e: e.tensor_tensor(out=L2[H], in0=L2[H], in1=L1[H], op=ALU.subtract), reads=["L1", "L2"], writes=["L2"])
            T.op("dve", lambda e: e.tensor_tensor_scan(out=Bc[H].rearrange("p a b -> p (a b)"), data0=m01[H].rearrange("p a b -> p (a b)"),
                                                       data1=L2[H].rearrange("p a b -> p (a b)"), initial=0.0, op0=ALU.mult, op1=ALU.add),
                 reads=["m01", "L2"], writes=["Bc"])
            B4 = Bc[H].rearrange("p h (c t) -> p h c t", t=64)
            blast, cmid, e1, e2, ec, scx = sc8
            T.op("dve", lambda e: e.tensor_copy(out=blast[H], in_=B4[:, :, :, 63]), reads=["Bc"], writes=["blast"])
            T.op("dve", lambda e: e.tensor_copy(out=cmid[H], in_=B4[:, :, :, 31]), reads=["Bc"], writes=["cmid"])
            T.op("act", lambda e: e.activation(out=e1[H], in_=blast[H], func=AF.Exp), reads=["blast"], writes=["e1"])
            T.op("act", lambda e: e.activation(out=ec[H], in_=cmid[H], func=AF.Exp), reads=["cmid"], writes=["ec"])
            T.op("dve", lambda e: e.tensor_tensor(out=scx[H], in0=blast[H], in1=cmid[H], op=ALU.subtract), reads=["blast", "cmid"], writes=["scx"])
            T.op("act", lambda e: e.activation(out=e2[H], in_=scx[H], func=AF.Exp), reads=["scx"], writes=["e2"])
            T.op("dve", lambda e: e.tensor_tensor(out=B4, in0=B4, in1=cmid[H].unsqueeze(3).to_broadcast([64, 4, 4, 64]), op=ALU.subtract),
                 reads=["Bc", "cmid"], writes=["Bc"])
            T.op("act", lambda e: e.activation(out=L1[H], in_=L2[H], func=AF.Exp), reads=["L2"], writes=["L1"])
            T.op("dve", lambda e: e.tensor_scalar(out=L1[H], in0=L1[H], scalar1=-1.0, scalar2=1.0, op0=ALU.mult, op1=ALU.add), reads=["L1"], writes=["L1"])
            T.op("act", lambda e: e.activation(out=Wk[H], in_=Bc[H], func=AF.Exp, scale=-1.0), reads=["Bc"], writes=["Wk"])
            T.op("dve", lambda e: e.tensor_tensor(out=ktl[H], in0=L1[H], in1=Wk[H], op=ALU.mult), reads=["L1", "Wk"], writes=["ktl"])
            T.op("act", lambda e: e.activation(out=E[H], in_=hq[H], func=AF.Exp, scale=-1.0), reads=hqk, writes=["E"])
            T.op("act", lambda e: e.activation(out=E[H], in_=E[H], func=AF.Ln, scale=1.0, bias=1.0), reads=["E"], writes=["E"])
            T.op("dve", lambda e: e.tensor_tensor(out=E[H], in0=Bc[H], in1=E[H], op=ALU.subtract), reads=["E", "Bc"], writes=["E"])
            T.op("act", lambda e: e.activation(out=Wk[H], in_=E[H], func=AF.Exp), reads=["E"], writes=["Wk"])
            T.op("dve", lambda e: e.tensor_tensor(out=qtl[H], in0=hq[H], in1=Wk[H], op=ALU.mult), reads=hqk + ["Wk"], writes=["qtl"])
            T.op("act", lambda e: e.activation(out=L2[H], in_=hg[H], func=AF.Exp, scale=-1.0), reads=hgk, writes=["L2"])
            T.op("act", lambda e: e.activation(out=L2[H], in_=L2[H], func=AF.Ln, scale=1.0, bias=1.0), reads=["L2"], writes=["L2"])
            T.op("act", lambda e: e.activation(out=L2[H], in_=L2[H], func=AF.Exp, scale=-1.0), reads=["L2"], writes=["L2"])
            T.op("dve", lambda e: e.tensor_tensor(out=L2[H], in0=L2[H], in1=hg[H], op=ALU.mult), reads=["L2"] + hgk, writes=["L2"])
            T.default_cost = {"pe": 120, "act": 600, "dve": 500, "pool": 1000, "sp": 100}
            ptr = ps[4][:].bitcast(BF16)
            for c in range(4):
                for h in range(4):
                    T.op("pe", lambda e, c=c, h=h: e.transpose(ptr[0:64, (c * 4 + h) * 64:(c * 4 + h + 1) * 64], ktl[0:64, h, c * 64:(c + 1) * 64], ident[0:64, 0:64]),
                         reads=["ktl", "ident"], writes=["ps4"])
            T.op("act", lambda e: e.activation(out=ktok[H].rearrange("p a b c -> p (a b c)"), in_=ptr[0:64, 0:1024], func=AF.Copy), reads=["ps4"], writes=["ktok"])
            for h in range(4):
                pb_ = ps[5] if h < 2 else ps[6]
                for c in range(4):
                    T.op("pe", lambda e, h=h, c=c, pb_=pb_: e.matmul(pb_[0:64, ((h % 2) * 4 + c) * 64:((h % 2) * 4 + c + 1) * 64],
                                                                   lhsT=ktl[0:64, h, c * 64:(c + 1) * 64], rhs=qtl[0:64, h, c * 64:(c + 1) * 64], start=True, stop=True),
                         reads=["ktl", "qtl"], writes=["ps5" if h < 2 else "ps6"])
            for hh2 in range(2):
                pb_ = ps[5 + hh2]
                T.op("dve", lambda e, hh2=hh2, pb_=pb_: e.tensor_tensor(out=attb[0:64, hh2 * 2:hh2 * 2 + 2], in0=pb_[0:64, :].rearrange("p (a b c) -> p a b c", a=2, b=4),
                                                                      in1=tri[0:64, 0:2], op=ALU.mult),
                     reads=[f"ps{5 + hh2}", "tri"], writes=["attb"])
            for h in range(4):
                pb_ = ps[7] if h < 2 else ps[4]
                for c in range(4):
                    T.op("pe", lambda e, h=h, c=c, pb_=pb_: e.matmul(pb_[0:64, ((h % 2) * 4 + c) * 64:((h % 2) * 4 + c + 1) * 64],
                                                                   lhsT=ktok[0:64, c, h, :], rhs=vtok[0:64, c, h * 64:(h + 1) * 64], start=True, stop=True),
                         reads=["ktok", "vtok"], writes=["ps7" if h < 2 else "ps4"])
            U7 = ps[7][0:64, :].rearrange("p (a c d) -> p a c d", a=2, c=4)
            U4 = ps[4][0:64, :].rearrange("p (a c d) -> p a c d", a=2, c=4)
            for c in range(4):
                T.op("dve", lambda e, c=c: e.tensor_tensor(out=Ssc[0:64, :, c, :], in0=Sst[H], in1=ec[0:64, :, c:c + 1].to_broadcast([64, 4, 64]), op=ALU.mult),
                     reads=["Sst", "ec"], writes=[("Ssc", c)])
                T.op("dve", lambda e, c=c: e.tensor_tensor(out=tmpU[0:64, 0:2, :], in0=U7[:, :, c, :], in1=e2[0:64, 0:2, c:c + 1].to_broadcast([64, 2, 64]), op=ALU.mult),
                     reads=["ps7", "e2"], writes=["tmpU0"])
                T.op("dve", lambda e, c=c: e.tensor_tensor(out=tmpU[0:64, 2:4, :], in0=U4[:, :, c, :], in1=e2[0:64, 2:4, c:c + 1].to_broadcast([64, 2, 64]), op=ALU.mult),
                     reads=["ps4", "e2"], writes=["tmpU1"])
                T.op("dve", lambda e, c=c: e.tensor_tensor(out=Sst[H], in0=Sst[H], in1=e1[0:64, :, c:c + 1].to_broadcast([64, 4, 64]), op=ALU.mult),
                     reads=["Sst", "e1", ("Ssc", c)], writes=["Sst"])
                T.op("dve", lambda e: e.tensor_tensor(out=Sst[H], in0=Sst[H], in1=tmpU[H], op=ALU.add), reads=["Sst", "tmpU0", "tmpU1"], writes=["Sst"])
            for h in range(4):
                p, pk = pbank()
                for c in range(4):
                    T.op("pe", lambda e, h=h, c=c, p=p: e.matmul(p[0:64, c * 64:(c + 1) * 64], lhsT=vtok[0:64, c, h * 64:(h + 1) * 64], rhs=attb[0:64, h, c, :],
                                                              start=True, stop=False),
                         reads=["vtok", "attb"], writes=[pk])
                    T.op("pe", lambda e, h=h, c=c, p=p: e.matmul(p[0:64, c * 64:(c + 1) * 64], lhsT=Ssc[0:64, h, c, :], rhs=qtl[0:64, h, c * 64:(c + 1) * 64],
                                                              start=False, stop=True),
                         reads=[("Ssc", c), "qtl"], writes=[pk])
                T.op("act", lambda e, p=p, h=h: e.activation(out=osq[0:64, h, :], in_=p[0:64, 0:TS], func=AF.Square), reads=[pk], writes=[("osq", h)])
                T.op("pe", lambda e, h=h: e.matmul(ps[5][0:64, 0:TS], lhsT=ones[0:64, 0:64], rhs=osq[0:64, h, :], start=True, stop=True),
                     reads=["ones", ("osq", h), "attb"], writes=["ps5"])
                g0, g1 = gen[4], gen[5]
                g0f = g0.rearrange("p a b -> p (a b)"); g1f = g1.rearrange("p a b -> p (a b)")
                self.rstd_from(g1f[0:64, 0:TS], ps[5][0:64, 0:TS], g0f[0:64, 0:TS], 1.0 / 64, EPS, ["ps5"], "g4", "g5")
                T.op("dve", lambda e, p=p, h=h, g0f=g0f, g1f=g1f: e.scalar_tensor_tensor(out=g0f[0:64, 0:TS], in0=p[0:64, 0:TS], scalar=go[:, h:h + 1], in1=g1f[0:64, 0:TS],
                                                                                     op0=ALU.mult, op1=ALU.mult),
                     reads=[pk, "g5", "vec"], writes=["g4"])
                T.op("dve", lambda e, h=h, g0f=g0f: e.tensor_tensor(out=yho[0:64, h, :], in0=g0f[0:64, 0:TS], in1=L2[0:64, h, :], op=ALU.mult),
                     reads=["g4", "L2"], writes=["yho"])
            T.dma("sp", "yhst", lambda e, b=b, s0=s0: e.dma_start(out=self.yh_s[b, :, :, s0:s0 + TS].rearrange("h p n -> p h n"), in_=yho[0:64]),
                  reads=["yho"], writes=[("yh_s", b)])

            T.mute = 'C' not in A1PARTS
            for j in range(2):
                inproj(O_CB + j * 128, 128, cbt[:, j, :], ("cb", j))
                inproj(O_CC + j * 128, 128, cct[:, j, :], ("cc", j))
                inproj(O_CX + j * 128, 128, cxt[:, j, :], ("cx", j))
            ck = [("cc", 0), ("cc", 1), ("cx", 0), ("cx", 1)]
            T.op("dve", lambda e: e.tensor_tensor(out=ubuf[:, :, 2:TS + 2], in0=cct, in1=cxt, op=ALU.mult), reads=ck + ["ucarry"], writes=["ubuf"])
            for j in range(2):
                T.op("dve", lambda e, j=j: e.tensor_scalar(out=yacc[:, j, :], in0=ubuf[:, j, 2:TS + 2], scalar1=cw[:, j * 3 + 2:j * 3 + 3], scalar2=None, op0=ALU.mult),
                     reads=["ubuf", "vec"], writes=[("yacc", j)])
                T.op("dve", lambda e, j=j: e.scalar_tensor_tensor(out=yacc[:, j, :], in0=ubuf[:, j, 1:TS + 1], scalar=cw[:, j * 3 + 1:j * 3 + 2], in1=yacc[:, j, :],
                                                                  op0=ALU.mult, op1=ALU.add),
                     reads=["ubuf", "vec", ("yacc", j)], writes=[("yacc", j)])
                T.op("dve", lambda e, j=j: e.scalar_tensor_tensor(out=yacc[:, j, :], in0=ubuf[:, j, 0:TS], scalar=cw[:, j * 3:j * 3 + 1], in1=yacc[:, j, :],
                                                                  op0=ALU.mult, op1=ALU.add),
                     reads=["ubuf", "vec", ("yacc", j)], writes=[("yacc", j)])
            T.op("dve", lambda e: e.tensor_tensor(out=yco, in0=yacc, in1=cbt, op=ALU.mult),
                 reads=[("yacc", 0), ("yacc", 1), ("cb", 0), ("cb", 1)], writes=["yco"])
            T.op("dve", lambda e: e.tensor_copy(out=ubuf[:, :, 0:2], in_=ubuf[:, :, TS:TS + 2]), reads=["ubuf"], writes=["ucarry", "ubuf"])
            T.dma("sp", "ycst", lambda e, b=b, s0=s0: e.dma_start(out=self.yc_s[b, :, s0:s0 + TS].rearrange("(j p) n -> p j n", p=128), in_=yco),
                  reads=["yco"], writes=[("yc_s", b)])
            T.mute = False

    def phase_A2(self, l, src):
        T, A, nc, ps = self.T, self.A, self.nc, self.ps
        NB, S = self.NB, self.S
        T.default_cost = {"pe": 260, "act": 550, "dve": 700, "pool": 1000, "sp": 100}
        TS = 512
        NKB = S // 128
        A.reset()
        al = A.alloc
        kc_ = al([8, S], BF16)
        vc_ = al([NKB, VROW], BF16)
        wom = al([4, D], BF16)
        woh = al([2, D], BF16)
        woc = al([2, D], BF16)
        cm = al([4, 512], BF16)
        ident = al([128], BF16)
        cst = al([128 + 64 + 256], F32)
        qt = [al([8, TS], BF16) for _ in range(2)]
        yh = al([2, TS], BF16)
        yc = al([2, TS], BF16)
        ym = al([4, TS], BF16)
        PT = [al([TS], BF16) for _ in range(4)]
        rc2 = [al([TS], F32) for _ in range(2)]
        xr = [al([TS], F32) for _ in range(3)]
        yT = self.yT
        T.dma("sp", "cst", lambda e: e.dma_start(out=cst, in_=self.consts), writes=["cst"])
        T.op("dve", lambda e: e.tensor_copy(out=ident, in_=cst[:, 0:128]), reads=["cst"], writes=["ident"])
        T.dma("pool", "cm", lambda e: e.dma_start(out=cm.rearrange("p a b -> p (a b)"), in_=self.cmask), writes=["cm"])
        self.wload(wom.rearrange("p a b -> p (a b)"), self.w_om[l], "wom", 4096, 4 * D)
        self.wload(woh.rearrange("p a b -> p (a b)"), self.w_oh[l], "woh", 2048, 2 * D)
        self.wload(woc.rearrange("p a b -> p (a b)"), self.w_oc[l], "woc", 2048, 2 * D)
        scale = 1.0 / math.sqrt(96.0)
        tps = S // TS
        sci = 0
        xri = 0
        for b in range(NB):
            NCH = S // 512
            for c in range(NCH):
                T.dma("sp", f"kcl{c}", lambda e, b=b, c=c: e.dma_start(out=kc_[0:96, :, c * 512:(c + 1) * 512],
                                                                     in_=self.k_s[b, :, :, c * 512:(c + 1) * 512].rearrange("h p n -> p h n")),
                      reads=[("k_s", b)], writes=[("kc", c)])
                T.dma("sp", f"vcl{c}", lambda e, b=b, c=c: e.dma_start(out=vc_[:, 4 * c:4 * c + 4, :],
                                                                     in_=self.v_s[b, c * 512:(c + 1) * 512, :].rearrange("(k p) c -> p k c", p=128)),
                      reads=[("v_s", b)], writes=[("vc", c)])
            vc4 = vc_.rearrange("p k (h w) -> p k h w", w=128)
            for i in range(tps):
                s0 = i * TS
                t = b * tps + i
                c0 = b * S + s0
                q = qt[t % 2]; qk = f"qt{t % 2}"
                T.dma("sp", qk, lambda e, q=q, b=b, s0=s0: e.dma_start(out=q[0:96], in_=self.q_s[b, :, :, s0:s0 + TS].rearrange("h p n -> p h n")),
                      reads=[("q_s", b)], writes=[qk])
                T.dma("sp", "yhl", lambda e, b=b, s0=s0: e.dma_start(out=yh, in_=self.yh_s[b, :, :, s0:s0 + TS].rearrange("(j hh) p n -> (hh p) j n", hh=2)),
                      reads=[("yh_s", b)], writes=["yh"])
                T.dma("sp", "ycl", lambda e, b=b, s0=s0: e.dma_start(out=yc, in_=self.yc_s[b, :, s0:s0 + TS].rearrange("(j p) n -> p j n", p=128)),
                      reads=[("yc_s", b)], writes=["yc"])
                nkb = 4 * i + 4
                LA = 2
                ND = 0
                steps = [(h, kb) for h in range(8) for kb in range(nkb)]
                slots = {}
                pending_norm = []

                def emit_qk(idx):
                    nonlocal sci
                    h, kb = steps[idx]
                    sb_ = sci % 4; sci += 1
                    pt = PT[sb_]; ptk = f"PT{sb_}"
                    dj = kb - 4 * i
                    q0 = 128 * dj if dj > 0 else 0
                    slots[idx] = (pt, ptk, q0)
                    T.op("pe", lambda e, h=h, kb=kb, sb_=sb_, q=q, dj=dj, q0=q0: e.matmul(ps[sb_][:, q0:], lhsT=kc_[0:96, h, kb * 128:(kb + 1) * 128], rhs=q[0:96, h, q0:],
                                                                                       start=True, stop=(dj < 0)),
                         reads=[("kc", kb // 4), qk], writes=[f"ps{sb_}"])
                    if dj >= 0:
                        T.op("pe", lambda e, sb_=sb_, dj=dj, q0=q0: e.matmul(ps[sb_][:, q0:], lhsT=ident, rhs=cm[:, dj, q0:], start=False, stop=True),
                             reads=["ident", "cm"], writes=[f"ps{sb_}"])
                    T.op("act", lambda e, pt=pt, sb_=sb_, q0=q0: e.activation(out=pt[:, q0:], in_=ps[sb_][:, q0:], func=AF.Exp, scale=scale),
                         reads=[f"ps{sb_}"], writes=[ptk])

                def emit_pv(idx):
                    h, kb = steps[idx]
                    pt, ptk, q0 = slots.pop(idx)
                    po = ps[4 + h % 2]; pok = f"ps{4 + h % 2}"
                    T.op("pe", lambda e, h=h, kb=kb, pt=pt, po=po, nkb=nkb, vc4=vc4, q0=q0: e.matmul(po[:, q0:], lhsT=vc4[:, kb, h, :], rhs=pt[:, q0:], start=(kb == 0), stop=(kb == nkb - 1)),
                         reads=[("vc", kb // 4), ptk], writes=[pok])
                    if kb == nkb - 1:
                        return h
                    return None

                def emit_norm(h):
                    po = ps[4 + h % 2]; pok = f"ps{4 + h % 2}"
                    rch = rc2[h % 2]; rck = f"rc{h % 2}"
                    if h % 2 == 0:
                        T.op("dve", lambda e, po=po, rch=rch: e.reciprocal(out=rch[64:128], in_=po[64:128, :]), reads=[pok], writes=[rck])
                        T.op("dve", lambda e, po=po, h=h, rch=rch: e.tensor_tensor(out=ym[0:64, h // 2, :], in0=po[0:64, :], in1=rch[64:128], op=ALU.mult),
                             reads=[pok, rck], writes=[("ym", h)])
                    else:
                        T.op("dve", lambda e, po=po, rch=rch: e.reciprocal(out=rch[0:64], in_=po[0:64, :]), reads=[pok], writes=[rck])
                        T.op("dve", lambda e, po=po, h=h, rch=rch: e.tensor_tensor(out=ym[64:128, h // 2, :], in0=po[64:128, :], in1=rch[0:64], op=ALU.mult),
                             reads=[pok, rck], writes=[("ym", h)])

                nst = len(steps)
                for idx in range(nst + LA):
                    if idx < nst:
                        emit_qk(idx)
                    if idx - LA >= 0:
                        hdone = emit_pv(idx - LA)
                        if hdone is not None:
                            pending_norm.append((idx + ND, hdone))
                    while pending_norm and pending_norm[0][0] <= idx:
                        emit_norm(pending_norm.pop(0)[1])
                for _, hh_ in pending_norm:
                    emit_norm(hh_)
                for oc in range(8):
                    p = ps[6 + oc % 2]; pk = f"ps{6 + oc % 2}"
                    x_ = xr[xri % 3]; xk = f"xrA{xri % 3}"; xri += 1
                    T.dma("sp", xk, lambda e, x_=x_, oc=oc, c0=c0: e.dma_start(out=x_, in_=src[oc * 128:(oc + 1) * 128, c0:c0 + TS]),
                          reads=[("y", c0 // 512)], writes=[xk])
                    for j in range(4):
                        T.op("pe", lambda e, j=j, oc=oc, p=p: e.matmul(p[:], lhsT=wom[:, j, oc * 128:(oc + 1) * 128], rhs=ym[:, j, :], start=(j == 0), stop=False),
                             reads=["wom", ("ym", 2 * j), ("ym", 2 * j + 1)], writes=[pk])
                    for j in range(2):
                        T.op("pe", lambda e, j=j, oc=oc, p=p: e.matmul(p[:], lhsT=woh[:, j, oc * 128:(oc + 1) * 128], rhs=yh[:, j, :], start=False, stop=False),
                             reads=["woh", "yh"], writes=[pk])
                    for j in range(2):
                        T.op("pe", lambda e, j=j, oc=oc, p=p: e.matmul(p[:], lhsT=woc[:, j, oc * 128:(oc + 1) * 128], rhs=yc[:, j, :], start=False, stop=(j == 1)),
                             reads=["woc", "yc"], writes=[pk])
                    T.op("dve", lambda e, p=p, x_=x_: e.tensor_tensor(out=x_, in0=p[:], in1=x_, op=ALU.add), reads=[pk, xk], writes=[xk])
                    T.dma("sp", xk, lambda e, x_=x_, oc=oc, c0=c0: e.dma_start(out=yT[oc * 128:(oc + 1) * 128, c0:c0 + TS], in_=x_),
                          reads=[xk], writes=[("y", c0 // 512)])


def _kpn(w, p=128):
    K, N = w.shape
    return np.ascontiguousarray(w.reshape(K // p, p, N).transpose(1, 0, 2).reshape(p, -1))


def host_consts(S, positions):
    ident = np.eye(128, dtype=np.float32)
    tri = np.zeros((128, 64), np.float32)
    tri[0:64] = (np.arange(64)[:, None] <= np.arange(64)[None, :]).astype(np.float32)
    m01 = np.ones((128, 256), np.float32)
    m01[:, ::64] = 0.0
    consts = np.concatenate([ident, tri, m01], axis=1)
    cmask = np.zeros((128, 4, 512), np.float32)
    p = np.arange(128)[:, None]
    n = np.arange(512)[None, :]
    for j in range(4):
        cmask[:, j, :] = np.where(128 * j + p <= n, 0.0, NEG)
    inv_freq = (10000.0 ** (-np.arange(0, 32, 2, dtype=np.float32) / 32)).astype(np.float32)
    ang = positions.astype(np.float32)[:, None] * inv_freq[None, :]
    cos = np.cos(ang).astype(np.float32).T
    sin = np.sin(ang).astype(np.float32).T
    cosT = np.ones((96, S), np.float32)
    sinS = np.zeros((96, S), np.float32)
    cosT[64:80] = cos
    cosT[80:96] = cos
    sinS[64:80] = -sin
    sinS[80:96] = sin
    return consts, cmask.reshape(128, -1), cosT, sinS


def host_weights(inp, L):
    out = {}
    perm = np.concatenate([np.arange(64), np.arange(80, 96), np.arange(64, 80)])
    w_in = np.asarray(inp["w_in"])
    kr = w_in[:, :, 640:672]
    krsw = np.concatenate([kr[:, :, 16:32], kr[:, :, 0:16]], axis=2)
    w_in2 = np.concatenate([w_in[:, :, :672], krsw, w_in[:, :, 672:]], axis=2)
    out["w_in"] = np.stack([_kpn(w_in2[l]) for l in range(L)])
    w_uq = np.asarray(inp["w_uq"])
    out["w_uq"] = np.stack([_kpn(w_uq[l]) for l in range(L)])
    w_uqsw = w_uq.reshape(L, 384, 8, 96)[:, :, :, perm].reshape(L, 384, 768)
    out["w_uqsw"] = np.stack([_kpn(w_uqsw[l]) for l in range(L)])
    w_ukv = np.asarray(inp["w_ukv"]).reshape(L, 256, 8, 128)
    w_uk = w_ukv[:, :, :, :64].reshape(L, 256, 512)
    w_uv = w_ukv[:, :, :, 64:].reshape(L, 256, 512)
    out["w_uk"] = np.stack([_kpn(w_uk[l]) for l in range(L)])
    out["w_uv"] = np.stack([_kpn(w_uv[l]) for l in range(L)])
    w_out = np.asarray(inp["w_out"])
    out["w_om"] = np.stack([_kpn(w_out[l, 0:512]) for l in range(L)])
    out["w_oh"] = np.stack([_kpn(w_out[l, 512:768]) for l in range(L)])
    out["w_oc"] = np.stack([_kpn(w_out[l, 768:1024]) for l in range(L)])
    for nm, key in (("w_xq", "w_xq"), ("w_xkv", "w_xkv"), ("w_xo", "w_xo"), ("w_up", "w_up"), ("w_dn", "w_down")):
        w = np.asarray(inp[key])
        out[nm] = np.stack([_kpn(w[l]) for l in range(L)])
    vecs = np.zeros((L, 128, NV), np.float32)

    def col(v):
        v = np.asarray(v)
        return v.reshape(-1, 128).T
    for l in range(L):
        vecs[l, :, V_GMIX:V_GMIX + 8] = col(inp["mix_norm_g"][l])
        vecs[l, :, V_GQ1:V_GQ1 + 3] = col(inp["mla_q_norm_g"][l])
        vecs[l, :, V_GKV:V_GKV + 2] = col(inp["mla_kv_norm_g"][l])
        gq = np.asarray(inp["mla_qn_g"][l]); gk = np.asarray(inp["mla_kn_g"][l])
        vecs[l, 0:96, V_GQN] = gq
        vecs[l, 0:96, V_GQNSW] = gq[perm]
        vecs[l, 0:96, V_GKN] = gk
        vecs[l, 0:96, V_GKNSW] = gk[perm]
        vecs[l, 0:64, V_GO:V_GO + 4] = np.asarray(inp["hgrn_o_norm_g"][l]).reshape(4, 64).T
        cwl = np.asarray(inp["conv_w"][l])
        vecs[l, :, V_CW:V_CW + 6] = cwl.reshape(3, 2, 128).transpose(2, 1, 0).reshape(128, 6)
        vecs[l, :, V_GXA:V_GXA + 8] = col(inp["xattn_norm_g"][l])
        vecs[l, :, V_GXQ] = np.asarray(inp["xq_norm_g"][l])
        vecs[l, :, V_GXK] = np.asarray(inp["xk_norm_g"][l])
        vecs[l, :, V_GMLP:V_GMLP + 8] = col(inp["mlp_norm_g"][l])
        vecs[l, :, V_GMEM:V_GMEM + 8] = col(inp["mem_norm_g"][l])
    out["vecs"] = vecs
    lb = np.asarray(inp["hgrn_lb_logits"])
    lbl = np.zeros((64, 4, 4), np.float32)
    lbl[:, :, :L] = lb.reshape(L, 4, 64).transpose(2, 1, 0)
    if L < 4:
        lbl[:, :, L:] = -1e4
    out["lbl"] = lbl.reshape(64, 16)
    return out


_CACHE = {}


def run(inputs, NB, S, L, ncores, phases="A1,A2,B,C", trace=False):
    x = np.asarray(inputs["x"], dtype=np.float32)
    mem = np.asarray(inputs["mem"], dtype=np.float32)
    MEM = mem.shape[1]
    key = (NB, S, L, MEM, phases)
    if key not in _CACHE:
        bld = Builder(NB, S, L, MEM, phases)
        bld.build()
        _CACHE[key] = bld
    bld = _CACHE[key]
    hw = host_weights(inputs, L)
    consts, cmask, cosT, sinS = host_consts(S, np.asarray(inputs["positions"]))
    in_maps = []
    for c in range(ncores):
        xs = x[c * NB:(c + 1) * NB].reshape(NB * S, D)
        m = dict(hw)
        m["xT"] = np.ascontiguousarray(xs.T)
        m["memT"] = np.ascontiguousarray(mem[c * NB:(c + 1) * NB].transpose(0, 2, 1))
        m["cosT"] = cosT
        m["sinS"] = sinS
        m["cmask"] = cmask
        m["consts"] = consts
        in_maps.append(m)
    res = run_bass_kernel_spmd(bld.nc, in_maps, core_ids=list(range(ncores)), trace=trace)
    outs = [np.ascontiguousarray(r["yT"].T).reshape(NB, S, D) for r in res.results]
    return np.concatenate(outs, axis=0), res


def kernel(**inputs):
    out, _ = run(inputs, NB=2, S=4096, L=4, ncores=8)
    return out.astype(np.float32)
```

```python
import contextlib
import math
A1PARTS = 'QKHC'
QSTAGE = 9
import numpy as np
import concourse.bass as bass
import concourse.mybir as mybir
from concourse.bass_utils import run_bass_kernel_spmd

F32 = mybir.dt.float32
BF16 = mybir.dt.bfloat16
AF = mybir.ActivationFunctionType
ALU = mybir.AluOpType
EPS = 1e-6

D = 1024
DFF = 4096
NIN = 2496
O_CQ, O_CKV, O_KR, O_HQ, O_HF, O_HI, O_HG, O_CB, O_CC, O_CX = 0, 384, 640, 704, 960, 1216, 1472, 1728, 1984, 2240
NV = 56
V_GMIX, V_GQ1, V_GKV, V_GQN, V_GQNSW, V_GKN, V_GKNSW, V_GO, V_CW, V_GXA, V_GXQ, V_GXK, V_GMLP, V_GMEM = \
    0, 8, 11, 13, 14, 15, 16, 17, 21, 27, 35, 36, 37, 45
VROW = 8 * 128
NEG = -30000.0

ENG_NAMES = ("pe", "act", "dve", "pool", "sp")


class Op:
    __slots__ = ("eng", "fn", "deps", "odeps", "cost", "inc_idx", "dsem", "dval", "needed")

    def __init__(self, eng, fn):
        self.eng = eng
        self.fn = fn
        self.deps = []
        self.odeps = []
        self.cost = None
        self.inc_idx = None
        self.dsem = None
        self.dval = None
        self.needed = False


class Tracker:
    def __init__(self, nc, es):
        self.nc = nc
        self.es = es
        self.ops = []
        self.last_w = {}
        self.readers = {}
        self.dsem_cnt = {}
        self.last_dma = {}
        self.default_cost = {"pe": 150, "act": 600, "dve": 500, "pool": 1000, "sp": 100}
        self.seg_start = 0
        self.engobj = {"pe": nc.tensor, "act": nc.scalar, "dve": nc.vector, "pool": nc.gpsimd, "sp": nc.sync}

    mute = False
    do_schedule = True

    def op(self, eng, fn, reads=(), writes=(), cost=None):
        if self.mute:
            return None
        o = Op(eng, fn)
        o.cost = cost if cost is not None else self.default_cost.get(eng, 500)
        self._deps(o, reads, writes)
        self.ops.append(o)
        return o

    def dma(self, queue, dsem, fn, reads=(), writes=()):
        if self.mute:
            return None
        o = Op(queue, fn)
        o.dsem = dsem
        self.dsem_cnt[dsem] = self.dsem_cnt.get(dsem, 0) + 16
        o.dval = self.dsem_cnt[dsem]
        self._deps(o, reads, writes)
        prev = self.last_dma.get(dsem)
        if prev is not None:
            o.odeps.append(prev)
        self.last_dma[dsem] = o
        self.ops.append(o)
        return o

    def _deps(self, o, reads, writes):
        deps = []
        for k in reads:
            w = self.last_w.get(k)
            if w is not None:
                deps.append((w, False))
            if isinstance(k, str) and k[:2] == "ps" and k[2:].isdigit():
                for r in self.readers.get(k, ()):
                    if r.eng != o.eng:
                        deps.append((r, False))
        for k in writes:
            w = self.last_w.get(k)
            if w is not None:
                deps.append((w, True))
            for r in self.readers.get(k, ()):
                deps.append((r, False))
        seen = set()
        for d, waw in deps:
            if id(d) in seen or d is o:
                continue
            if d.dsem is None and o.dsem is None and d.eng == "pe" and o.eng == "pe":
                o.odeps.append(d)
                continue
            if waw and d.dsem is not None and o.dsem is not None and d.dsem == o.dsem and d.eng == o.eng:
                o.odeps.append(d)
                continue
            seen.add(id(d))
            o.deps.append(d)
            d.needed = True
        for k in writes:
            self.last_w[k] = o
            self.readers[k] = []
        for k in reads:
            lst = self.readers.setdefault(k, [])
            if o.dsem is None:
                keep = []
                for r in lst:
                    if r.dsem is None and r.eng == o.eng:
                        if r is not o:
                            o.odeps.append(r)
                    else:
                        keep.append(r)
                lst[:] = keep
            lst.append(o)


    def schedule_segment(self):
        import heapq
        seg = self.ops[self.seg_start:]
        n = len(seg)
        if n == 0:
            return
        pos = {id(o): i for i, o in enumerate(seg)}
        succ = [[] for _ in range(n)]
        indeg = [0] * n
        for i, o in enumerate(seg):
            for d in list(o.deps) + list(o.odeps):
                j = pos.get(id(d))
                if j is not None:
                    succ[j].append(i)
                    indeg[i] += 1
        ready_t = [0.0] * n
        fin = [0.0] * n
        engs = ENG_NAMES
        avail = {e: [] for e in engs}
        future = {e: [] for e in engs}
        free_t = {e: 0.0 for e in engs}
        for i, o in enumerate(seg):
            if indeg[i] == 0:
                heapq.heappush(future[o.eng], (0.0, i))
        order = []
        LAT = 150.0
        done = 0
        while done < n:
            best = None
            for e in engs:
                fu, av = future[e], avail[e]
                while fu and fu[0][0] <= free_t[e]:
                    rt, i = heapq.heappop(fu)
                    heapq.heappush(av, i)
                if av:
                    cand = (free_t[e], av[0], e, True)
                elif fu:
                    cand = (fu[0][0], fu[0][1], e, False)
                else:
                    continue
                if best is None or cand[:2] < best[:2]:
                    best = cand
            st, i, e, from_av = best
            if from_av:
                heapq.heappop(avail[e])
            else:
                heapq.heappop(future[e])
            o = seg[i]
            if o.dsem is not None:
                free_t[e] = st + 100.0
                fin[i] = st + 2500.0
            else:
                c = float(o.cost or 500)
                free_t[e] = st + c
                fin[i] = st + c
            order.append((st, i))
            done += 1
            for j in succ[i]:
                rt = fin[i] + (LAT if seg[j].eng != e or o.dsem is not None else 30.0)
                if rt > ready_t[j]:
                    ready_t[j] = rt
                indeg[j] -= 1
                if indeg[j] == 0:
                    heapq.heappush(future[seg[j].eng], (ready_t[j], j))
        order.sort()
        self.ops[self.seg_start:] = [seg[i] for _, i in order]

    def _lasts(self):
        last = {}
        for o in self.ops:
            if o.fn is None:
                continue
            last[(o.eng, o.dsem)] = o
        return list(last.values())

    def full_barrier(self):
        if self.do_schedule:
            self.schedule_segment()
        lasts = self._lasts()
        for e in ENG_NAMES:
            o = Op(e, None)
            for d in lasts:
                o.deps.append(d)
                d.needed = True
            self.ops.append(o)
        self.last_w.clear()
        self.readers.clear()
        self.seg_start = len(self.ops)

    def final_wait(self, eng="sp"):
        if self.do_schedule:
            self.schedule_segment()
        o = Op(eng, None)
        for d in self._lasts():
            o.deps.append(d)
            d.needed = True
        self.ops.append(o)

    def emit(self):
        nc = self.nc
        cnt = {e: 0 for e in ENG_NAMES}
        for o in self.ops:
            if o.dsem is None and o.needed and o.fn is not None:
                cnt[o.eng] += 1
                o.inc_idx = cnt[o.eng]
        sems = {}
        for e in ENG_NAMES:
            if cnt[e] > 0:
                sems[e] = self.es.enter_context(nc.semaphore("s_" + e))
        dsems = {}
        for name in self.dsem_cnt:
            dsems[name] = self.es.enter_context(nc.semaphore("d_" + name))
        waited = {e: {} for e in ENG_NAMES}
        n_wait = 0
        for o in self.ops:
            eng = self.engobj[o.eng]
            wt = waited[o.eng]
            need = {}
            for d in o.deps:
                if d.dsem is not None:
                    key = ("d", d.dsem)
                    val = d.dval
                else:
                    key = ("c", d.eng)
                    val = d.inc_idx
                if wt.get(key, 0) >= val:
                    continue
                if need.get(key, 0) < val:
                    need[key] = val
            for key, val in need.items():
                sem = dsems[key[1]] if key[0] == "d" else sems[key[1]]
                eng.wait_ge(sem, val)
                wt[key] = val
                n_wait += 1
            if o.fn is None:
                continue
            ins = o.fn(eng)
            if o.dsem is not None:
                ins.then_inc(dsems[o.dsem], 16)
            elif o.inc_idx is not None:
                ins.then_inc(sems[o.eng], 1)
        return dict(n_ops=len(self.ops), n_wait=n_wait, incs=cnt, n_dsems=len(dsems))


class Arena:
    def __init__(self, nc, es, nbytes):
        self.cap = nbytes
        self.t = es.enter_context(nc.sbuf_tensor("arena", [128, nbytes // 4], F32))
        self.off = 0

    def reset(self, off=0):
        self.off = off

    def alloc(self, shape, dt):
        n = 1
        for s in shape:
            n *= s
        size = n * (4 if dt == F32 else 2)
        size = (size + 3) // 4 * 4
        off = (self.off + 63) // 64 * 64
        assert off + size <= self.cap, f"arena overflow {off + size} > {self.cap}"
        ap = self.t[:, off // 4:(off + size) // 4]
        if dt != F32:
            ap = ap.bitcast(dt)
            if ap.shape[-1] != n:
                ap = ap[:, 0:n]
        if len(shape) == 2:
            ap = ap.rearrange("p (a b) -> p a b", b=shape[1])
        elif len(shape) == 3:
            ap = ap.rearrange("p (a b c) -> p a b c", b=shape[1], c=shape[2])
        self.off = off + size
        return ap


class Builder:
    def __init__(self, NB, S, L, MEM=256, phases="A1,A2,B,C"):
        self.NB, self.S, self.L, self.MEM = NB, S, L, MEM
        self.NT = NB * S
        self.phases = phases.split(",")
        nc = bass.Bass("TRN2", target_bir_lowering=False)
        self.nc = nc
        NT = self.NT
        dt = nc.dram_tensor
        self.xT = dt("xT", [D, NT], F32, kind="ExternalInput").ap()
        self.yT = dt("yT", [D, NT], F32, kind="ExternalOutput").ap()
        self.memT = dt("memT", [NB, D, MEM], F32, kind="ExternalInput").ap()
        self.w_in = dt("w_in", [L, 128, 8 * NIN], F32, kind="ExternalInput").ap()
        self.w_uq = dt("w_uq", [L, 128, 3 * 768], F32, kind="ExternalInput").ap()
        self.w_uqsw = dt("w_uqsw", [L, 128, 3 * 768], F32, kind="ExternalInput").ap()
        self.w_uk = dt("w_uk", [L, 128, 2 * 512], F32, kind="ExternalInput").ap()
        self.w_uv = dt("w_uv", [L, 128, 2 * 512], F32, kind="ExternalInput").ap()
        self.w_om = dt("w_om", [L, 128, 4 * D], F32, kind="ExternalInput").ap()
        self.w_oh = dt("w_oh", [L, 128, 2 * D], F32, kind="ExternalInput").ap()
        self.w_oc = dt("w_oc", [L, 128, 2 * D], F32, kind="ExternalInput").ap()
        self.w_xq = dt("w_xq", [L, 128, 8 * 512], F32, kind="ExternalInput").ap()
        self.w_xkv = dt("w_xkv", [L, 128, 8 * 1024], F32, kind="ExternalInput").ap()
        self.w_xo = dt("w_xo", [L, 128, 4 * D], F32, kind="ExternalInput").ap()
        self.w_up = dt("w_up", [L, 128, 8 * DFF], F32, kind="ExternalInput").ap()
        self.w_dn = dt("w_dn", [L, 128, 32 * D], F32, kind="ExternalInput").ap()
        self.vecs = dt("vecs", [L, 128, NV], F32, kind="ExternalInput").ap()
        self.lbl = dt("lbl", [64, 4 * 4], F32, kind="ExternalInput").ap()
        self.cosT = dt("cosT", [96, S], F32, kind="ExternalInput").ap()
        self.sinS = dt("sinS", [96, S], F32, kind="ExternalInput").ap()
        self.cmask = dt("cmask", [128, 4 * 512], F32, kind="ExternalInput").ap()
        self.consts = dt("consts", [128, 128 + 64 + 256], F32, kind="ExternalInput").ap()
        self.q_s = dt("q_s", [NB, 8, 96, S], BF16, kind="Internal").ap()
        self.k_s = dt("k_s", [NB, 8, 96, S], BF16, kind="Internal").ap()
        self.v_s = dt("v_s", [NB, S, VROW], BF16, kind="Internal").ap()
        self.yh_s = dt("yh_s", [NB, 4, 64, S], BF16, kind="Internal").ap()
        self.yc_s = dt("yc_s", [NB, 256, S], BF16, kind="Internal").ap()

    def build(self):
        nc = self.nc
        with contextlib.ExitStack() as es:
            self.es = es
            self.T = Tracker(nc, es)
            self.A = Arena(nc, es, 212480)
            self.ps = [es.enter_context(nc.psum_tensor(f"psb{i}", [128, 512], F32)) for i in range(8)]
            T = self.T
            for l in range(self.L):
                src = self.xT if l == 0 else self.yT
                if "A1" in self.phases:
                    self.phase_A1(l, src)
                    T.full_barrier()
                if "A2" in self.phases:
                    self.phase_A2(l, src)
                    T.full_barrier()
                elif l == 0:
                    self.copy_x()
                    T.full_barrier()
                if "B" in self.phases:
                    self.phase_B(l)
                    T.full_barrier()
                if "C" in self.phases:
                    self.phase_C(l)
                    T.full_barrier()
            T.final_wait("sp")
            self.stats = T.emit()
        return nc

    def copy_x(self):
        T = self.T
        for t in range(self.NT // 512):
            T.dma("sp", "cp", lambda e, t=t: e.dma_start(out=self.yT[:, t * 512:(t + 1) * 512], in_=self.xT[:, t * 512:(t + 1) * 512]),
                  writes=[("y", t)])

    def wload(self, dst, src, key, piece_cols, ncols):
        T = self.T
        for c0 in range(0, ncols, piece_cols):
            c1 = min(ncols, c0 + piece_cols)
            T.dma("pool", key, lambda e, c0=c0, c1=c1: e.dma_start(out=dst[:, c0:c1], in_=src[:, c0:c1]), writes=[key])

    def rstd_from(self, out_ap, in_ap, tmp_ap, scale, bias_ap_or_f, keys_r, key_tmp, key_out):
        T = self.T
        T.op("act", lambda e: e.activation(out=tmp_ap, in_=in_ap, func=AF.Ln, scale=scale, bias=bias_ap_or_f),
             reads=keys_r, writes=[key_tmp])
        T.op("act", lambda e: e.activation(out=out_ap, in_=tmp_ap, func=AF.Exp, scale=-0.5), reads=[key_tmp], writes=[key_out])

    def phase_C(self, l):
        T, A, nc, ps = self.T, self.A, self.nc, self.ps
        T.default_cost = {"pe": 215, "act": 520, "dve": 560, "pool": 1000, "sp": 100}
        NT = self.NT
        TS = 512
        KC, FC = 8, 32
        A.reset()
        wup = A.alloc([KC, DFF], BF16)
        wdn = A.alloc([FC, D], BF16)
        vec = A.alloc([NV], F32)
        ones = A.alloc([128], BF16)
        xin = A.alloc([KC, TS], F32)
        xb = A.alloc([KC, TS], BF16)
        sq = A.alloc([KC, TS], BF16)
        a = A.alloc([FC, TS], BF16)
        lnt = A.alloc([TS], F32)
        rstd = A.alloc([TS], F32)
        t1 = [A.alloc([TS], F32) for _ in range(2)]
        xr = [A.alloc([TS], F32) for _ in range(3)]
        yT = self.yT
        T.op("pool", lambda e: e.memset(ones, 1.0), writes=["ones"])
        T.dma("sp", "vec", lambda e: e.dma_start(out=vec, in_=self.vecs[l]), writes=["vec"])
        wup_d = self.w_up[l].rearrange("p (k n) -> p k n", n=DFF)
        for j in range(8):
            T.dma("pool", f"wup{j}", lambda e, j=j: e.dma_start(out=wup[:, :, j * 512:(j + 1) * 512], in_=wup_d[:, :, j * 512:(j + 1) * 512]),
                  writes=[("wup", j)])
        wdn_d = self.w_dn[l].rearrange("p (k n) -> p k n", n=D)
        for j in range(8):
            T.dma("pool", f"wdn{j}", lambda e, j=j: e.dma_start(out=wdn[:, 4 * j:4 * j + 4, :], in_=wdn_d[:, 4 * j:4 * j + 4, :]),
                  writes=[("wdn", j)])
        g = vec[:, V_GMLP:V_GMLP + 8]
        yT_t = yT.rearrange("(kc p) n -> p kc n", p=128)
        ntile = NT // TS
        T.dma("sp", "xin", lambda e: e.dma_start(out=xin, in_=yT_t[:, :, 0:TS]), reads=[("y", 0)], writes=["xin"])
        psi = 0
        xri = 0
        for t in range(ntile):
            c0 = t * TS
            T.op("dve", lambda e: e.tensor_tensor(out=xb, in0=xin, in1=g.unsqueeze(2).to_broadcast([128, KC, TS]), op=ALU.mult),
                 reads=["xin", "vec"], writes=["xb"])
            T.op("act", lambda e: e.activation(out=sq, in_=xin, func=AF.Square), reads=["xin"], writes=["sq"])
            if t + 1 < ntile:
                T.dma("sp", "xin", lambda e, c1=c0 + TS: e.dma_start(out=xin, in_=yT_t[:, :, c1:c1 + TS]),
                      reads=[("y", t + 1)], writes=["xin"])
            pss = ps[7]
            for kc in range(KC):
                T.op("pe", lambda e, kc=kc: e.matmul(pss[:], lhsT=ones, rhs=sq[:, kc, :], start=(kc == 0), stop=(kc == KC - 1)),
                     reads=["ones", "sq"], writes=["ps7"])
            self.rstd_from(rstd, pss[:], lnt, 1.0 / D, EPS, ["ps7"], "lnt", "rstd")
            for oc in range(FC):
                b_ = psi % 4; p = ps[b_]; pk = f"ps{b_}"; tt = t1[psi % 2]; tk = f"t1_{psi % 2}"; psi += 1
                for kc in range(KC):
                    T.op("pe", lambda e, kc=kc, oc=oc, p=p: e.matmul(p[:], lhsT=wup[:, kc, oc * 128:(oc + 1) * 128], rhs=xb[:, kc, :],
                                                                  start=(kc == 0), stop=(kc == KC - 1)),
                         reads=[("wup", oc // 4), "xb"], writes=[pk])
                T.op("dve", lambda e, p=p, tt=tt: e.scalar_tensor_tensor(out=tt, in0=p[:], scalar=0.0, in1=rstd, op0=ALU.max, op1=ALU.mult),
                     reads=[pk, "rstd"], writes=[tk])
                T.op("act", lambda e, tt=tt, oc=oc: e.activation(out=a[:, oc, :], in_=tt, func=AF.Square), reads=[tk], writes=[("a", oc)])
            for oc in range(8):
                b_ = psi % 4; p = ps[b_]; pk = f"ps{b_}"; psi += 1
                x_ = xr[xri % 3]; xk = f"xr_{xri % 3}"; xri += 1
                T.dma("sp", xk, lambda e, x_=x_, oc=oc, c0=c0: e.dma_start(out=x_, in_=yT[oc * 128:(oc + 1) * 128, c0:c0 + TS]),
                      reads=[("y", t)], writes=[xk])
                for kc in range(FC):
                    T.op("pe", lambda e, kc=kc, oc=oc, p=p: e.matmul(p[:], lhsT=wdn[:, kc, oc * 128:(oc + 1) * 128], rhs=a[:, kc, :],
                                                                  start=(kc == 0), stop=(kc == FC - 1)),
                         reads=[("wdn", kc // 4), ("a", kc)], writes=[pk])
                T.op("dve", lambda e, p=p, x_=x_: e.tensor_tensor(out=x_, in0=p[:], in1=x_, op=ALU.add), reads=[pk, xk], writes=[xk])
                T.dma("sp", xk, lambda e, x_=x_, oc=oc, c0=c0: e.dma_start(out=yT[oc * 128:(oc + 1) * 128, c0:c0 + TS], in_=x_),
                      reads=[xk], writes=[("y", t)])

    def phase_B(self, l):
        T, A, nc, ps = self.T, self.A, self.nc, self.ps
        NB, S, MEM = self.NB, self.S, self.MEM
        T.default_cost = {"pe": 260, "act": 550, "dve": 600, "pool": 1000, "sp": 100}
        TS = 512
        KC = 8
        MB = MEM // 128
        A.reset()
        wq = A.alloc([KC, 512], BF16)
        wkv = A.alloc([KC, 1024], BF16)
        wo = A.alloc([4, D], BF16)
        vec = A.alloc([NV], F32)
        ones = A.alloc([128], BF16)
        kT = A.alloc([NB, 4, MEM], BF16)
        vv = A.alloc([NB, MB, 512], BF16)
        memf = A.alloc([KC, MEM], F32)
        memb = A.alloc([KC, MEM], BF16)
        msq = A.alloc([KC, MEM], BF16)
        rsm = A.alloc([MEM], F32)
        tmpm = A.alloc([MEM], F32)
        kraw = A.alloc([MEM], F32)
        ksq = A.alloc([MEM], BF16)
        rk = A.alloc([MEM], F32)
        rtok = A.alloc([2 * MB], F32)
        ttok = A.alloc([2 * MB], F32)
        xin = [A.alloc([KC, TS], F32) for _ in range(2)]
        xb = A.alloc([KC, TS], BF16)
        sq = A.alloc([KC, TS], BF16)
        epsx = A.alloc([TS], F32)
        psq4 = A.alloc([4, TS], BF16)
        tq4 = A.alloc([4, TS], F32)
        lq4 = A.alloc([4, TS], F32)
        rq4 = A.alloc([4, TS], F32)
        qT = A.alloc([4, TS], BF16)
        PT = [A.alloc([TS], BF16) for _ in range(8)]
        rec2 = [A.alloc([TS], F32) for _ in range(2)]
        ob = A.alloc([4, TS], BF16)
        xo = [A.alloc([TS], F32) for _ in range(3)]
        yT = self.yT
        T.op("pool", lambda e: e.memset(ones, 1.0), writes=["ones"])
        T.dma("sp", "vec", lambda e: e.dma_start(out=vec, in_=self.vecs[l]), writes=["vec"])
        self.wload(wkv.rearrange("p a b -> p (a b)"), self.w_xkv[l], "wkv", 4096, KC * 1024)
        self.wload(wq.rearrange("p a b -> p (a b)"), self.w_xq[l], "wq", 4096, KC * 512)
        self.wload(wo.rearrange("p a b -> p (a b)"), self.w_xo[l], "wo", 4096, 4 * D)
        gmem = vec[:, V_GMEM:V_GMEM + 8]
        gxa = vec[:, V_GXA:V_GXA + 8]
        gxq = vec[:, V_GXQ:V_GXQ + 1]
        gxk = vec[:, V_GXK:V_GXK + 1]
        for b in range(NB):
            T.dma("sp", "memf", lambda e, b=b: e.dma_start(out=memf, in_=self.memT[b].rearrange("(kc p) m -> p kc m", p=128)),
                  writes=["memf"])
            T.op("dve", lambda e: e.tensor_tensor(out=memb, in0=memf, in1=gmem.unsqueeze(2).to_broadcast([128, KC, MEM]), op=ALU.mult),
                 reads=["memf", "vec"], writes=["memb"])
            T.op("act", lambda e: e.activation(out=msq, in_=memf, func=AF.Square), reads=["memf"], writes=["msq"])
            for kc in range(KC):
                T.op("pe", lambda e, kc=kc: e.matmul(ps[7][:, 0:MEM], lhsT=ones, rhs=msq[:, kc, :], start=(kc == 0), stop=(kc == KC - 1)),
                     reads=["ones", "msq"], writes=["ps7"])
            self.rstd_from(rsm, ps[7][:, 0:MEM], tmpm, 1.0 / D, EPS, ["ps7"], "tmpm", "rsm")
            for mb in range(MB):
                for kc in range(KC):
                    T.op("pe", lambda e, kc=kc, mb=mb: e.matmul(ps[6][:, 2 * mb:2 * mb + 2], lhsT=msq[:, kc, mb * 128:(mb + 1) * 128], rhs=ones[:, 0:2],
                                                               start=(kc == 0), stop=(kc == KC - 1)),
                         reads=["ones", "msq"], writes=["ps6"])
            self.rstd_from(rtok, ps[6][:, 0:2 * MB], ttok, 1.0 / D, EPS, ["ps6"], "ttok", "rtok")
            for h in range(4):
                p = ps[h % 2]; pk = f"ps{h % 2}"
                for kc in range(KC):
                    T.op("pe", lambda e, kc=kc, h=h, p=p: e.matmul(p[:, 0:MEM], lhsT=wkv[:, kc, h * 128:(h + 1) * 128], rhs=memb[:, kc, :],
                                                                start=(kc == 0), stop=(kc == KC - 1)),
                         reads=["wkv", "memb"], writes=[pk])
                T.op("dve", lambda e, p=p: e.tensor_tensor(out=kraw, in0=p[:, 0:MEM], in1=rsm, op=ALU.mult), reads=[pk, "rsm"], writes=["kraw"])
                T.op("act", lambda e: e.activation(out=ksq, in_=kraw, func=AF.Square), reads=["kraw"], writes=["ksq"])
                T.op("pe", lambda e: e.matmul(ps[5][:, 0:MEM], lhsT=ones, rhs=ksq, start=True, stop=True), reads=["ones", "ksq"], writes=["ps5"])
                self.rstd_from(rk, ps[5][:, 0:MEM], tmpm, 1.0 / 128, EPS, ["ps5"], "tmpm", "rk")
                T.op("dve", lambda e, b=b, h=h: e.scalar_tensor_tensor(out=kT[:, b, h, :], in0=kraw, scalar=gxk, in1=rk, op0=ALU.mult, op1=ALU.mult),
                     reads=["kraw", "rk", "vec"], writes=["kT"])
            for mb in range(MB):
                p = ps[2 + mb % 2]; pk = f"ps{2 + mb % 2}"
                for kc in range(KC):
                    T.op("pe", lambda e, kc=kc, mb=mb, p=p: e.matmul(p[:], lhsT=memb[:, kc, mb * 128:(mb + 1) * 128], rhs=wkv[:, kc, 512:1024],
                                                                  start=(kc == 0), stop=(kc == KC - 1)),
                         reads=["wkv", "memb"], writes=[pk])
                T.op("dve", lambda e, p=p, b=b, mb=mb: e.tensor_scalar(out=vv[:, b, mb, :], in0=p[:], scalar1=rtok[:, 2 * mb:2 * mb + 1], scalar2=None, op0=ALU.mult),
                     reads=[pk, "rtok"], writes=["vv"])
        yT_t = yT.rearrange("(kc p) n -> p kc n", p=128)
        ntile = self.NT // TS
        T.dma("sp", "xinB0", lambda e: e.dma_start(out=xin[0], in_=yT_t[:, :, 0:TS]), reads=[("y", 0)], writes=["xinB0"])
        sc_i = 0
        xoi = 0
        for t in range(ntile):
            c0 = t * TS
            b = c0 // S
            xi = xin[t % 2]; xik = f"xinB{t % 2}"
            if t + 1 < ntile:
                T.dma("sp", f"xinB{(t + 1) % 2}", lambda e, c1=c0 + TS, x2=xin[(t + 1) % 2]: e.dma_start(out=x2, in_=yT_t[:, :, c1:c1 + TS]),
                      reads=[("y", t + 1)], writes=[f"xinB{(t + 1) % 2}"])
            T.op("dve", lambda e, xi=xi: e.tensor_tensor(out=xb, in0=xi, in1=gxa.unsqueeze(2).to_broadcast([128, KC, TS]), op=ALU.mult),
                 reads=[xik, "vec"], writes=["xb"])
            T.op("act", lambda e, xi=xi: e.activation(out=sq, in_=xi, func=AF.Square), reads=[xik], writes=["sq"])
            for kc in range(KC):
                T.op("pe", lambda e, kc=kc: e.matmul(ps[7][:], lhsT=ones, rhs=sq[:, kc, :], start=(kc == 0), stop=(kc == KC - 1)),
                     reads=["ones", "sq"], writes=["ps7"])
            T.op("dve", lambda e: e.tensor_scalar(out=epsx, in0=ps[7][:], scalar1=EPS / D, scalar2=EPS * EPS, op0=ALU.mult, op1=ALU.add),
                 reads=["ps7"], writes=["epsx"])
            for h in range(4):
                p = ps[h]; pk = f"ps{h}"
                for kc in range(KC):
                    T.op("pe", lambda e, kc=kc, h=h, p=p: e.matmul(p[:], lhsT=wq[:, kc, h * 128:(h + 1) * 128], rhs=xb[:, kc, :],
                                                                start=(kc == 0), stop=(kc == KC - 1)),
                         reads=["wq", "xb"], writes=[pk])
                T.op("act", lambda e, p=p, h=h: e.activation(out=psq4[:, h, :], in_=p[:], func=AF.Square), reads=[pk], writes=[("psq", h)])
            for h in range(4):
                T.op("pe", lambda e, h=h: e.matmul(ps[4 + h][:], lhsT=ones, rhs=psq4[:, h, :], start=True, stop=True),
                     reads=["ones", ("psq", h)], writes=[f"ps{4 + h}"])
            for h in range(4):
                T.op("dve", lambda e, h=h: e.scalar_tensor_tensor(out=tq4[:, h, :], in0=ps[4 + h][:], scalar=1.0 / 128, in1=epsx, op0=ALU.mult, op1=ALU.add),
                     reads=[f"ps{4 + h}", "epsx"], writes=[("tq", h)])
            T.op("act", lambda e: e.activation(out=lq4, in_=tq4, func=AF.Ln), reads=[("tq", h) for h in range(4)], writes=["lq4"])
            T.op("act", lambda e: e.activation(out=rq4, in_=lq4, func=AF.Exp, scale=-0.5), reads=["lq4"], writes=["rq4"])
            for h in range(4):
                T.op("dve", lambda e, h=h: e.scalar_tensor_tensor(out=qT[:, h, :], in0=ps[h][:], scalar=gxq, in1=rq4[:, h, :], op0=ALU.mult, op1=ALU.mult),
                     reads=[f"ps{h}", "rq4", "vec"], writes=[("qT", h)])
            for h in range(4):
                for mb in range(MB):
                    sb_ = (h * MB + mb) % 4
                    pt = PT[(h * MB + mb) % 8]; ptk = f"PT{(h * MB + mb) % 8}"
                    T.op("pe", lambda e, h=h, mb=mb, sb_=sb_, b=b: e.matmul(ps[sb_][:], lhsT=kT[:, b, h, mb * 128:(mb + 1) * 128], rhs=qT[:, h, :],
                                                                         start=True, stop=True),
                         reads=["kT", ("qT", h)], writes=[f"ps{sb_}"])
                    T.op("act", lambda e, pt=pt, sb_=sb_: e.activation(out=pt, in_=ps[sb_][:], func=AF.Exp, scale=1.0 / math.sqrt(128.0)),
                         reads=[f"ps{sb_}"], writes=[ptk])
            for h in range(4):
                po = ps[4 + h % 2]; pok = f"ps{4 + h % 2}"
                pz = ps[6 + h % 2]; pzk = f"ps{6 + h % 2}"
                for mb in range(MB):
                    pt = PT[(h * MB + mb) % 8]; ptk = f"PT{(h * MB + mb) % 8}"
                    T.op("pe", lambda e, h=h, mb=mb, pt=pt, b=b, po=po: e.matmul(po[:], lhsT=vv[:, b, mb, h * 128:(h + 1) * 128], rhs=pt,
                                                                              start=(mb == 0), stop=(mb == MB - 1)),
                         reads=["vv", ptk], writes=[pok])
                for mb in range(MB):
                    pt = PT[(h * MB + mb) % 8]; ptk = f"PT{(h * MB + mb) % 8}"
                    T.op("pe", lambda e, mb=mb, pt=pt, pz=pz: e.matmul(pz[:], lhsT=ones, rhs=pt, start=(mb == 0), stop=(mb == MB - 1)),
                         reads=["ones", ptk], writes=[pzk])
                rc_ = rec2[h % 2]; rck = f"rec{h % 2}"
                T.op("act", lambda e, pz=pz, rc_=rc_: e.activation(out=rc_, in_=pz[:], func=AF.Ln), reads=[pzk], writes=[rck])
                T.op("act", lambda e, rc_=rc_: e.activation(out=rc_, in_=rc_, func=AF.Exp, scale=-1.0), reads=[rck], writes=[rck])
                T.op("dve", lambda e, h=h, po=po, rc_=rc_: e.tensor_tensor(out=ob[:, h, :], in0=po[:], in1=rc_, op=ALU.mult), reads=[pok, rck], writes=[("ob", h)])
            for oc in range(8):
                p = ps[oc % 2]; pk = f"ps{oc % 2}"
                x_ = xo[xoi % 3]; xk = f"xoB{xoi % 3}"; xoi += 1
                for h in range(4):
                    T.op("pe", lambda e, h=h, oc=oc, p=p: e.matmul(p[:], lhsT=wo[:, h, oc * 128:(oc + 1) * 128], rhs=ob[:, h, :],
                                                                start=(h == 0), stop=(h == 3)),
                         reads=["wo", ("ob", h)], writes=[pk])
                T.op("dve", lambda e, p=p, x_=x_, xi=xi, oc=oc: e.tensor_tensor(out=x_, in0=p[:], in1=xi[:, oc, :], op=ALU.add),
                     reads=[pk, xik], writes=[xk])
                T.dma("sp", xk, lambda e, x_=x_, oc=oc, c0=c0: e.dma_start(out=yT[oc * 128:(oc + 1) * 128, c0:c0 + TS], in_=x_),
                      reads=[xk], writes=[("y", t)])

    def phase_A1(self, l, src):
        T, A, nc, ps = self.T, self.A, self.nc, self.ps
        NB, S = self.NB, self.S
        T.default_cost = {"pe": 150, "act": 600, "dve": 500, "pool": 1000, "sp": 100}
        TS = 256
        KC = 8
        A.reset()
        al = A.alloc
        win = al([KC, NIN], BF16)
        wuq = al([3, 768], BF16)
        wuqs = al([3, 768], BF16)
        wuk = al([2, 512], BF16)
        wuv = al([2, 512], BF16)
        vec = al([NV], F32)
        ones = al([128], BF16)
        cst = al([128 + 64 + 256], F32)
        ident = al([128], BF16)
        tri = al([4, 4, 64], F32)
        m01 = al([4, TS], F32)
        lbt = al([4, 4], F32)
        lbe = al([4, 4], F32)
        lbs = al([4], F32)
        lbv = al([4], F32)
        xin = [al([KC, TS], F32) for _ in range(2)]
        xb = al([KC, TS], BF16)
        sq = al([KC, TS], BF16)
        rs0 = al([TS], F32)
        lt0 = al([TS], F32)
        rtok = al([8], F32)
        ttok = al([8], F32)
        cosb = [al([TS], F32) for _ in range(2)]
        sinb = [al([TS], F32) for _ in range(2)]
        cq = al([3, TS], F32)
        cqb = al([3, TS], BF16)
        cqs = al([3, TS], BF16)
        epsq = al([TS], F32)
        ckv = al([2, TS], F32)
        ckvs = al([2, TS], BF16)
        rskv = al([TS], F32)
        ckvb = al([2, TS], BF16)
        krt = al([TS], F32)
        kcat = al([8, TS], F32)
        ksw = al([TS], F32)
        kbt = al([TS], F32)
        ksq = al([8, TS], BF16)
        gen = [al([2, TS], F32) for _ in range(6)]
        qTo = al([8, TS], BF16)
        kTo = al([8, TS], BF16)
        vto = al([2, 8, 128], BF16)
        hq = al([4, TS], F32)
        hf = al([4, TS], F32)
        hg = al([4, TS], F32)
        hgen = [al([4, TS], F32) for _ in range(5)]
        qtl = al([4, TS], BF16)
        ktl = al([4, TS], BF16)
        ktok = al([4, 4, 64], BF16)
        vtok = al([4, 256], BF16)
        attb = al([4, 4, 64], BF16)
        sc8 = [al([4, 4], F32) for _ in range(6)]
        Sst = al([4, 64], F32)
        Ssc = al([4, 4, 64], BF16)
        tmpU = al([4, 64], F32)
        osq = al([4, TS], BF16)
        yho = al([4, TS], BF16)
        cbt = al([2, TS], F32)
        cct = al([2, TS], F32)
        cxt = al([2, TS], F32)
        ubuf = al([2, TS + 2], F32)
        yacc = al([2, TS], F32)
        yco = al([2, TS], BF16)

        T.op("pool", lambda e: e.memset(ones, 1.0), writes=["ones"])
        T.dma("sp", "vec", lambda e: e.dma_start(out=vec, in_=self.vecs[l]), writes=["vec"])
        T.dma("sp", "cst", lambda e: e.dma_start(out=cst, in_=self.consts), writes=["cst"])
        T.dma("sp", "lbt", lambda e: e.dma_start(out=lbt[0:64].rearrange("p a b -> p (a b)"), in_=self.lbl), writes=["lbt"])
        T.op("dve", lambda e: e.tensor_copy(out=ident, in_=cst[:, 0:128]), reads=["cst"], writes=["ident"])
        T.op("dve", lambda e: e.tensor_copy(out=tri[0:64], in_=cst[0:64, 128:192].unsqueeze(1).unsqueeze(1).to_broadcast([64, 4, 4, 64])),
             reads=["cst"], writes=["tri"])
        T.op("dve", lambda e: e.tensor_copy(out=m01[0:64], in_=cst[0:64, 192:448].unsqueeze(1).to_broadcast([64, 4, TS])),
             reads=["cst"], writes=["m01"])
        T.op("pool", lambda e: e.memset(ksw, 0.0), writes=["ksw"])
        T.op("pool", lambda e: e.memset(vto, 1.0), writes=["vto"])
        T.op("act", lambda e: e.activation(out=lbe[0:64], in_=lbt[0:64], func=AF.Exp), reads=["lbt"], writes=["lbe"])
        T.op("dve", lambda e: e.tensor_reduce(out=lbs[0:64], in_=lbe[0:64], axis=mybir.AxisListType.X, op=ALU.add), reads=["lbe"], writes=["lbs"])
        T.op("dve", lambda e: e.reciprocal(out=lbs[0:64], in_=lbs[0:64]), reads=["lbs"], writes=["lbs"])
        if l == 0:
            T.op("pool", lambda e: e.memset(lbv, 0.0), writes=["lbv"])
        else:
            T.op("dve", lambda e: e.tensor_reduce(out=lbv[0:64], in_=lbe[0:64, :, 1:l + 1], axis=mybir.AxisListType.X, op=ALU.add),
                 reads=["lbe"], writes=["lbv"])
            T.op("dve", lambda e: e.tensor_tensor(out=lbv[0:64], in0=lbv[0:64], in1=lbs[0:64], op=ALU.mult), reads=["lbv", "lbs"], writes=["lbv"])
        self.wload(win.rearrange("p a b -> p (a b)"), self.w_in[l], "win", NIN, KC * NIN)
        self.wload(wuq.rearrange("p a b -> p (a b)"), self.w_uq[l], "wuq", 2304, 2304)
        self.wload(wuqs.rearrange("p a b -> p (a b)"), self.w_uqsw[l], "wuqs", 2304, 2304)
        self.wload(wuk.rearrange("p a b -> p (a b)"), self.w_uk[l], "wuk", 1024, 1024)
        self.wload(wuv.rearrange("p a b -> p (a b)"), self.w_uv[l], "wuv", 1024, 1024)

        gmix = vec[:, V_GMIX:V_GMIX + 8]
        gq1 = vec[:, V_GQ1:V_GQ1 + 3]
        gkv = vec[:, V_GKV:V_GKV + 2]
        gqn = vec[0:96, V_GQN:V_GQN + 1]
        gqns = vec[0:96, V_GQNSW:V_GQNSW + 1]
        gkn = vec[0:96, V_GKN:V_GKN + 1]
        gkns = vec[0:96, V_GKNSW:V_GKNSW + 1]
        go = vec[0:64, V_GO:V_GO + 4]
        cw = vec[:, V_CW:V_CW + 6]
        src_t = src.rearrange("(kc p) n -> p kc n", p=128)
        ntile = self.NT // TS
        tps = S // TS
        T.dma("sp", "xinA0", lambda e: e.dma_start(out=xin[0], in_=src_t[:, :, 0:TS]), reads=[("y", 0)], writes=["xinA0"])
        pcnt = [0]

        def pbank():
            b_ = pcnt[0] % 4
            pcnt[0] += 1
            return ps[b_], f"ps{b_}"

        for t in range(ntile):
            c0 = t * TS
            b = c0 // S
            s0 = c0 - b * S
            xi = xin[t % 2]; xik = f"xinA{t % 2}"
            cs = cosb[t % 2]; csk = f"cos{t % 2}"; sn = sinb[t % 2]; snk = f"sin{t % 2}"
            T.dma("sp", csk, lambda e, cs=cs, s0=s0: e.dma_start(out=cs[0:96], in_=self.cosT[:, s0:s0 + TS]), writes=[csk])
            T.dma("sp", snk, lambda e, sn=sn, s0=s0: e.dma_start(out=sn[0:96], in_=self.sinS[:, s0:s0 + TS]), writes=[snk])
            if t + 1 < ntile:
                T.dma("sp", f"xinA{(t + 1) % 2}", lambda e, c1=c0 + TS, x2=xin[(t + 1) % 2]: e.dma_start(out=x2, in_=src_t[:, :, c1:c1 + TS]),
                      reads=[("y", (t + 1) // 2)], writes=[f"xinA{(t + 1) % 2}"])
            T.default_cost = {"pe": 150, "act": 600, "dve": 500, "pool": 1000, "sp": 100}
            if s0 == 0:
                T.op("pool", lambda e: e.memset(Sst, 0.0), writes=["Sst"])
                T.op("pool", lambda e: e.memset(ubuf, 0.0), writes=["ubuf"])
            T.op("dve", lambda e, xi=xi: e.tensor_tensor(out=xb, in0=xi, in1=gmix.unsqueeze(2).to_broadcast([128, KC, TS]), op=ALU.mult),
                 reads=[xik, "vec"], writes=["xb"])
            T.op("act", lambda e, xi=xi: e.activation(out=sq, in_=xi, func=AF.Square), reads=[xik], writes=["sq"])
            for kc in range(KC):
                T.op("pe", lambda e, kc=kc: e.matmul(ps[7][:, 0:TS], lhsT=ones, rhs=sq[:, kc, :], start=(kc == 0), stop=(kc == KC - 1)),
                     reads=["ones", "sq"], writes=["ps7"])
            self.rstd_from(rs0, ps[7][:, 0:TS], lt0, 1.0 / D, EPS, ["ps7"], "lt0", "rs0")
            for blk in range(2):
                for kc in range(KC):
                    T.op("pe", lambda e, kc=kc, blk=blk: e.matmul(ps[6][:, 2 * blk:2 * blk + 2], lhsT=sq[:, kc, blk * 128:(blk + 1) * 128], rhs=ones[:, 0:2],
                                                                 start=(kc == 0), stop=(kc == KC - 1)),
                         reads=["ones", "sq"], writes=["ps6"])
            self.rstd_from(rtok[:, 0:4], ps[6][:, 0:4], ttok[:, 0:4], 1.0 / D, EPS, ["ps6"], "ttok", "rtok")

            def inproj(col0, M, dst, dkey, extra_reads=()):
                p, pk = pbank()
                for kc in range(KC):
                    T.op("pe", lambda e, kc=kc, p=p: e.matmul(p[0:M, 0:TS], lhsT=win[:, kc, col0:col0 + M], rhs=xb[:, kc, :],
                                                             start=(kc == 0), stop=(kc == KC - 1)),
                         reads=["win", "xb"], writes=[pk])
                T.op("dve", lambda e, p=p: e.tensor_tensor(out=dst, in0=p[0:M, 0:TS], in1=rs0[0:M], op=ALU.mult),
                     reads=[pk, "rs0"], writes=[dkey])

            T.mute = 'Q' not in A1PARTS
            for j in range(3):
                inproj(O_CQ + j * 128, 128, cq[:, j, :], ("cq", j))
            T.op("act", lambda e: e.activation(out=cqs, in_=cq, func=AF.Square), reads=[("cq", 0), ("cq", 1), ("cq", 2)], writes=["cqs"])
            T.op("dve", lambda e: e.tensor_tensor(out=cqb, in0=cq, in1=gq1.unsqueeze(2).to_broadcast([128, 3, TS]), op=ALU.mult),
                 reads=[("cq", 0), ("cq", 1), ("cq", 2), "vec"], writes=["cqb"])
            for j in range(3):
                T.op("pe", lambda e, j=j: e.matmul(ps[7][:, 0:TS], lhsT=ones, rhs=cqs[:, j, :], start=(j == 0), stop=(j == 2)),
                     reads=["ones", "cqs"], writes=["ps7"])
            self.rstd_from(epsq, ps[7][:, 0:TS], lt0, 1.0 / 384, EPS, ["ps7"], "lt0", "epsq")
            T.op("dve", lambda e: e.tensor_tensor(out=cqb, in0=cqb, in1=epsq.unsqueeze(1).to_broadcast([128, 3, TS]), op=ALU.mult),
                 reads=["cqb", "epsq"], writes=["cqb"])
            for hp in range(4):
                pa, pak = pbank()
                pb, pbk = pbank()
                for hh in range(2):
                    h = hp * 2 + hh
                    for j in range(3):
                        T.op("pe", lambda e, j=j, h=h, hh=hh, pa=pa: e.matmul(pa[0:96, hh * TS:(hh + 1) * TS], lhsT=wuq[:, j, h * 96:(h + 1) * 96], rhs=cqb[:, j, :],
                                                                            start=(j == 0), stop=(j == 2)),
                             reads=["wuq", "cqb"], writes=[pak])
                for hh in range(2):
                    h = hp * 2 + hh
                    for j in range(3):
                        T.op("pe", lambda e, j=j, h=h, hh=hh, pb=pb: e.matmul(pb[0:96, hh * TS:(hh + 1) * TS], lhsT=wuqs[:, j, h * 96:(h + 1) * 96], rhs=cqb[:, j, :],
                                                                            start=(j == 0), stop=(j == 2)),
                             reads=["wuqs", "cqb"], writes=[pbk])
                if QSTAGE < 1:
                    continue
                g0, g1, g2, g3 = gen[0], gen[1], gen[2], gen[3]
                g0f = g0.rearrange("p a b -> p (a b)"); g1f = g1.rearrange("p a b -> p (a b)")
                g2f = g2.rearrange("p a b -> p (a b)"); g3f = g3.rearrange("p a b -> p (a b)")
                sqp = ksq[0:96, 0:2, :].rearrange("p a b -> p (a b)")
                T.op("act", lambda e, pa=pa, sqp=sqp: e.activation(out=sqp, in_=pa[0:96, :], func=AF.Square), reads=[pak], writes=["ksq"])
                T.op("pe", lambda e, sqp=sqp: e.matmul(ps[5][0:96, :], lhsT=ones[0:96, 0:96], rhs=sqp, start=True, stop=True),
                     reads=["ones", "ksq"], writes=["ps5"])
                self.rstd_from(g1f[0:96], ps[5][0:96, :], g0f[0:96], 1.0 / 96, EPS, ["ps5"], "g0", "g1")
                if QSTAGE < 2:
                    continue
                csb = cs[0:96].unsqueeze(1).to_broadcast([96, 2, TS])
                snb = sn[0:96].unsqueeze(1).to_broadcast([96, 2, TS])
                for hh in range(2):
                    T.op("dve", lambda e, pa=pa, g2=g2, hh=hh, cs=cs: e.scalar_tensor_tensor(out=g2[0:96, hh, :], in0=pa[0:96, hh * TS:(hh + 1) * TS], scalar=gqn, in1=cs[0:96],
                                                                                       op0=ALU.mult, op1=ALU.mult),
                         reads=[pak, csk, "vec"], writes=["g2"])
                    T.op("dve", lambda e, pb=pb, g3=g3, hh=hh, sn=sn: e.scalar_tensor_tensor(out=g3[0:96, hh, :], in0=pb[0:96, hh * TS:(hh + 1) * TS], scalar=gqns, in1=sn[0:96],
                                                                                       op0=ALU.mult, op1=ALU.mult),
                         reads=[pbk, snk, "vec"], writes=["g3"])
                T.op("dve", lambda e, g2=g2, g3=g3: e.tensor_tensor(out=g2[0:96], in0=g2[0:96], in1=g3[0:96], op=ALU.add), reads=["g2", "g3"], writes=["g2"])
                T.op("dve", lambda e, g2=g2, g1=g1, hp=hp: e.tensor_tensor(out=qTo[0:96, hp * 2:hp * 2 + 2, :], in0=g2[0:96], in1=g1[0:96], op=ALU.mult),
                     reads=["g2", "g1"], writes=["qTo"])
            T.dma("sp", "qst", lambda e, b=b, s0=s0: e.dma_start(out=self.q_s[b, :, :, s0:s0 + TS].rearrange("h p n -> p h n"), in_=qTo[0:96]),
                  reads=["qTo"], writes=[("q_s", b)])

            T.mute = 'K' not in A1PARTS
            for j in range(2):
                inproj(O_CKV + j * 128, 128, ckv[:, j, :], ("ckv", j))
            T.op("act", lambda e: e.activation(out=ckvs, in_=ckv, func=AF.Square), reads=[("ckv", 0), ("ckv", 1)], writes=["ckvs"])
            for j in range(2):
                T.op("pe", lambda e, j=j: e.matmul(ps[7][:, 0:TS], lhsT=ones, rhs=ckvs[:, j, :], start=(j == 0), stop=(j == 1)),
                     reads=["ones", "ckvs"], writes=["ps7"])
            self.rstd_from(rskv, ps[7][:, 0:TS], lt0, 1.0 / 256, EPS, ["ps7"], "lt0", "rskv")
            T.op("dve", lambda e: e.tensor_tensor(out=ckv, in0=ckv, in1=rskv.unsqueeze(1).to_broadcast([128, 2, TS]), op=ALU.mult),
                 reads=[("ckv", 0), ("ckv", 1), "rskv"], writes=[("ckv", 0), ("ckv", 1)])
            T.op("dve", lambda e: e.tensor_tensor(out=ckvb, in0=ckv, in1=gkv.unsqueeze(2).to_broadcast([128, 2, TS]), op=ALU.mult),
                 reads=[("ckv", 0), ("ckv", 1), "vec"], writes=["ckvb"])
            inproj(O_KR, 64, krt[0:64], "krt")
            T.op("act", lambda e: e.activation(out=kcat[64:96], in_=krt[0:32].unsqueeze(1).to_broadcast([32, 8, TS]), func=AF.Copy),
                 reads=["krt"], writes=["kcat_r"])
            T.op("act", lambda e: e.activation(out=ksw[64:96], in_=krt[32:64], func=AF.Copy), reads=["krt"], writes=["ksw"])
            for hp in range(4):
                p, pk = pbank()
                for hh in range(2):
                    h = hp * 2 + hh
                    for j in range(2):
                        T.op("pe", lambda e, j=j, h=h, hh=hh, p=p: e.matmul(p[0:64, hh * TS:(hh + 1) * TS], lhsT=wuk[:, j, h * 64:(h + 1) * 64], rhs=ckvb[:, j, :],
                                                                          start=(j == 0), stop=(j == 1)),
                             reads=["wuk", "ckvb"], writes=[pk])
                T.op("act", lambda e, p=p, hp=hp: e.activation(out=kcat[0:64, hp * 2:hp * 2 + 2, :], in_=p[0:64, :].rearrange("p (a b) -> p a b", a=2), func=AF.Copy),
                     reads=[pk], writes=[("kcat_n", hp)])
            kcat_keys = ["kcat_r"] + [("kcat_n", hp) for hp in range(4)]
            T.op("act", lambda e: e.activation(out=ksq[0:96], in_=kcat[0:96], func=AF.Square), reads=kcat_keys, writes=["ksq"])
            T.op("dve", lambda e, sn=sn: e.scalar_tensor_tensor(out=kbt[0:96], in0=ksw[0:96], scalar=gkns, in1=sn[0:96], op0=ALU.mult, op1=ALU.mult),
                 reads=["ksw", snk, "vec"], writes=["kbt"])
            for hp in range(4):
                g0, g1, g2 = gen[0], gen[1], gen[2]
                g0f = g0.rearrange("p a b -> p (a b)"); g1f = g1.rearrange("p a b -> p (a b)")
                T.op("pe", lambda e, hp=hp: e.matmul(ps[5][0:96, :], lhsT=ones[0:96, 0:96], rhs=ksq[0:96, hp * 2:hp * 2 + 2, :].rearrange("p a b -> p (a b)"),
                                                    start=True, stop=True),
                     reads=["ones", "ksq"], writes=["ps5"])
                self.rstd_from(g1f[0:96], ps[5][0:96, :], g0f[0:96], 1.0 / 96, EPS, ["ps5"], "g0", "g1")
                csb = cs[0:96].unsqueeze(1).to_broadcast([96, 2, TS])
                T.op("dve", lambda e, g2=g2, hp=hp, csb=csb: e.scalar_tensor_tensor(out=g2[0:96], in0=kcat[0:96, hp * 2:hp * 2 + 2, :], scalar=gkn, in1=csb,
                                                                                op0=ALU.mult, op1=ALU.mult),
                     reads=kcat_keys + [csk, "vec"], writes=["g2"])
                T.op("dve", lambda e, g2=g2: e.tensor_tensor(out=g2[0:96], in0=g2[0:96], in1=kbt[0:96].unsqueeze(1).to_broadcast([96, 2, TS]), op=ALU.add),
                     reads=["g2", "kbt"], writes=["g2"])
                T.op("dve", lambda e, g2=g2, g1=g1, hp=hp: e.tensor_tensor(out=kTo[0:96, hp * 2:hp * 2 + 2, :], in0=g2[0:96], in1=g1[0:96], op=ALU.mult),
                     reads=["g2", "g1"], writes=["kTo"])
            T.dma("sp", "kst", lambda e, b=b, s0=s0: e.dma_start(out=self.k_s[b, :, :, s0:s0 + TS].rearrange("h p n -> p h n"), in_=kTo[0:96]),
                  reads=["kTo"], writes=[("k_s", b)])
            for blk in range(2):
                p, pk = pbank()
                for j in range(2):
                    T.op("pe", lambda e, j=j, blk=blk, p=p: e.matmul(p[:], lhsT=ckvb[:, j, blk * 128:(blk + 1) * 128], rhs=wuv[:, j, :],
                                                                  start=(j == 0), stop=(j == 1)),
                         reads=["wuv", "ckvb"], writes=[pk])
                p3 = p[:].rearrange("p (h d) -> p h d", h=8)
                T.op("act", lambda e, p3=p3, blk=blk: e.activation(out=vto[:, blk, 0:8:2, 0:64], in_=p3[:, 0:8:2, :], func=AF.Copy),
                     reads=[pk], writes=["vto"])
                T.op("act", lambda e, p3=p3, blk=blk: e.activation(out=vto[:, blk, 1:8:2, 64:128], in_=p3[:, 1:8:2, :], func=AF.Copy),
                     reads=[pk], writes=["vto"])
            T.dma("sp", "vst", lambda e, b=b, s0=s0: e.dma_start(out=self.v_s[b, s0:s0 + TS, :].rearrange("(k p) c -> p k c", p=128),
                                                               in_=vto.rearrange("p k h w -> p k (h w)")),
                  reads=["vto"], writes=[("v_s", b)])

            T.mute = 'H' not in A1PARTS
            for (col, dstt, nm) in ((O_HQ, hq, "hq"), (O_HF, hf, "hf"), (O_HG, hg, "hg")):
                for j in range(2):
                    p, pk = pbank()
                    for kc in range(KC):
                        T.op("pe", lambda e, kc=kc, p=p, col=col, j=j: e.matmul(p[:, 0:TS], lhsT=win[:, kc, col + j * 128:col + (j + 1) * 128], rhs=xb[:, kc, :],
                                                                            start=(kc == 0), stop=(kc == KC - 1)),
                             reads=["win", "xb"], writes=[pk])
                    T.op("dve", lambda e, p=p, dstt=dstt, j=j: e.tensor_tensor(out=dstt[0:64, 2 * j, :], in0=p[0:64, 0:TS], in1=rs0[0:64], op=ALU.mult),
                         reads=[pk, "rs0"], writes=[(nm, 2 * j)])
                    T.op("dve", lambda e, p=p, dstt=dstt, j=j: e.tensor_tensor(out=dstt[0:64, 2 * j + 1, :], in0=p[64:128, 0:TS], in1=rs0[64:128], op=ALU.mult),
                         reads=[pk, "rs0"], writes=[(nm, 2 * j + 1)])
            hqk = [("hq", h) for h in range(4)]; hfk = [("hf", h) for h in range(4)]; hgk = [("hg", h) for h in range(4)]
            for blk in range(2):
                p, pk = pbank()
                for kc in range(KC):
                    T.op("pe", lambda e, kc=kc, blk=blk, p=p: e.matmul(p[:, 0:256], lhsT=xb[:, kc, blk * 128:(blk + 1) * 128], rhs=win[:, kc, O_HI:O_HI + 256],
                                                                    start=(kc == 0), stop=(kc == KC - 1)),
                         reads=["win", "xb"], writes=[pk])
                T.op("dve", lambda e, p=p, blk=blk: e.tensor_scalar(out=vtok[0:64, 2 * blk, :], in0=p[0:64, 0:256], scalar1=rtok[0:64, 2 * blk:2 * blk + 1], scalar2=None, op0=ALU.mult),
                     reads=[pk, "rtok"], writes=["vtok"])
                T.op("dve", lambda e, p=p, blk=blk: e.tensor_scalar(out=vtok[0:64, 2 * blk + 1, :], in0=p[64:128, 0:256], scalar1=rtok[64:128, 2 * blk:2 * blk + 1], scalar2=None, op0=ALU.mult),
                     reads=[pk, "rtok"], writes=["vtok"])
            T.default_cost = {"pe": 120, "act": 1900, "dve": 1600, "pool": 1000, "sp": 100}
            E, L1, L2, Bc, Wk = hgen
            H = slice(0, 64)
            lbb = lbv[0:64].unsqueeze(2).to_broadcast([64, 4, TS])
            T.op("act", lambda e: e.activation(out=E[H], in_=hf[H], func=AF.Exp, scale=-1.0), reads=hfk, writes=["E"])
            T.op("act", lambda e: e.activation(out=L1[H], in_=E[H], func=AF.Ln, scale=1.0, bias=1.0), reads=["E"], writes=["L1"])
            T.op("dve", lambda e: e.tensor_tensor(out=E[H], in0=E[H], in1=lbb, op=ALU.mult), reads=["E", "lbv"], writes=["E"])
            T.op("act", lambda e: e.activation(out=L2[H], in_=E[H], func=AF.Ln, scale=1.0, bias=1.0), reads=["E"], writes=["L2"])
            T.op("dve", lambda e: e.tensor_tensor(out=L2[H], in0=L2[H], in1=L1[H], op=ALU.subtract), reads=["L1", "L2"], writes=["L2"])
            T.op("dve", lambda e: e.tensor_tensor_scan(out=Bc[H].rearrange("p a b -> p (a b)"), data0=m01[H].rearrange("p a b -> p (a b)"),
                                                       data1=L2[H].rearrange("p a b -> p (a b)"), initial=0.0, op0=ALU.mult, op1=ALU.add),
                 reads=["m01", "L2"], writes=["Bc"])
            B4 = Bc[H].rearrange("p h (c t) -> p h c t", t=64)
            blast, cmid, e1, e2, ec, scx = sc8
            T.op("dve", lambda e: e.tensor_copy(out=blast[H], in_=B4[:, :, :, 63]), reads=["Bc"], writes=["blast"])
            T.op("dve", lambda e: e.tensor_copy(out=cmid[H], in_=B4[:, :, :, 31]), reads=["Bc"], writes=["cmid"])
            T.op("act", lambda e: e.activation(out=e1[H], in_=blast[H], func=AF.Exp), reads=["blast"], writes=["e1"])
            T.op("act", lambda e: e.activation(out=ec[H], in_=cmid[H], func=AF.Exp), reads=["cmid"], writes=["ec"])
            T.op("dve", lambda e: e.tensor_tensor(out=scx[H], in0=blast[H], in1=cmid[H], op=ALU.subtract), reads=["blast", "cmid"], writes=["scx"])
            T.op("act", lambda e: e.activation(out=e2[H], in_=scx[H], func=AF.Exp), reads=["scx"], writes=["e2"])
            T.op("dve", lambda e: e.tensor_tensor(out=B4, in0=B4, in1=cmid[H].unsqueeze(3).to_broadcast([64, 4, 4, 64]), op=ALU.subtract),
                 reads=["Bc", "cmid"], writes=["Bc"])
            T.op("act", lambda e: e.activation(out=L1[H], in_=L2[H], func=AF.Exp), reads=["L2"], writes=["L1"])
            T.op("dve", lambda e: e.tensor_scalar(out=L1[H], in0=L1[H], scalar1=-1.0, scalar2=1.0, op0=ALU.mult, op1=ALU.add), reads=["L1"], writes=["L1"])
            T.op("act", lambda e: e.activation(out=Wk[H], in_=Bc[H], func=AF.Exp, scale=-1.0), reads=["Bc"], writes=["Wk"])
            T.op("dve", lambda e: e.tensor_tensor(out=ktl[H], in0=L1[H], in1=Wk[H], op=ALU.mult), reads=["L1", "Wk"], writes=["ktl"])
            T.op("act", lambda e: e.activation(out=E[H], in_=hq[H], func=AF.Exp, scale=-1.0), reads=hqk, writes=["E"])
            T.op("act", lambda e: e.activation(out=E[H], in_=E[H], func=AF.Ln, scale=1.0, bias=1.0), reads=["E"], writes=["E"])
            T.op("dve", lambda e: e.tensor_tensor(out=E[H], in0=Bc[H], in1=E[H], op=ALU.subtract), reads=["E", "Bc"], writes=["E"])
            T.op("act", lambda e: e.activation(out=Wk[H], in_=E[H], func=AF.Exp), reads=["E"], writes=["Wk"])
            T.op("dve", lambda e: e.tensor_tensor(out=qtl[H], in0=hq[H], in1=Wk[H], op=ALU.mult), reads=hqk + ["Wk"], writes=["qtl"])
            T.op("act", lambda e: e.activation(out=L2[H], in_=hg[H], func=AF.Exp, scale=-1.0), reads=hgk, writes=["L2"])
            T.op("act", lambda e: e.activation(out=L2[H], in_=L2[H], func=AF.Ln, scale=1.0, bias=1.0), reads=["L2"], writes=["L2"])
            T.op("act", lambda e: e.activation(out=L2[H], in_=L2[H], func=AF.Exp, scale=-1.0), reads=["L2"], writes=["L2"])
            T.op("dve", lambda e: e.tensor_tensor(out=L2[H], in0=L2[H], in1=hg[H], op=ALU.mult), reads=["L2"] + hgk, writes=["L2"])
            T.default_cost = {"pe": 120, "act": 600, "dve": 500, "pool": 1000, "sp": 100}
            ptr = ps[4][:].bitcast(BF16)
            for c in range(4):
                for h in range(4):
                    T.op("pe", lambda e, c=c, h=h: e.transpose(ptr[0:64, (c * 4 + h) * 64:(c * 4 + h + 1) * 64], ktl[0:64, h, c * 64:(c + 1) * 64], ident[0:64, 0:64]),
                         reads=["ktl", "ident"], writes=["ps4"])
            T.op("act", lambda e: e.activation(out=ktok[H].rearrange("p a b c -> p (a b c)"), in_=ptr[0:64, 0:1024], func=AF.Copy), reads=["ps4"], writes=["ktok"])
            for h in range(4):
                pb_ = ps[5] if h < 2 else ps[6]
                for c in range(4):
                    T.op("pe", lambda e, h=h, c=c, pb_=pb_: e.matmul(pb_[0:64, ((h % 2) * 4 + c) * 64:((h % 2) * 4 + c + 1) * 64],
                                                                   lhsT=ktl[0:64, h, c * 64:(c + 1) * 64], rhs=qtl[0:64, h, c * 64:(c + 1) * 64], start=True, stop=True),
                         reads=["ktl", "qtl"], writes=["ps5" if h < 2 else "ps6"])
            for hh2 in range(2):
                pb_ = ps[5 + hh2]
                T.op("dve", lambda e, hh2=hh2, pb_=pb_: e.tensor_tensor(out=attb[0:64, hh2 * 2:hh2 * 2 + 2], in0=pb_[0:64, :].rearrange("p (a b c) -> p a b c", a=2, b=4),
                                                                      in1=tri[0:64, 0:2], op=ALU.mult),
                     reads=[f"ps{5 + hh2}", "tri"], writes=["attb"])
            for h in range(4):
                pb_ = ps[7] if h < 2 else ps[4]
                for c in range(4):
                    T.op("pe", lambda e, h=h, c=c, pb_=pb_: e.matmul(pb_[0:64, ((h % 2) * 4 + c) * 64:((h % 2) * 4 + c + 1) * 64],
                                                                   lhsT=ktok[0:64, c, h, :], rhs=vtok[0:64, c, h * 64:(h + 1) * 64], start=True, stop=True),
                         reads=["ktok", "vtok"], writes=["ps7" if h < 2 else "ps4"])
            U7 = ps[7][0:64, :].rearrange("p (a c d) -> p a c d", a=2, c=4)
            U4 = ps[4][0:64, :].rearrange("p (a c d) -> p a c d", a=2, c=4)
            for c in range(4):
                T.op("dve", lambda e, c=c: e.tensor_tensor(out=Ssc[0:64, :, c, :], in0=Sst[H], in1=ec[0:64, :, c:c + 1].to_broadcast([64, 4, 64]), op=ALU.mult),
                     reads=["Sst", "ec"], writes=[("Ssc", c)])
                T.op("dve", lambda e, c=c: e.tensor_tensor(out=tmpU[0:64, 0:2, :], in0=U7[:, :, c, :], in1=e2[0:64, 0:2, c:c + 1].to_broadcast([64, 2, 64]), op=ALU.mult),
                     reads=["ps7", "e2"], writes=["tmpU0"])
                T.op("dve", lambda e, c=c: e.tensor_tensor(out=tmpU[0:64, 2:4, :], in0=U4[:, :, c, :], in1=e2[0:64, 2:4, c:c + 1].to_broadcast([64, 2, 64]), op=ALU.mult),
                     reads=["ps4", "e2"], writes=["tmpU1"])
                T.op("dve", lambda e, c=c: e.tensor_tensor(out=Sst[H], in0=Sst[H], in1=e1[0:64, :, c:c + 1].to_broadcast([64, 4, 64]), op=ALU.mult),
                     reads=["Sst", "e1", ("Ssc", c)], writes=["Sst"])
                T.op("dve", lambda e: e.tensor_tensor(out=Sst[H], in0=Sst[H], in1=tmpU[H], op=ALU.add), reads=["Sst", "tmpU0", "tmpU1"], writes=["Sst"])
            for h in range(4):
                p, pk = pbank()
                for c in range(4):
                    T.op("pe", lambda e, h=h, c=c, p=p: e.matmul(p[0:64, c * 64:(c + 1) * 64], lhsT=vtok[0:64, c, h * 64:(h + 1) * 64], rhs=attb[0:64, h, c, :],
                                                              start=True, stop=False),
                         reads=["vtok", "attb"], writes=[pk])
                    T.op("pe", lambda e, h=h, c=c, p=p: e.matmul(p[0:64, c * 64:(c + 1) * 64], lhsT=Ssc[0:64, h, c, :], rhs=qtl[0:64, h, c * 64:(c + 1) * 64],
                                                              start=False, stop=True),
                         reads=[("Ssc", c), "qtl"], writes=[pk])
                T.op("act", lambda e, p=p, h=h: e.activation(out=osq[0:64, h, :], in_=p[0:64, 0:TS], func=AF.Square), reads=[pk], writes=[("osq", h)])
                T.op("pe", lambda e, h=h: e.matmul(ps[5][0:64, 0:TS], lhsT=ones[0:64, 0:64], rhs=osq[0:64, h, :], start=True, stop=True),
                     reads=["ones", ("osq", h), "attb"], writes=["ps5"])
                g0, g1 = gen[4], gen[5]
                g0f = g0.rearrange("p a b -> p (a b)"); g1f = g1.rearrange("p a b -> p (a b)")
                self.rstd_from(g1f[0:64, 0:TS], ps[5][0:64, 0:TS], g0f[0:64, 0:TS], 1.0 / 64, EPS, ["ps5"], "g4", "g5")
                T.op("dve", lambda e, p=p, h=h, g0f=g0f, g1f=g1f: e.scalar_tensor_tensor(out=g0f[0:64, 0:TS], in0=p[0:64, 0:TS], scalar=go[:, h:h + 1], in1=g1f[0:64, 0:TS],
                                                                                     op0=ALU.mult, op1=ALU.mult),
                     reads=[pk, "g5", "vec"], writes=["g4"])
                T.op("dve", lambda e, h=h, g0f=g0f: e.tensor_tensor(out=yho[0:64, h, :], in0=g0f[0:64, 0:TS], in1=L2[0:64, h, :], op=ALU.mult),
                     reads=["g4", "L2"], writes=["yho"])
            T.dma("sp", "yhst", lambda e, b=b, s0=s0: e.dma_start(out=self.yh_s[b, :, :, s0:s0 + TS].rearrange("h p n -> p h n"), in_=yho[0:64]),
                  reads=["yho"], writes=[("yh_s", b)])

            T.mute = 'C' not in A1PARTS
            for j in range(2):
                inproj(O_CB + j * 128, 128, cbt[:, j, :], ("cb", j))
                inproj(O_CC + j * 128, 128, cct[:, j, :], ("cc", j))
                inproj(O_CX + j * 128, 128, cxt[:, j, :], ("cx", j))
            ck = [("cc", 0), ("cc", 1), ("cx", 0), ("cx", 1)]
            T.op("dve", lambda e: e.tensor_tensor(out=ubuf[:, :, 2:TS + 2], in0=cct, in1=cxt, op=ALU.mult), reads=ck + ["ucarry"], writes=["ubuf"])
            for j in range(2):
                T.op("dve", lambda e, j=j: e.tensor_scalar(out=yacc[:, j, :], in0=ubuf[:, j, 2:TS + 2], scalar1=cw[:, j * 3 + 2:j * 3 + 3], scalar2=None, op0=ALU.mult),
                     reads=["ubuf", "vec"], writes=[("yacc", j)])
                T.op("dve", lambda e, j=j: e.scalar_tensor_tensor(out=yacc[:, j, :], in0=ubuf[:, j, 1:TS + 1], scalar=cw[:, j * 3 + 1:j * 3 + 2], in1=yacc[:, j, :],
                                                                  op0=ALU.mult, op1=ALU.add),
                     reads=["ubuf", "vec", ("yacc", j)], writes=[("yacc", j)])
                T.op("dve", lambda e, j=j: e.scalar_tensor_tensor(out=yacc[:, j, :], in0=ubuf[:, j, 0:TS], scalar=cw[:, j * 3:j * 3 + 1], in1=yacc[:, j, :],
                                                                  op0=ALU.mult, op1=ALU.add),
                     reads=["ubuf", "vec", ("yacc", j)], writes=[("yacc", j)])
            T.op("dve", lambda e: e.tensor_tensor(out=yco, in0=yacc, in1=cbt, op=ALU.mult),
                 reads=[("yacc", 0), ("yacc", 1), ("cb", 0), ("cb", 1)], writes=["yco"])
            T.op("dve", lambda e: e.tensor_copy(out=ubuf[:, :, 0:2], in_=ubuf[:, :, TS:TS + 2]), reads=["ubuf"], writes=["ucarry", "ubuf"])
            T.dma("sp", "ycst", lambda e, b=b, s0=s0: e.dma_start(out=self.yc_s[b, :, s0:s0 + TS].rearrange("(j p) n -> p j n", p=128), in_=yco),
                  reads=["yco"], writes=[("yc_s", b)])
            T.mute = False

    def phase_A2(self, l, src):
        T, A, nc, ps = self.T, self.A, self.nc, self.ps
        NB, S = self.NB, self.S
        T.default_cost = {"pe": 260, "act": 550, "dve": 700, "pool": 1000, "sp": 100}
        TS = 512
        NKB = S // 128
        A.reset()
        al = A.alloc
        kc_ = al([8, S], BF16)
        vc_ = al([NKB, VROW], BF16)
        wom = al([4, D], BF16)
        woh = al([2, D], BF16)
        woc = al([2, D], BF16)
        cm = al([4, 512], BF16)
        ident = al([128], BF16)
        cst = al([128 + 64 + 256], F32)
        qt = [al([8, TS], BF16) for _ in range(2)]
        yh = al([2, TS], BF16)
        yc = al([2, TS], BF16)
        ym = al([4, TS], BF16)
        PT = [al([TS], BF16) for _ in range(4)]
        rc2 = [al([TS], F32) for _ in range(2)]
        xr = [al([TS], F32) for _ in range(3)]
        yT = self.yT
        T.dma("sp", "cst", lambda e: e.dma_start(out=cst, in_=self.consts), writes=["cst"])
        T.op("dve", lambda e: e.tensor_copy(out=ident, in_=cst[:, 0:128]), reads=["cst"], writes=["ident"])
        T.dma("pool", "cm", lambda e: e.dma_start(out=cm.rearrange("p a b -> p (a b)"), in_=self.cmask), writes=["cm"])
        self.wload(wom.rearrange("p a b -> p (a b)"), self.w_om[l], "wom", 4096, 4 * D)
        self.wload(woh.rearrange("p a b -> p (a b)"), self.w_oh[l], "woh", 2048, 2 * D)
        self.wload(woc.rearrange("p a b -> p (a b)"), self.w_oc[l], "woc", 2048, 2 * D)
        scale = 1.0 / math.sqrt(96.0)
        tps = S // TS
        sci = 0
        xri = 0
        for b in range(NB):
            NCH = S // 512
            for c in range(NCH):
                T.dma("sp", f"kcl{c}", lambda e, b=b, c=c: e.dma_start(out=kc_[0:96, :, c * 512:(c + 1) * 512],
                                                                     in_=self.k_s[b, :, :, c * 512:(c + 1) * 512].rearrange("h p n -> p h n")),
                      reads=[("k_s", b)], writes=[("kc", c)])
                T.dma("sp", f"vcl{c}", lambda e, b=b, c=c: e.dma_start(out=vc_[:, 4 * c:4 * c + 4, :],
                                                                     in_=self.v_s[b, c * 512:(c + 1) * 512, :].rearrange("(k p) c -> p k c", p=128)),
                      reads=[("v_s", b)], writes=[("vc", c)])
            vc4 = vc_.rearrange("p k (h w) -> p k h w", w=128)
            for i in range(tps):
                s0 = i * TS
                t = b * tps + i
                c0 = b * S + s0
                q = qt[t % 2]; qk = f"qt{t % 2}"
                T.dma("sp", qk, lambda e, q=q, b=b, s0=s0: e.dma_start(out=q[0:96], in_=self.q_s[b, :, :, s0:s0 + TS].rearrange("h p n -> p h n")),
                      reads=[("q_s", b)], writes=[qk])
                T.dma("sp", "yhl", lambda e, b=b, s0=s0: e.dma_start(out=yh, in_=self.yh_s[b, :, :, s0:s0 + TS].rearrange("(j hh) p n -> (hh p) j n", hh=2)),
                      reads=[("yh_s", b)], writes=["yh"])
                T.dma("sp", "ycl", lambda e, b=b, s0=s0: e.dma_start(out=yc, in_=self.yc_s[b, :, s0:s0 + TS].rearrange("(j p) n -> p j n", p=128)),
                      reads=[("yc_s", b)], writes=["yc"])
                nkb = 4 * i + 4
                LA = 2
                ND = 0
                steps = [(h, kb) for h in range(8) for kb in range(nkb)]
                slots = {}
                pending_norm = []

                def emit_qk(idx):
                    nonlocal sci
                    h, kb = steps[idx]
                    sb_ = sci % 4; sci += 1
                    pt = PT[sb_]; ptk = f"PT{sb_}"
                    dj = kb - 4 * i
                    q0 = 128 * dj if dj > 0 else 0
                    slots[idx] = (pt, ptk, q0)
                    T.op("pe", lambda e, h=h, kb=kb, sb_=sb_, q=q, dj=dj, q0=q0: e.matmul(ps[sb_][:, q0:], lhsT=kc_[0:96, h, kb * 128:(kb + 1) * 128], rhs=q[0:96, h, q0:],
                                                                                       start=True, stop=(dj < 0)),
                         reads=[("kc", kb // 4), qk], writes=[f"ps{sb_}"])
                    if dj >= 0:
                        T.op("pe", lambda e, sb_=sb_, q0=q0: e.matmul(ps[sb_][:, q0:q0 + 128], lhsT=ident, rhs=cm[:, 0, 0:128], start=False, stop=True),
                             reads=["ident", "cm"], writes=[f"ps{sb_}"], cost=90)
                    T.op("act", lambda e, pt=pt, sb_=sb_, q0=q0: e.activation(out=pt[:, q0:], in_=ps[sb_][:, q0:], func=AF.Exp, scale=scale),
                         reads=[f"ps{sb_}"], writes=[ptk])

                def emit_pv(idx):
                    h, kb = steps[idx]
                    pt, ptk, q0 = slots.pop(idx)
                    po = ps[4 + h % 2]; pok = f"ps{4 + h % 2}"
                    T.op("pe", lambda e, h=h, kb=kb, pt=pt, po=po, nkb=nkb, vc4=vc4, q0=q0: e.matmul(po[:, q0:], lhsT=vc4[:, kb, h, :], rhs=pt[:, q0:], start=(kb == 0), stop=(kb == nkb - 1)),
                         reads=[("vc", kb // 4), ptk], writes=[pok])
                    if kb == nkb - 1:
                        return h
                    return None

                def emit_norm(h):
                    po = ps[4 + h % 2]; pok = f"ps{4 + h % 2}"
                    rch = rc2[h % 2]; rck = f"rc{h % 2}"
                    if h % 2 == 0:
                        T.op("dve", lambda e, po=po, rch=rch: e.reciprocal(out=rch[64:128], in_=po[64:128, :]), reads=[pok], writes=[rck])
                        T.op("dve", lambda e, po=po, h=h, rch=rch: e.tensor_tensor(out=ym[0:64, h // 2, :], in0=po[0:64, :], in1=rch[64:128], op=ALU.mult),
                             reads=[pok, rck], writes=[("ym", h)])
                    else:
                        T.op("dve", lambda e, po=po, rch=rch: e.reciprocal(out=rch[0:64], in_=po[0:64, :]), reads=[pok], writes=[rck])
                        T.op("dve", lambda e, po=po, h=h, rch=rch: e.tensor_tensor(out=ym[64:128, h // 2, :], in0=po[64:128, :], in1=rch[0:64], op=ALU.mult),
                             reads=[pok, rck], writes=[("ym", h)])

                nst = len(steps)
                for idx in range(nst + LA):
                    if idx < nst:
                        emit_qk(idx)
                    if idx - LA >= 0:
                        hdone = emit_pv(idx - LA)
                        if hdone is not None:
                            pending_norm.append((idx + ND, hdone))
                    while pending_norm and pending_norm[0][0] <= idx:
                        emit_norm(pending_norm.pop(0)[1])
                for _, hh_ in pending_norm:
                    emit_norm(hh_)
                for oc in range(8):
                    p = ps[6 + oc % 2]; pk = f"ps{6 + oc % 2}"
                    x_ = xr[xri % 3]; xk = f"xrA{xri % 3}"; xri += 1
                    T.dma("sp", xk, lambda e, x_=x_, oc=oc, c0=c0: e.dma_start(out=x_, in_=src[oc * 128:(oc + 1) * 128, c0:c0 + TS]),
                          reads=[("y", c0 // 512)], writes=[xk])
                    for j in range(4):
                        T.op("pe", lambda e, j=j, oc=oc, p=p: e.matmul(p[:], lhsT=wom[:, j, oc * 128:(oc + 1) * 128], rhs=ym[:, j, :], start=(j == 0), stop=False),
                             reads=["wom", ("ym", 2 * j), ("ym", 2 * j + 1)], writes=[pk])
                    for j in range(2):
                        T.op("pe", lambda e, j=j, oc=oc, p=p: e.matmul(p[:], lhsT=woh[:, j, oc * 128:(oc + 1) * 128], rhs=yh[:, j, :], start=False, stop=False),
                             reads=["woh", "yh"], writes=[pk])
                    for j in range(2):
                        T.op("pe", lambda e, j=j, oc=oc, p=p: e.matmul(p[:], lhsT=woc[:, j, oc * 128:(oc + 1) * 128], rhs=yc[:, j, :], start=False, stop=(j == 1)),
                             reads=["woc", "yc"], writes=[pk])
                    T.op("dve", lambda e, p=p, x_=x_: e.tensor_tensor(out=x_, in0=p[:], in1=x_, op=ALU.add), reads=[pk, xk], writes=[xk])
                    T.dma("sp", xk, lambda e, x_=x_, oc=oc, c0=c0: e.dma_start(out=yT[oc * 128:(oc + 1) * 128, c0:c0 + TS], in_=x_),
                          reads=[xk], writes=[("y", c0 // 512)])


def _kpn(w, p=128):
    K, N = w.shape
    return np.ascontiguousarray(w.reshape(K // p, p, N).transpose(1, 0, 2).reshape(p, -1))


def host_consts(S, positions):
    ident = np.eye(128, dtype=np.float32)
    tri = np.zeros((128, 64), np.float32)
    tri[0:64] = (np.arange(64)[:, None] <= np.arange(64)[None, :]).astype(np.float32)
    m01 = np.ones((128, 256), np.float32)
    m01[:, ::64] = 0.0
    consts = np.concatenate([ident, tri, m01], axis=1)
    cmask = np.zeros((128, 4, 512), np.float32)
    p = np.arange(128)[:, None]
    n = np.arange(512)[None, :]
    for j in range(4):
        cmask[:, j, :] = np.where(128 * j + p <= n, 0.0, NEG)
    inv_freq = (10000.0 ** (-np.arange(0, 32, 2, dtype=np.float32) / 32)).astype(np.float32)
    ang = positions.astype(np.float32)[:, None] * inv_freq[None, :]
    cos = np.cos(ang).astype(np.float32).T
    sin = np.sin(ang).astype(np.float32).T
    cosT = np.ones((96, S), np.float32)
    sinS = np.zeros((96, S), np.float32)
    cosT[64:80] = cos
    cosT[80:96] = cos
    sinS[64:80] = -sin
    sinS[80:96] = sin
    return consts, cmask.reshape(128, -1), cosT, sinS


def host_weights(inp, L):
    out = {}
    perm = np.concatenate([np.arange(64), np.arange(80, 96), np.arange(64, 80)])
    w_in = np.asarray(inp["w_in"])
    kr = w_in[:, :, 640:672]
    krsw = np.concatenate([kr[:, :, 16:32], kr[:, :, 0:16]], axis=2)
    w_in2 = np.concatenate([w_in[:, :, :672], krsw, w_in[:, :, 672:]], axis=2)
    out["w_in"] = np.stack([_kpn(w_in2[l]) for l in range(L)])
    w_uq = np.asarray(inp["w_uq"])
    out["w_uq"] = np.stack([_kpn(w_uq[l]) for l in range(L)])
    w_uqsw = w_uq.reshape(L, 384, 8, 96)[:, :, :, perm].reshape(L, 384, 768)
    out["w_uqsw"] = np.stack([_kpn(w_uqsw[l]) for l in range(L)])
    w_ukv = np.asarray(inp["w_ukv"]).reshape(L, 256, 8, 128)
    w_uk = w_ukv[:, :, :, :64].reshape(L, 256, 512)
    w_uv = w_ukv[:, :, :, 64:].reshape(L, 256, 512)
    out["w_uk"] = np.stack([_kpn(w_uk[l]) for l in range(L)])
    out["w_uv"] = np.stack([_kpn(w_uv[l]) for l in range(L)])
    w_out = np.asarray(inp["w_out"])
    out["w_om"] = np.stack([_kpn(w_out[l, 0:512]) for l in range(L)])
    out["w_oh"] = np.stack([_kpn(w_out[l, 512:768]) for l in range(L)])
    out["w_oc"] = np.stack([_kpn(w_out[l, 768:1024]) for l in range(L)])
    for nm, key in (("w_xq", "w_xq"), ("w_xkv", "w_xkv"), ("w_xo", "w_xo"), ("w_up", "w_up"), ("w_dn", "w_down")):
        w = np.asarray(inp[key])
        out[nm] = np.stack([_kpn(w[l]) for l in range(L)])
    vecs = np.zeros((L, 128, NV), np.float32)

    def col(v):
        v = np.asarray(v)
        return v.reshape(-1, 128).T
    for l in range(L):
        vecs[l, :, V_GMIX:V_GMIX + 8] = col(inp["mix_norm_g"][l])
        vecs[l, :, V_GQ1:V_GQ1 + 3] = col(inp["mla_q_norm_g"][l])
        vecs[l, :, V_GKV:V_GKV + 2] = col(inp["mla_kv_norm_g"][l])
        gq = np.asarray(inp["mla_qn_g"][l]); gk = np.asarray(inp["mla_kn_g"][l])
        vecs[l, 0:96, V_GQN] = gq
        vecs[l, 0:96, V_GQNSW] = gq[perm]
        vecs[l, 0:96, V_GKN] = gk
        vecs[l, 0:96, V_GKNSW] = gk[perm]
        vecs[l, 0:64, V_GO:V_GO + 4] = np.asarray(inp["hgrn_o_norm_g"][l]).reshape(4, 64).T
        cwl = np.asarray(inp["conv_w"][l])
        vecs[l, :, V_CW:V_CW + 6] = cwl.reshape(3, 2, 128).transpose(2, 1, 0).reshape(128, 6)
        vecs[l, :, V_GXA:V_GXA + 8] = col(inp["xattn_norm_g"][l])
        vecs[l, :, V_GXQ] = np.asarray(inp["xq_norm_g"][l])
        vecs[l, :, V_GXK] = np.asarray(inp["xk_norm_g"][l])
        vecs[l, :, V_GMLP:V_GMLP + 8] = col(inp["mlp_norm_g"][l])
        vecs[l, :, V_GMEM:V_GMEM + 8] = col(inp["mem_norm_g"][l])
    out["vecs"] = vecs
    lb = np.asarray(inp["hgrn_lb_logits"])
    lbl = np.zeros((64, 4, 4), np.float32)
    lbl[:, :, :L] = lb.reshape(L, 4, 64).transpose(2, 1, 0)
    if L < 4:
        lbl[:, :, L:] = -1e4
    out["lbl"] = lbl.reshape(64, 16)
    return out


_CACHE = {}


def run(inputs, NB, S, L, ncores, phases="A1,A2,B,C", trace=False):
    x = np.asarray(inputs["x"], dtype=np.float32)
    mem = np.asarray(inputs["mem"], dtype=np.float32)
    MEM = mem.shape[1]
    key = (NB, S, L, MEM, phases)
    if key not in _CACHE:
        bld = Builder(NB, S, L, MEM, phases)
        bld.build()
        _CACHE[key] = bld
    bld = _CACHE[key]
    hw = host_weights(inputs, L)
    consts, cmask, cosT, sinS = host_consts(S, np.asarray(inputs["positions"]))
    in_maps = []
    for c in range(ncores):
        xs = x[c * NB:(c + 1) * NB].reshape(NB * S, D)
        m = dict(hw)
        m["xT"] = np.ascontiguousarray(xs.T)
        m["memT"] = np.ascontiguousarray(mem[c * NB:(c + 1) * NB].transpose(0, 2, 1))
        m["cosT"] = cosT
        m["sinS"] = sinS
        m["cmask"] = cmask
        m["consts"] = consts
        in_maps.append(m)
    res = run_bass_kernel_spmd(bld.nc, in_maps, core_ids=list(range(ncores)), trace=trace)
    outs = [np.ascontiguousarray(r["yT"].T).reshape(NB, S, D) for r in res.results]
    return np.concatenate(outs, axis=0), res


def kernel(**inputs):
    out, _ = run(inputs, NB=2, S=4096, L=4, ncores=8)
    return out.astype(np.float32)
```

```python
import contextlib
import math
A1PARTS = 'QKHC'
QSTAGE = 9
import numpy as np
import concourse.bass as bass
import concourse.mybir as mybir
from concourse.bass_utils import run_bass_kernel_spmd

F32 = mybir.dt.float32
BF16 = mybir.dt.bfloat16
AF = mybir.ActivationFunctionType
ALU = mybir.AluOpType
EPS = 1e-6

D = 1024
DFF = 4096
NIN = 2496
O_CQ, O_CKV, O_KR, O_HQ, O_HF, O_HI, O_HG, O_CB, O_CC, O_CX = 0, 384, 640, 704, 960, 1216, 1472, 1728, 1984, 2240
NV = 56
V_GMIX, V_GQ1, V_GKV, V_GQN, V_GQNSW, V_GKN, V_GKNSW, V_GO, V_CW, V_GXA, V_GXQ, V_GXK, V_GMLP, V_GMEM = \
    0, 8, 11, 13, 14, 15, 16, 17, 21, 27, 35, 36, 37, 45
VROW = 8 * 128
NEG = -30000.0

ENG_NAMES = ("pe", "act", "dve", "pool", "sp")


class Op:
    __slots__ = ("eng", "fn", "deps", "odeps", "cost", "inc_idx", "dsem", "dval", "needed")

    def __init__(self, eng, fn):
        self.eng = eng
        self.fn = fn
        self.deps = []
        self.odeps = []
        self.cost = None
        self.inc_idx = None
        self.dsem = None
        self.dval = None
        self.needed = False


class Tracker:
    def __init__(self, nc, es):
        self.nc = nc
        self.es = es
        self.ops = []
        self.last_w = {}
        self.readers = {}
        self.dsem_cnt = {}
        self.last_dma = {}
        self.default_cost = {"pe": 150, "act": 600, "dve": 500, "pool": 1000, "sp": 100}
        self.seg_start = 0
        self.engobj = {"pe": nc.tensor, "act": nc.scalar, "dve": nc.vector, "pool": nc.gpsimd, "sp": nc.sync}

    mute = False
    do_schedule = True

    def op(self, eng, fn, reads=(), writes=(), cost=None):
        if self.mute:
            return None
        o = Op(eng, fn)
        o.cost = cost if cost is not None else self.default_cost.get(eng, 500)
        self._deps(o, reads, writes)
        self.ops.append(o)
        return o

    def dma(self, queue, dsem, fn, reads=(), writes=()):
        if self.mute:
            return None
        o = Op(queue, fn)
        o.dsem = dsem
        self.dsem_cnt[dsem] = self.dsem_cnt.get(dsem, 0) + 16
        o.dval = self.dsem_cnt[dsem]
        self._deps(o, reads, writes)
        prev = self.last_dma.get(dsem)
        if prev is not None:
            o.odeps.append(prev)
        self.last_dma[dsem] = o
        self.ops.append(o)
        return o

    def _deps(self, o, reads, writes):
        deps = []
        for k in reads:
            w = self.last_w.get(k)
            if w is not None:
                deps.append((w, False))
            if isinstance(k, str) and k[:2] == "ps" and k[2:].isdigit():
                for r in self.readers.get(k, ()):
                    if r.eng != o.eng:
                        deps.append((r, False))
        for k in writes:
            w = self.last_w.get(k)
            if w is not None:
                deps.append((w, True))
            for r in self.readers.get(k, ()):
                deps.append((r, False))
        seen = set()
        for d, waw in deps:
            if id(d) in seen or d is o:
                continue
            if d.dsem is None and o.dsem is None and d.eng == "pe" and o.eng == "pe":
                o.odeps.append(d)
                continue
            if waw and d.dsem is not None and o.dsem is not None and d.dsem == o.dsem and d.eng == o.eng:
                o.odeps.append(d)
                continue
            seen.add(id(d))
            o.deps.append(d)
            d.needed = True
        for k in writes:
            self.last_w[k] = o
            self.readers[k] = []
        for k in reads:
            lst = self.readers.setdefault(k, [])
            if o.dsem is None:
                keep = []
                for r in lst:
                    if r.dsem is None and r.eng == o.eng:
                        if r is not o:
                            o.odeps.append(r)
                    else:
                        keep.append(r)
                lst[:] = keep
            lst.append(o)


    def schedule_segment(self):
        import heapq
        seg = self.ops[self.seg_start:]
        n = len(seg)
        if n == 0:
            return
        pos = {id(o): i for i, o in enumerate(seg)}
        succ = [[] for _ in range(n)]
        indeg = [0] * n
        for i, o in enumerate(seg):
            for d in list(o.deps) + list(o.odeps):
                j = pos.get(id(d))
                if j is not None:
                    succ[j].append(i)
                    indeg[i] += 1
        ready_t = [0.0] * n
        fin = [0.0] * n
        engs = ENG_NAMES
        avail = {e: [] for e in engs}
        future = {e: [] for e in engs}
        free_t = {e: 0.0 for e in engs}
        for i, o in enumerate(seg):
            if indeg[i] == 0:
                heapq.heappush(future[o.eng], (0.0, i))
        order = []
        LAT = 150.0
        done = 0
        while done < n:
            best = None
            for e in engs:
                fu, av = future[e], avail[e]
                while fu and fu[0][0] <= free_t[e]:
                    rt, i = heapq.heappop(fu)
                    heapq.heappush(av, i)
                if av:
                    cand = (free_t[e], av[0], e, True)
                elif fu:
                    cand = (fu[0][0], fu[0][1], e, False)
                else:
                    continue
                if best is None or cand[:2] < best[:2]:
                    best = cand
            st, i, e, from_av = best
            if from_av:
                heapq.heappop(avail[e])
            else:
                heapq.heappop(future[e])
            o = seg[i]
            if o.dsem is not None:
                free_t[e] = st + 100.0
                fin[i] = st + 2500.0
            else:
                c = float(o.cost or 500)
                free_t[e] = st + c
                fin[i] = st + c
            order.append((st, i))
            done += 1
            for j in succ[i]:
                rt = fin[i] + (LAT if seg[j].eng != e or o.dsem is not None else 30.0)
                if rt > ready_t[j]:
                    ready_t[j] = rt
                indeg[j] -= 1
                if indeg[j] == 0:
                    heapq.heappush(future[seg[j].eng], (ready_t[j], j))
        order.sort()
        self.ops[self.seg_start:] = [seg[i] for _, i in order]

    def _lasts(self):
        last = {}
        for o in self.ops:
            if o.fn is None:
                continue
            last[(o.eng, o.dsem)] = o
        return list(last.values())

    def full_barrier(self):
        if self.do_schedule:
            self.schedule_segment()
        lasts = self._lasts()
        for e in ENG_NAMES:
            o = Op(e, None)
            for d in lasts:
                o.deps.append(d)
                d.needed = True
            self.ops.append(o)
        self.last_w.clear()
        self.readers.clear()
        self.seg_start = len(self.ops)

    def final_wait(self, eng="sp"):
        if self.do_schedule:
            self.schedule_segment()
        o = Op(eng, None)
        for d in self._lasts():
            o.deps.append(d)
            d.needed = True
        self.ops.append(o)

    def emit(self):
        nc = self.nc
        cnt = {e: 0 for e in ENG_NAMES}
        for o in self.ops:
            if o.dsem is None and o.needed and o.fn is not None:
                cnt[o.eng] += 1
                o.inc_idx = cnt[o.eng]
        sems = {}
        for e in ENG_NAMES:
            if cnt[e] > 0:
                sems[e] = self.es.enter_context(nc.semaphore("s_" + e))
        dsems = {}
        for name in self.dsem_cnt:
            dsems[name] = self.es.enter_context(nc.semaphore("d_" + name))
        waited = {e: {} for e in ENG_NAMES}
        n_wait = 0
        for o in self.ops:
            eng = self.engobj[o.eng]
            wt = waited[o.eng]
            need = {}
            for d in o.deps:
                if d.dsem is not None:
                    key = ("d", d.dsem)
                    val = d.dval
                else:
                    key = ("c", d.eng)
                    val = d.inc_idx
                if wt.get(key, 0) >= val:
                    continue
                if need.get(key, 0) < val:
                    need[key] = val
            for key, val in need.items():
                sem = dsems[key[1]] if key[0] == "d" else sems[key[1]]
                eng.wait_ge(sem, val)
                wt[key] = val
                n_wait += 1
            if o.fn is None:
                continue
            ins = o.fn(eng)
            if o.dsem is not None:
                ins.then_inc(dsems[o.dsem], 16)
            elif o.inc_idx is not None:
                ins.then_inc(sems[o.eng], 1)
        return dict(n_ops=len(self.ops), n_wait=n_wait, incs=cnt, n_dsems=len(dsems))


class Arena:
    def __init__(self, nc, es, nbytes):
        self.cap = nbytes
        self.t = es.enter_context(nc.sbuf_tensor("arena", [128, nbytes // 4], F32))
        self.off = 0

    def reset(self, off=0):
        self.off = off

    def alloc(self, shape, dt):
        n = 1
        for s in shape:
            n *= s
        size = n * (2 if dt == BF16 else 4)
        size = (size + 3) // 4 * 4
        off = (self.off + 63) // 64 * 64
        assert off + size <= self.cap, f"arena overflow {off + size} > {self.cap}"
        ap = self.t[:, off // 4:(off + size) // 4]
        if dt != F32:
            ap = ap.bitcast(dt)
            if ap.shape[-1] != n:
                ap = ap[:, 0:n]
        if len(shape) == 2:
            ap = ap.rearrange("p (a b) -> p a b", b=shape[1])
        elif len(shape) == 3:
            ap = ap.rearrange("p (a b c) -> p a b c", b=shape[1], c=shape[2])
        self.off = off + size
        return ap


class Builder:
    def __init__(self, NB, S, L, MEM=256, phases="A1,A2,B,C"):
        self.NB, self.S, self.L, self.MEM = NB, S, L, MEM
        self.NT = NB * S
        self.phases = phases.split(",")
        nc = bass.Bass("TRN2", target_bir_lowering=False)
        self.nc = nc
        NT = self.NT
        dt = nc.dram_tensor
        self.xT = dt("xT", [D, NT], F32, kind="ExternalInput").ap()
        self.yT = dt("yT", [D, NT], F32, kind="ExternalOutput").ap()
        self.memT = dt("memT", [NB, D, MEM], F32, kind="ExternalInput").ap()
        self.w_in = dt("w_in", [L, 128, 8 * NIN], F32, kind="ExternalInput").ap()
        self.w_uq = dt("w_uq", [L, 128, 3 * 768], F32, kind="ExternalInput").ap()
        self.w_uqsw = dt("w_uqsw", [L, 128, 3 * 768], F32, kind="ExternalInput").ap()
        self.w_uk = dt("w_uk", [L, 128, 2 * 512], F32, kind="ExternalInput").ap()
        self.w_uv = dt("w_uv", [L, 128, 2 * 512], F32, kind="ExternalInput").ap()
        self.w_om = dt("w_om", [L, 128, 4 * D], F32, kind="ExternalInput").ap()
        self.w_oh = dt("w_oh", [L, 128, 2 * D], F32, kind="ExternalInput").ap()
        self.w_oc = dt("w_oc", [L, 128, 2 * D], F32, kind="ExternalInput").ap()
        self.w_xq = dt("w_xq", [L, 128, 8 * 512], F32, kind="ExternalInput").ap()
        self.w_xkv = dt("w_xkv", [L, 128, 8 * 1024], F32, kind="ExternalInput").ap()
        self.w_xo = dt("w_xo", [L, 128, 4 * D], F32, kind="ExternalInput").ap()
        self.w_up = dt("w_up", [L, 128, 8 * DFF], F32, kind="ExternalInput").ap()
        self.w_dn = dt("w_dn", [L, 128, 32 * D], F32, kind="ExternalInput").ap()
        self.vecs = dt("vecs", [L, 128, NV], F32, kind="ExternalInput").ap()
        self.lbl = dt("lbl", [64, 4 * 4], F32, kind="ExternalInput").ap()
        self.posr = dt("posr", [96, S], mybir.dt.int32, kind="ExternalInput").ap()
        self.invf = dt("invf", [96, 2], F32, kind="ExternalInput").ap()
        self.cosT = dt("cos_d", [96, S], F32, kind="Internal").ap()
        self.sinS = dt("sin_d", [96, S], F32, kind="Internal").ap()
        self.cmask = dt("cmask", [128, 4 * 512], F32, kind="ExternalInput").ap()
        self.consts = dt("consts", [128, 128 + 64 + 256], F32, kind="ExternalInput").ap()
        self.q_s = dt("q_s", [NB, 8, 96, S], BF16, kind="Internal").ap()
        self.k_s = dt("k_s", [NB, 8, 96, S], BF16, kind="Internal").ap()
        self.v_s = dt("v_s", [NB, S, VROW], BF16, kind="Internal").ap()
        self.yh_s = dt("yh_s", [NB, 4, 64, S], BF16, kind="Internal").ap()
        self.yc_s = dt("yc_s", [NB, 256, S], BF16, kind="Internal").ap()

    def build(self):
        nc = self.nc
        with contextlib.ExitStack() as es:
            self.es = es
            self.T = Tracker(nc, es)
            self.A = Arena(nc, es, 212480)
            self.ps = [es.enter_context(nc.psum_tensor(f"psb{i}", [128, 512], F32)) for i in range(8)]
            T = self.T
            if "A1" in self.phases:
                self.phase_R()
                T.full_barrier()
            for l in range(self.L):
                src = self.xT if l == 0 else self.yT
                if "A1" in self.phases:
                    self.phase_A1(l, src)
                    T.full_barrier()
                if "A2" in self.phases:
                    self.phase_A2(l, src)
                    T.full_barrier()
                elif l == 0:
                    self.copy_x()
                    T.full_barrier()
                if "B" in self.phases:
                    self.phase_B(l)
                    T.full_barrier()
                if "C" in self.phases:
                    self.phase_C(l)
                    T.full_barrier()
            T.final_wait("sp")
            self.stats = T.emit()
        return nc

    def copy_x(self):
        T = self.T
        for t in range(self.NT // 512):
            T.dma("sp", "cp", lambda e, t=t: e.dma_start(out=self.yT[:, t * 512:(t + 1) * 512], in_=self.xT[:, t * 512:(t + 1) * 512]),
                  writes=[("y", t)])

    def wload(self, dst, src, key, piece_cols, ncols):
        T = self.T
        for c0 in range(0, ncols, piece_cols):
            c1 = min(ncols, c0 + piece_cols)
            T.dma("pool", key, lambda e, c0=c0, c1=c1: e.dma_start(out=dst[:, c0:c1], in_=src[:, c0:c1]), writes=[key])

    def rstd_from(self, out_ap, in_ap, tmp_ap, scale, bias_ap_or_f, keys_r, key_tmp, key_out):
        T = self.T
        T.op("act", lambda e: e.activation(out=tmp_ap, in_=in_ap, func=AF.Ln, scale=scale, bias=bias_ap_or_f),
             reads=keys_r, writes=[key_tmp])
        T.op("act", lambda e: e.activation(out=out_ap, in_=tmp_ap, func=AF.Exp, scale=-0.5), reads=[key_tmp], writes=[key_out])


    def phase_R(self):
        T, A, nc = self.T, self.A, self.nc
        S = self.S
        CH = min(1024, S)
        PI = math.pi
        A.reset()
        al = A.alloc
        I32 = mybir.dt.int32
        ivf = al([2], F32)
        pint = al([CH], I32)
        pf = al([CH], F32)
        ang = al([CH], F32)
        ki = al([CH], I32)
        kf = al([CH], F32)
        r = al([CH], F32)
        st = al([CH], F32)
        res = [al([CH], F32) for _ in range(2)]
        P = slice(0, 96)
        T.dma("sp", "ivf", lambda e: e.dma_start(out=ivf[P], in_=self.invf), writes=["ivf"])
        for ci, c0 in enumerate(range(0, S, CH)):
            T.dma("sp", "pint", lambda e, c0=c0: e.dma_start(out=pint[P], in_=self.posr[:, c0:c0 + CH]), writes=["pint"])
            T.op("dve", lambda e: e.tensor_copy(out=pf[P], in_=pint[P]), reads=["pint"], writes=["pf"])
            for which in range(2):
                if which == 0:
                    T.op("dve", lambda e: e.tensor_scalar(out=ang[P], in0=pf[P], scalar1=ivf[P, 0:1], scalar2=None, op0=ALU.mult),
                         reads=["pf", "ivf"], writes=["ang"])
                else:
                    T.op("dve", lambda e: e.tensor_scalar(out=ang[P], in0=pf[P], scalar1=ivf[P, 0:1], scalar2=PI / 2, op0=ALU.mult, op1=ALU.add),
                         reads=["pf", "ivf"], writes=["ang"])
                T.op("dve", lambda e: e.tensor_scalar(out=ki[P], in0=ang[P], scalar1=1.0 / (2 * PI), scalar2=None, op0=ALU.mult),
                     reads=["ang"], writes=["ki"])
                T.op("dve", lambda e: e.tensor_copy(out=kf[P], in_=ki[P]), reads=["ki"], writes=["kf"])
                T.op("dve", lambda e: e.scalar_tensor_tensor(out=r[P], in0=kf[P], scalar=-2 * PI, in1=ang[P], op0=ALU.mult, op1=ALU.add),
                     reads=["kf", "ang"], writes=["r"])
                T.op("dve", lambda e: e.tensor_scalar(out=st[P], in0=r[P], scalar1=-PI, scalar2=1e30, op0=ALU.add, op1=ALU.mult), reads=["r"], writes=["st"])
                T.op("dve", lambda e: e.tensor_scalar(out=st[P], in0=st[P], scalar1=0.0, scalar2=1.0, op0=ALU.max, op1=ALU.min), reads=["st"], writes=["st"])
                T.op("dve", lambda e: e.scalar_tensor_tensor(out=r[P], in0=st[P], scalar=-2 * PI, in1=r[P], op0=ALU.mult, op1=ALU.add),
                     reads=["st", "r"], writes=["r"])
                T.op("dve", lambda e: e.tensor_scalar(out=st[P], in0=r[P], scalar1=PI, scalar2=-1e30, op0=ALU.add, op1=ALU.mult), reads=["r"], writes=["st"])
                T.op("dve", lambda e: e.tensor_scalar(out=st[P], in0=st[P], scalar1=0.0, scalar2=1.0, op0=ALU.max, op1=ALU.min), reads=["st"], writes=["st"])
                T.op("dve", lambda e: e.scalar_tensor_tensor(out=r[P], in0=st[P], scalar=2 * PI, in1=r[P], op0=ALU.mult, op1=ALU.add),
                     reads=["st", "r"], writes=["r"])
                T.op("dve", lambda e: e.tensor_scalar(out=r[P], in0=r[P], scalar1=PI, scalar2=-PI, op0=ALU.min, op1=ALU.max), reads=["r"], writes=["r"])
                rs = res[which]; rk = f"res{which}"
                T.op("act", lambda e, rs=rs: e.activation(out=rs[P], in_=r[P], func=AF.Sin), reads=["r"], writes=[rk])
                dst = self.sinS if which == 0 else self.cosT
                dk = "sin_d" if which == 0 else "cos_d"
                T.dma("sp", rk, lambda e, rs=rs, dst=dst, c0=c0: e.dma_start(out=dst[:, c0:c0 + CH], in_=rs[P]), reads=[rk], writes=[dk])

    def phase_C(self, l):
        T, A, nc, ps = self.T, self.A, self.nc, self.ps
        T.default_cost = {"pe": 215, "act": 520, "dve": 560, "pool": 1000, "sp": 100}
        NT = self.NT
        TS = 512
        KC, FC = 8, 32
        A.reset()
        wup = A.alloc([KC, DFF], BF16)
        wdn = A.alloc([FC, D], BF16)
        vec = A.alloc([NV], F32)
        ones = A.alloc([128], BF16)
        xin = A.alloc([KC, TS], F32)
        xb = A.alloc([KC, TS], BF16)
        sq = A.alloc([KC, TS], BF16)
        a = A.alloc([FC, TS], BF16)
        lnt = A.alloc([TS], F32)
        rstd = A.alloc([TS], F32)
        t1 = [A.alloc([TS], F32) for _ in range(2)]
        xr = [A.alloc([TS], F32) for _ in range(3)]
        yT = self.yT
        T.op("pool", lambda e: e.memset(ones, 1.0), writes=["ones"])
        T.dma("sp", "vec", lambda e: e.dma_start(out=vec, in_=self.vecs[l]), writes=["vec"])
        wup_d = self.w_up[l].rearrange("p (k n) -> p k n", n=DFF)
        for j in range(8):
            T.dma("pool", f"wup{j}", lambda e, j=j: e.dma_start(out=wup[:, :, j * 512:(j + 1) * 512], in_=wup_d[:, :, j * 512:(j + 1) * 512]),
                  writes=[("wup", j)])
        wdn_d = self.w_dn[l].rearrange("p (k n) -> p k n", n=D)
        for j in range(8):
            T.dma("pool", f"wdn{j}", lambda e, j=j: e.dma_start(out=wdn[:, 4 * j:4 * j + 4, :], in_=wdn_d[:, 4 * j:4 * j + 4, :]),
                  writes=[("wdn", j)])
        g = vec[:, V_GMLP:V_GMLP + 8]
        yT_t = yT.rearrange("(kc p) n -> p kc n", p=128)
        ntile = NT // TS
        T.dma("sp", "xin", lambda e: e.dma_start(out=xin, in_=yT_t[:, :, 0:TS]), reads=[("y", 0)], writes=["xin"])
        psi = 0
        xri = 0
        for t in range(ntile):
            c0 = t * TS
            T.op("dve", lambda e: e.tensor_tensor(out=xb, in0=xin, in1=g.unsqueeze(2).to_broadcast([128, KC, TS]), op=ALU.mult),
                 reads=["xin", "vec"], writes=["xb"])
            T.op("act", lambda e: e.activation(out=sq, in_=xin, func=AF.Square), reads=["xin"], writes=["sq"])
            if t + 1 < ntile:
                T.dma("sp", "xin", lambda e, c1=c0 + TS: e.dma_start(out=xin, in_=yT_t[:, :, c1:c1 + TS]),
                      reads=[("y", t + 1)], writes=["xin"])
            pss = ps[7]
            for kc in range(KC):
                T.op("pe", lambda e, kc=kc: e.matmul(pss[:], lhsT=ones, rhs=sq[:, kc, :], start=(kc == 0), stop=(kc == KC - 1)),
                     reads=["ones", "sq"], writes=["ps7"])
            self.rstd_from(rstd, pss[:], lnt, 1.0 / D, EPS, ["ps7"], "lnt", "rstd")
            for oc in range(FC):
                b_ = psi % 4; p = ps[b_]; pk = f"ps{b_}"; tt = t1[psi % 2]; tk = f"t1_{psi % 2}"; psi += 1
                for kc in range(KC):
                    T.op("pe", lambda e, kc=kc, oc=oc, p=p: e.matmul(p[:], lhsT=wup[:, kc, oc * 128:(oc + 1) * 128], rhs=xb[:, kc, :],
                                                                  start=(kc == 0), stop=(kc == KC - 1)),
                         reads=[("wup", oc // 4), "xb"], writes=[pk])
                T.op("dve", lambda e, p=p, tt=tt: e.scalar_tensor_tensor(out=tt, in0=p[:], scalar=0.0, in1=rstd, op0=ALU.max, op1=ALU.mult),
                     reads=[pk, "rstd"], writes=[tk])
                T.op("act", lambda e, tt=tt, oc=oc: e.activation(out=a[:, oc, :], in_=tt, func=AF.Square), reads=[tk], writes=[("a", oc)])
            for oc in range(8):
                b_ = psi % 4; p = ps[b_]; pk = f"ps{b_}"; psi += 1
                x_ = xr[xri % 3]; xk = f"xr_{xri % 3}"; xri += 1
                T.dma("sp", xk, lambda e, x_=x_, oc=oc, c0=c0: e.dma_start(out=x_, in_=yT[oc * 128:(oc + 1) * 128, c0:c0 + TS]),
                      reads=[("y", t)], writes=[xk])
                for kc in range(FC):
                    T.op("pe", lambda e, kc=kc, oc=oc, p=p: e.matmul(p[:], lhsT=wdn[:, kc, oc * 128:(oc + 1) * 128], rhs=a[:, kc, :],
                                                                  start=(kc == 0), stop=(kc == FC - 1)),
                         reads=[("wdn", kc // 4), ("a", kc)], writes=[pk])
                T.op("dve", lambda e, p=p, x_=x_: e.tensor_tensor(out=x_, in0=p[:], in1=x_, op=ALU.add), reads=[pk, xk], writes=[xk])
                T.dma("sp", xk, lambda e, x_=x_, oc=oc, c0=c0: e.dma_start(out=yT[oc * 128:(oc + 1) * 128, c0:c0 + TS], in_=x_),
                      reads=[xk], writes=[("y", t)])

    def phase_B(self, l):
        T, A, nc, ps = self.T, self.A, self.nc, self.ps
        NB, S, MEM = self.NB, self.S, self.MEM
        T.default_cost = {"pe": 260, "act": 550, "dve": 600, "pool": 1000, "sp": 100}
        TS = 512
        KC = 8
        MB = MEM // 128
        A.reset()
        wq = A.alloc([KC, 512], BF16)
        wkv = A.alloc([KC, 1024], BF16)
        wo = A.alloc([4, D], BF16)
        vec = A.alloc([NV], F32)
        ones = A.alloc([128], BF16)
        kT = A.alloc([NB, 4, MEM], BF16)
        vv = A.alloc([NB, MB, 512], BF16)
        memf = A.alloc([KC, MEM], F32)
        memb = A.alloc([KC, MEM], BF16)
        msq = A.alloc([KC, MEM], BF16)
        rsm = A.alloc([MEM], F32)
        tmpm = A.alloc([MEM], F32)
        kraw = A.alloc([MEM], F32)
        ksq = A.alloc([MEM], BF16)
        rk = A.alloc([MEM], F32)
        rtok = A.alloc([2 * MB], F32)
        ttok = A.alloc([2 * MB], F32)
        xin = [A.alloc([KC, TS], F32) for _ in range(2)]
        xb = A.alloc([KC, TS], BF16)
        sq = A.alloc([KC, TS], BF16)
        epsx = A.alloc([TS], F32)
        psq4 = A.alloc([4, TS], BF16)
        tq4 = A.alloc([4, TS], F32)
        lq4 = A.alloc([4, TS], F32)
        rq4 = A.alloc([4, TS], F32)
        qT = A.alloc([4, TS], BF16)
        PT = [A.alloc([TS], BF16) for _ in range(8)]
        rec2 = [A.alloc([TS], F32) for _ in range(2)]
        ob = A.alloc([4, TS], BF16)
        xo = [A.alloc([TS], F32) for _ in range(3)]
        yT = self.yT
        T.op("pool", lambda e: e.memset(ones, 1.0), writes=["ones"])
        T.dma("sp", "vec", lambda e: e.dma_start(out=vec, in_=self.vecs[l]), writes=["vec"])
        self.wload(wkv.rearrange("p a b -> p (a b)"), self.w_xkv[l], "wkv", 4096, KC * 1024)
        self.wload(wq.rearrange("p a b -> p (a b)"), self.w_xq[l], "wq", 4096, KC * 512)
        self.wload(wo.rearrange("p a b -> p (a b)"), self.w_xo[l], "wo", 4096, 4 * D)
        gmem = vec[:, V_GMEM:V_GMEM + 8]
        gxa = vec[:, V_GXA:V_GXA + 8]
        gxq = vec[:, V_GXQ:V_GXQ + 1]
        gxk = vec[:, V_GXK:V_GXK + 1]
        for b in range(NB):
            T.dma("sp", "memf", lambda e, b=b: e.dma_start(out=memf, in_=self.memT[b].rearrange("(kc p) m -> p kc m", p=128)),
                  writes=["memf"])
            T.op("dve", lambda e: e.tensor_tensor(out=memb, in0=memf, in1=gmem.unsqueeze(2).to_broadcast([128, KC, MEM]), op=ALU.mult),
                 reads=["memf", "vec"], writes=["memb"])
            T.op("act", lambda e: e.activation(out=msq, in_=memf, func=AF.Square), reads=["memf"], writes=["msq"])
            for kc in range(KC):
                T.op("pe", lambda e, kc=kc: e.matmul(ps[7][:, 0:MEM], lhsT=ones, rhs=msq[:, kc, :], start=(kc == 0), stop=(kc == KC - 1)),
                     reads=["ones", "msq"], writes=["ps7"])
            self.rstd_from(rsm, ps[7][:, 0:MEM], tmpm, 1.0 / D, EPS, ["ps7"], "tmpm", "rsm")
            for mb in range(MB):
                for kc in range(KC):
                    T.op("pe", lambda e, kc=kc, mb=mb: e.matmul(ps[6][:, 2 * mb:2 * mb + 2], lhsT=msq[:, kc, mb * 128:(mb + 1) * 128], rhs=ones[:, 0:2],
                                                               start=(kc == 0), stop=(kc == KC - 1)),
                         reads=["ones", "msq"], writes=["ps6"])
            self.rstd_from(rtok, ps[6][:, 0:2 * MB], ttok, 1.0 / D, EPS, ["ps6"], "ttok", "rtok")
            for h in range(4):
                p = ps[h % 2]; pk = f"ps{h % 2}"
                for kc in range(KC):
                    T.op("pe", lambda e, kc=kc, h=h, p=p: e.matmul(p[:, 0:MEM], lhsT=wkv[:, kc, h * 128:(h + 1) * 128], rhs=memb[:, kc, :],
                                                                start=(kc == 0), stop=(kc == KC - 1)),
                         reads=["wkv", "memb"], writes=[pk])
                T.op("dve", lambda e, p=p: e.tensor_tensor(out=kraw, in0=p[:, 0:MEM], in1=rsm, op=ALU.mult), reads=[pk, "rsm"], writes=["kraw"])
                T.op("act", lambda e: e.activation(out=ksq, in_=kraw, func=AF.Square), reads=["kraw"], writes=["ksq"])
                T.op("pe", lambda e: e.matmul(ps[5][:, 0:MEM], lhsT=ones, rhs=ksq, start=True, stop=True), reads=["ones", "ksq"], writes=["ps5"])
                self.rstd_from(rk, ps[5][:, 0:MEM], tmpm, 1.0 / 128, EPS, ["ps5"], "tmpm", "rk")
                T.op("dve", lambda e, b=b, h=h: e.scalar_tensor_tensor(out=kT[:, b, h, :], in0=kraw, scalar=gxk, in1=rk, op0=ALU.mult, op1=ALU.mult),
                     reads=["kraw", "rk", "vec"], writes=["kT"])
            for mb in range(MB):
                p = ps[2 + mb % 2]; pk = f"ps{2 + mb % 2}"
                for kc in range(KC):
                    T.op("pe", lambda e, kc=kc, mb=mb, p=p: e.matmul(p[:], lhsT=memb[:, kc, mb * 128:(mb + 1) * 128], rhs=wkv[:, kc, 512:1024],
                                                                  start=(kc == 0), stop=(kc == KC - 1)),
                         reads=["wkv", "memb"], writes=[pk])
                T.op("dve", lambda e, p=p, b=b, mb=mb: e.tensor_scalar(out=vv[:, b, mb, :], in0=p[:], scalar1=rtok[:, 2 * mb:2 * mb + 1], scalar2=None, op0=ALU.mult),
                     reads=[pk, "rtok"], writes=["vv"])
        yT_t = yT.rearrange("(kc p) n -> p kc n", p=128)
        ntile = self.NT // TS
        T.dma("sp", "xinB0", lambda e: e.dma_start(out=xin[0], in_=yT_t[:, :, 0:TS]), reads=[("y", 0)], writes=["xinB0"])
        sc_i = 0
        xoi = 0
        for t in range(ntile):
            c0 = t * TS
            b = c0 // S
            xi = xin[t % 2]; xik = f"xinB{t % 2}"
            if t + 1 < ntile:
                T.dma("sp", f"xinB{(t + 1) % 2}", lambda e, c1=c0 + TS, x2=xin[(t + 1) % 2]: e.dma_start(out=x2, in_=yT_t[:, :, c1:c1 + TS]),
                      reads=[("y", t + 1)], writes=[f"xinB{(t + 1) % 2}"])
            T.op("dve", lambda e, xi=xi: e.tensor_tensor(out=xb, in0=xi, in1=gxa.unsqueeze(2).to_broadcast([128, KC, TS]), op=ALU.mult),
                 reads=[xik, "vec"], writes=["xb"])
            T.op("act", lambda e, xi=xi: e.activation(out=sq, in_=xi, func=AF.Square), reads=[xik], writes=["sq"])
            for kc in range(KC):
                T.op("pe", lambda e, kc=kc: e.matmul(ps[7][:], lhsT=ones, rhs=sq[:, kc, :], start=(kc == 0), stop=(kc == KC - 1)),
                     reads=["ones", "sq"], writes=["ps7"])
            T.op("dve", lambda e: e.tensor_scalar(out=epsx, in0=ps[7][:], scalar1=EPS / D, scalar2=EPS * EPS, op0=ALU.mult, op1=ALU.add),
                 reads=["ps7"], writes=["epsx"])
            for h in range(4):
                p = ps[h]; pk = f"ps{h}"
                for kc in range(KC):
                    T.op("pe", lambda e, kc=kc, h=h, p=p: e.matmul(p[:], lhsT=wq[:, kc, h * 128:(h + 1) * 128], rhs=xb[:, kc, :],
                                                                start=(kc == 0), stop=(kc == KC - 1)),
                         reads=["wq", "xb"], writes=[pk])
                T.op("act", lambda e, p=p, h=h: e.activation(out=psq4[:, h, :], in_=p[:], func=AF.Square), reads=[pk], writes=[("psq", h)])
            for h in range(4):
                T.op("pe", lambda e, h=h: e.matmul(ps[4 + h][:], lhsT=ones, rhs=psq4[:, h, :], start=True, stop=True),
                     reads=["ones", ("psq", h)], writes=[f"ps{4 + h}"])
            for h in range(4):
                T.op("dve", lambda e, h=h: e.scalar_tensor_tensor(out=tq4[:, h, :], in0=ps[4 + h][:], scalar=1.0 / 128, in1=epsx, op0=ALU.mult, op1=ALU.add),
                     reads=[f"ps{4 + h}", "epsx"], writes=[("tq", h)])
            T.op("act", lambda e: e.activation(out=lq4, in_=tq4, func=AF.Ln), reads=[("tq", h) for h in range(4)], writes=["lq4"])
            T.op("act", lambda e: e.activation(out=rq4, in_=lq4, func=AF.Exp, scale=-0.5), reads=["lq4"], writes=["rq4"])
            for h in range(4):
                T.op("dve", lambda e, h=h: e.scalar_tensor_tensor(out=qT[:, h, :], in0=ps[h][:], scalar=gxq, in1=rq4[:, h, :], op0=ALU.mult, op1=ALU.mult),
                     reads=[f"ps{h}", "rq4", "vec"], writes=[("qT", h)])
            for h in range(4):
                for mb in range(MB):
                    sb_ = (h * MB + mb) % 4
                    pt = PT[(h * MB + mb) % 8]; ptk = f"PT{(h * MB + mb) % 8}"
                    T.op("pe", lambda e, h=h, mb=mb, sb_=sb_, b=b: e.matmul(ps[sb_][:], lhsT=kT[:, b, h, mb * 128:(mb + 1) * 128], rhs=qT[:, h, :],
                                                                         start=True, stop=True),
                         reads=["kT", ("qT", h)], writes=[f"ps{sb_}"])
                    T.op("act", lambda e, pt=pt, sb_=sb_: e.activation(out=pt, in_=ps[sb_][:], func=AF.Exp, scale=1.0 / math.sqrt(128.0)),
                         reads=[f"ps{sb_}"], writes=[ptk])
            for h in range(4):
                po = ps[4 + h % 2]; pok = f"ps{4 + h % 2}"
                pz = ps[6 + h % 2]; pzk = f"ps{6 + h % 2}"
                for mb in range(MB):
                    pt = PT[(h * MB + mb) % 8]; ptk = f"PT{(h * MB + mb) % 8}"
                    T.op("pe", lambda e, h=h, mb=mb, pt=pt, b=b, po=po: e.matmul(po[:], lhsT=vv[:, b, mb, h * 128:(h + 1) * 128], rhs=pt,
                                                                              start=(mb == 0), stop=(mb == MB - 1)),
                         reads=["vv", ptk], writes=[pok])
                for mb in range(MB):
                    pt = PT[(h * MB + mb) % 8]; ptk = f"PT{(h * MB + mb) % 8}"
                    T.op("pe", lambda e, mb=mb, pt=pt, pz=pz: e.matmul(pz[:], lhsT=ones, rhs=pt, start=(mb == 0), stop=(mb == MB - 1)),
                         reads=["ones", ptk], writes=[pzk])
                rc_ = rec2[h % 2]; rck = f"rec{h % 2}"
                T.op("act", lambda e, pz=pz, rc_=rc_: e.activation(out=rc_, in_=pz[:], func=AF.Ln), reads=[pzk], writes=[rck])
                T.op("act", lambda e, rc_=rc_: e.activation(out=rc_, in_=rc_, func=AF.Exp, scale=-1.0), reads=[rck], writes=[rck])
                T.op("dve", lambda e, h=h, po=po, rc_=rc_: e.tensor_tensor(out=ob[:, h, :], in0=po[:], in1=rc_, op=ALU.mult), reads=[pok, rck], writes=[("ob", h)])
            for oc in range(8):
                p = ps[oc % 2]; pk = f"ps{oc % 2}"
                x_ = xo[xoi % 3]; xk = f"xoB{xoi % 3}"; xoi += 1
                for h in range(4):
                    T.op("pe", lambda e, h=h, oc=oc, p=p: e.matmul(p[:], lhsT=wo[:, h, oc * 128:(oc + 1) * 128], rhs=ob[:, h, :],
                                                                start=(h == 0), stop=(h == 3)),
                         reads=["wo", ("ob", h)], writes=[pk])
                T.op("dve", lambda e, p=p, x_=x_, xi=xi, oc=oc: e.tensor_tensor(out=x_, in0=p[:], in1=xi[:, oc, :], op=ALU.add),
                     reads=[pk, xik], writes=[xk])
                T.dma("sp", xk, lambda e, x_=x_, oc=oc, c0=c0: e.dma_start(out=yT[oc * 128:(oc + 1) * 128, c0:c0 + TS], in_=x_),
                      reads=[xk], writes=[("y", t)])

    def phase_A1(self, l, src):
        T, A, nc, ps = self.T, self.A, self.nc, self.ps
        NB, S = self.NB, self.S
        T.default_cost = {"pe": 150, "act": 600, "dve": 500, "pool": 1000, "sp": 100}
        TS = 256
        KC = 8
        A.reset()
        al = A.alloc
        win = al([KC, NIN], BF16)
        wuq = al([3, 768], BF16)
        wuqs = al([3, 768], BF16)
        wuk = al([2, 512], BF16)
        wuv = al([2, 512], BF16)
        vec = al([NV], F32)
        ones = al([128], BF16)
        cst = al([128 + 64 + 256], F32)
        ident = al([128], BF16)
        tri = al([4, 4, 64], F32)
        m01 = al([4, TS], F32)
        lbt = al([4, 4], F32)
        lbe = al([4, 4], F32)
        lbs = al([4], F32)
        lbv = al([4], F32)
        xin = [al([KC, TS], F32) for _ in range(2)]
        xb = al([KC, TS], BF16)
        sq = al([KC, TS], BF16)
        rs0 = al([TS], F32)
        lt0 = al([TS], F32)
        rtok = al([8], F32)
        ttok = al([8], F32)
        cosb = [al([TS], F32) for _ in range(2)]
        sinb = [al([TS], F32) for _ in range(2)]
        cq = al([3, TS], F32)
        cqb = al([3, TS], BF16)
        cqs = al([3, TS], BF16)
        epsq = al([TS], F32)
        ckv = al([2, TS], F32)
        ckvs = al([2, TS], BF16)
        rskv = al([TS], F32)
        ckvb = al([2, TS], BF16)
        krt = al([TS], F32)
        kcat = al([8, TS], F32)
        ksw = al([TS], F32)
        kbt = al([TS], F32)
        ksq = al([8, TS], BF16)
        gen = [al([2, TS], F32) for _ in range(6)]
        qTo = al([8, TS], BF16)
        kTo = al([8, TS], BF16)
        vto = al([2, 8, 128], BF16)
        hq = al([4, TS], F32)
        hf = al([4, TS], F32)
        hg = al([4, TS], F32)
        hgen = [al([4, TS], F32) for _ in range(5)]
        qtl = al([4, TS], BF16)
        ktl = al([4, TS], BF16)
        ktok = al([4, 4, 64], BF16)
        vtok = al([4, 256], BF16)
        attb = al([4, 4, 64], BF16)
        sc8 = [al([4, 4], F32) for _ in range(6)]
        Sst = al([4, 64], F32)
        Ssc = al([4, 4, 64], BF16)
        tmpU = al([4, 64], F32)
        osq = al([4, TS], BF16)
        yho = al([4, TS], BF16)
        cbt = al([2, TS], F32)
        cct = al([2, TS], F32)
        cxt = al([2, TS], F32)
        ubuf = al([2, TS + 2], F32)
        yacc = al([2, TS], F32)
        yco = al([2, TS], BF16)

        T.op("pool", lambda e: e.memset(ones, 1.0), writes=["ones"])
        T.dma("sp", "vec", lambda e: e.dma_start(out=vec, in_=self.vecs[l]), writes=["vec"])
        T.dma("sp", "cst", lambda e: e.dma_start(out=cst, in_=self.consts), writes=["cst"])
        T.dma("sp", "lbt", lambda e: e.dma_start(out=lbt[0:64].rearrange("p a b -> p (a b)"), in_=self.lbl), writes=["lbt"])
        T.op("dve", lambda e: e.tensor_copy(out=ident, in_=cst[:, 0:128]), reads=["cst"], writes=["ident"])
        T.op("dve", lambda e: e.tensor_copy(out=tri[0:64], in_=cst[0:64, 128:192].unsqueeze(1).unsqueeze(1).to_broadcast([64, 4, 4, 64])),
             reads=["cst"], writes=["tri"])
        T.op("dve", lambda e: e.tensor_copy(out=m01[0:64], in_=cst[0:64, 192:448].unsqueeze(1).to_broadcast([64, 4, TS])),
             reads=["cst"], writes=["m01"])
        T.op("pool", lambda e: e.memset(ksw, 0.0), writes=["ksw"])
        T.op("pool", lambda e: e.memset(vto, 1.0), writes=["vto"])
        T.op("act", lambda e: e.activation(out=lbe[0:64], in_=lbt[0:64], func=AF.Exp), reads=["lbt"], writes=["lbe"])
        T.op("dve", lambda e: e.tensor_reduce(out=lbs[0:64], in_=lbe[0:64], axis=mybir.AxisListType.X, op=ALU.add), reads=["lbe"], writes=["lbs"])
        T.op("dve", lambda e: e.reciprocal(out=lbs[0:64], in_=lbs[0:64]), reads=["lbs"], writes=["lbs"])
        if l == 0:
            T.op("pool", lambda e: e.memset(lbv, 0.0), writes=["lbv"])
        else:
            T.op("dve", lambda e: e.tensor_reduce(out=lbv[0:64], in_=lbe[0:64, :, 1:l + 1], axis=mybir.AxisListType.X, op=ALU.add),
                 reads=["lbe"], writes=["lbv"])
            T.op("dve", lambda e: e.tensor_tensor(out=lbv[0:64], in0=lbv[0:64], in1=lbs[0:64], op=ALU.mult), reads=["lbv", "lbs"], writes=["lbv"])
        self.wload(win.rearrange("p a b -> p (a b)"), self.w_in[l], "win", NIN, KC * NIN)
        self.wload(wuq.rearrange("p a b -> p (a b)"), self.w_uq[l], "wuq", 2304, 2304)
        self.wload(wuqs.rearrange("p a b -> p (a b)"), self.w_uqsw[l], "wuqs", 2304, 2304)
        self.wload(wuk.rearrange("p a b -> p (a b)"), self.w_uk[l], "wuk", 1024, 1024)
        self.wload(wuv.rearrange("p a b -> p (a b)"), self.w_uv[l], "wuv", 1024, 1024)

        gmix = vec[:, V_GMIX:V_GMIX + 8]
        gq1 = vec[:, V_GQ1:V_GQ1 + 3]
        gkv = vec[:, V_GKV:V_GKV + 2]
        gqn = vec[0:96, V_GQN:V_GQN + 1]
        gqns = vec[0:96, V_GQNSW:V_GQNSW + 1]
        gkn = vec[0:96, V_GKN:V_GKN + 1]
        gkns = vec[0:96, V_GKNSW:V_GKNSW + 1]
        go = vec[0:64, V_GO:V_GO + 4]
        cw = vec[:, V_CW:V_CW + 6]
        src_t = src.rearrange("(kc p) n -> p kc n", p=128)
        ntile = self.NT // TS
        tps = S // TS
        T.dma("sp", "xinA0", lambda e: e.dma_start(out=xin[0], in_=src_t[:, :, 0:TS]), reads=[("y", 0)], writes=["xinA0"])
        pcnt = [0]

        def pbank():
            b_ = pcnt[0] % 4
            pcnt[0] += 1
            return ps[b_], f"ps{b_}"

        for t in range(ntile):
            c0 = t * TS
            b = c0 // S
            s0 = c0 - b * S
            xi = xin[t % 2]; xik = f"xinA{t % 2}"
            cs = cosb[t % 2]; csk = f"cos{t % 2}"; sn = sinb[t % 2]; snk = f"sin{t % 2}"
            T.dma("sp", csk, lambda e, cs=cs, s0=s0: e.dma_start(out=cs[0:96], in_=self.cosT[:, s0:s0 + TS]), reads=["cos_d"], writes=[csk])
            T.dma("sp", snk, lambda e, sn=sn, s0=s0: e.dma_start(out=sn[0:96], in_=self.sinS[:, s0:s0 + TS]), reads=["sin_d"], writes=[snk])
            if t + 1 < ntile:
                T.dma("sp", f"xinA{(t + 1) % 2}", lambda e, c1=c0 + TS, x2=xin[(t + 1) % 2]: e.dma_start(out=x2, in_=src_t[:, :, c1:c1 + TS]),
                      reads=[("y", (t + 1) // 2)], writes=[f"xinA{(t + 1) % 2}"])
            T.default_cost = {"pe": 150, "act": 600, "dve": 500, "pool": 1000, "sp": 100}
            if s0 == 0:
                T.op("pool", lambda e: e.memset(Sst, 0.0), writes=["Sst"])
                T.op("pool", lambda e: e.memset(ubuf, 0.0), writes=["ubuf"])
            T.op("dve", lambda e, xi=xi: e.tensor_tensor(out=xb, in0=xi, in1=gmix.unsqueeze(2).to_broadcast([128, KC, TS]), op=ALU.mult),
                 reads=[xik, "vec"], writes=["xb"])
            T.op("act", lambda e, xi=xi: e.activation(out=sq, in_=xi, func=AF.Square), reads=[xik], writes=["sq"])
            for kc in range(KC):
                T.op("pe", lambda e, kc=kc: e.matmul(ps[7][:, 0:TS], lhsT=ones, rhs=sq[:, kc, :], start=(kc == 0), stop=(kc == KC - 1)),
                     reads=["ones", "sq"], writes=["ps7"])
            self.rstd_from(rs0, ps[7][:, 0:TS], lt0, 1.0 / D, EPS, ["ps7"], "lt0", "rs0")
            for blk in range(2):
                for kc in range(KC):
                    T.op("pe", lambda e, kc=kc, blk=blk: e.matmul(ps[6][:, 2 * blk:2 * blk + 2], lhsT=sq[:, kc, blk * 128:(blk + 1) * 128], rhs=ones[:, 0:2],
                                                                 start=(kc == 0), stop=(kc == KC - 1)),
                         reads=["ones", "sq"], writes=["ps6"])
            self.rstd_from(rtok[:, 0:4], ps[6][:, 0:4], ttok[:, 0:4], 1.0 / D, EPS, ["ps6"], "ttok", "rtok")

            def inproj(col0, M, dst, dkey, extra_reads=()):
                p, pk = pbank()
                for kc in range(KC):
                    T.op("pe", lambda e, kc=kc, p=p: e.matmul(p[0:M, 0:TS], lhsT=win[:, kc, col0:col0 + M], rhs=xb[:, kc, :],
                                                             start=(kc == 0), stop=(kc == KC - 1)),
                         reads=["win", "xb"], writes=[pk])
                T.op("dve", lambda e, p=p: e.tensor_tensor(out=dst, in0=p[0:M, 0:TS], in1=rs0[0:M], op=ALU.mult),
                     reads=[pk, "rs0"], writes=[dkey])

            T.mute = 'Q' not in A1PARTS
            for j in range(3):
                inproj(O_CQ + j * 128, 128, cq[:, j, :], ("cq", j))
            T.op("act", lambda e: e.activation(out=cqs, in_=cq, func=AF.Square), reads=[("cq", 0), ("cq", 1), ("cq", 2)], writes=["cqs"])
            T.op("dve", lambda e: e.tensor_tensor(out=cqb, in0=cq, in1=gq1.unsqueeze(2).to_broadcast([128, 3, TS]), op=ALU.mult),
                 reads=[("cq", 0), ("cq", 1), ("cq", 2), "vec"], writes=["cqb"])
            for j in range(3):
                T.op("pe", lambda e, j=j: e.matmul(ps[7][:, 0:TS], lhsT=ones, rhs=cqs[:, j, :], start=(j == 0), stop=(j == 2)),
                     reads=["ones", "cqs"], writes=["ps7"])
            self.rstd_from(epsq, ps[7][:, 0:TS], lt0, 1.0 / 384, EPS, ["ps7"], "lt0", "epsq")
            T.op("dve", lambda e: e.tensor_tensor(out=cqb, in0=cqb, in1=epsq.unsqueeze(1).to_broadcast([128, 3, TS]), op=ALU.mult),
                 reads=["cqb", "epsq"], writes=["cqb"])
            for hp in range(4):
                pa, pak = pbank()
                pb, pbk = pbank()
                for hh in range(2):
                    h = hp * 2 + hh
                    for j in range(3):
                        T.op("pe", lambda e, j=j, h=h, hh=hh, pa=pa: e.matmul(pa[0:96, hh * TS:(hh + 1) * TS], lhsT=wuq[:, j, h * 96:(h + 1) * 96], rhs=cqb[:, j, :],
                                                                            start=(j == 0), stop=(j == 2)),
                             reads=["wuq", "cqb"], writes=[pak])
                for hh in range(2):
                    h = hp * 2 + hh
                    for j in range(3):
                        T.op("pe", lambda e, j=j, h=h, hh=hh, pb=pb: e.matmul(pb[0:96, hh * TS:(hh + 1) * TS], lhsT=wuqs[:, j, h * 96:(h + 1) * 96], rhs=cqb[:, j, :],
                                                                            start=(j == 0), stop=(j == 2)),
                             reads=["wuqs", "cqb"], writes=[pbk])
                if QSTAGE < 1:
                    continue
                g0, g1, g2, g3 = gen[0], gen[1], gen[2], gen[3]
                g0f = g0.rearrange("p a b -> p (a b)"); g1f = g1.rearrange("p a b -> p (a b)")
                g2f = g2.rearrange("p a b -> p (a b)"); g3f = g3.rearrange("p a b -> p (a b)")
                sqp = ksq[0:96, 0:2, :].rearrange("p a b -> p (a b)")
                T.op("act", lambda e, pa=pa, sqp=sqp: e.activation(out=sqp, in_=pa[0:96, :], func=AF.Square), reads=[pak], writes=["ksq"])
                T.op("pe", lambda e, sqp=sqp: e.matmul(ps[5][0:96, :], lhsT=ones[0:96, 0:96], rhs=sqp, start=True, stop=True),
                     reads=["ones", "ksq"], writes=["ps5"])
                self.rstd_from(g1f[0:96], ps[5][0:96, :], g0f[0:96], 1.0 / 96, EPS, ["ps5"], "g0", "g1")
                if QSTAGE < 2:
                    continue
                csb = cs[0:96].unsqueeze(1).to_broadcast([96, 2, TS])
                snb = sn[0:96].unsqueeze(1).to_broadcast([96, 2, TS])
                for hh in range(2):
                    T.op("dve", lambda e, pa=pa, g2=g2, hh=hh, cs=cs: e.scalar_tensor_tensor(out=g2[0:96, hh, :], in0=pa[0:96, hh * TS:(hh + 1) * TS], scalar=gqn, in1=cs[0:96],
                                                                                       op0=ALU.mult, op1=ALU.mult),
                         reads=[pak, csk, "vec"], writes=["g2"])
                    T.op("dve", lambda e, pb=pb, g3=g3, hh=hh, sn=sn: e.scalar_tensor_tensor(out=g3[0:96, hh, :], in0=pb[0:96, hh * TS:(hh + 1) * TS], scalar=gqns, in1=sn[0:96],
                                                                                       op0=ALU.mult, op1=ALU.mult),
                         reads=[pbk, snk, "vec"], writes=["g3"])
                T.op("dve", lambda e, g2=g2, g3=g3: e.tensor_tensor(out=g2[0:96], in0=g2[0:96], in1=g3[0:96], op=ALU.add), reads=["g2", "g3"], writes=["g2"])
                T.op("dve", lambda e, g2=g2, g1=g1, hp=hp: e.tensor_tensor(out=qTo[0:96, hp * 2:hp * 2 + 2, :], in0=g2[0:96], in1=g1[0:96], op=ALU.mult),
                     reads=["g2", "g1"], writes=["qTo"])
            T.dma("sp", "qst", lambda e, b=b, s0=s0: e.dma_start(out=self.q_s[b, :, :, s0:s0 + TS].rearrange("h p n -> p h n"), in_=qTo[0:96]),
                  reads=["qTo"], writes=[("q_s", b)])

            T.mute = 'K' not in A1PARTS
            for j in range(2):
                inproj(O_CKV + j * 128, 128, ckv[:, j, :], ("ckv", j))
            T.op("act", lambda e: e.activation(out=ckvs, in_=ckv, func=AF.Square), reads=[("ckv", 0), ("ckv", 1)], writes=["ckvs"])
            for j in range(2):
                T.op("pe", lambda e, j=j: e.matmul(ps[7][:, 0:TS], lhsT=ones, rhs=ckvs[:, j, :], start=(j == 0), stop=(j == 1)),
                     reads=["ones", "ckvs"], writes=["ps7"])
            self.rstd_from(rskv, ps[7][:, 0:TS], lt0, 1.0 / 256, EPS, ["ps7"], "lt0", "rskv")
            T.op("dve", lambda e: e.tensor_tensor(out=ckv, in0=ckv, in1=rskv.unsqueeze(1).to_broadcast([128, 2, TS]), op=ALU.mult),
                 reads=[("ckv", 0), ("ckv", 1), "rskv"], writes=[("ckv", 0), ("ckv", 1)])
            T.op("dve", lambda e: e.tensor_tensor(out=ckvb, in0=ckv, in1=gkv.unsqueeze(2).to_broadcast([128, 2, TS]), op=ALU.mult),
                 reads=[("ckv", 0), ("ckv", 1), "vec"], writes=["ckvb"])
            inproj(O_KR, 64, krt[0:64], "krt")
            T.op("act", lambda e: e.activation(out=kcat[64:96], in_=krt[0:32].unsqueeze(1).to_broadcast([32, 8, TS]), func=AF.Copy),
                 reads=["krt"], writes=["kcat_r"])
            T.op("act", lambda e: e.activation(out=ksw[64:96], in_=krt[32:64], func=AF.Copy), reads=["krt"], writes=["ksw"])
            for hp in range(4):
                p, pk = pbank()
                for hh in range(2):
                    h = hp * 2 + hh
                    for j in range(2):
                        T.op("pe", lambda e, j=j, h=h, hh=hh, p=p: e.matmul(p[0:64, hh * TS:(hh + 1) * TS], lhsT=wuk[:, j, h * 64:(h + 1) * 64], rhs=ckvb[:, j, :],
                                                                          start=(j == 0), stop=(j == 1)),
                             reads=["wuk", "ckvb"], writes=[pk])
                T.op("act", lambda e, p=p, hp=hp: e.activation(out=kcat[0:64, hp * 2:hp * 2 + 2, :], in_=p[0:64, :].rearrange("p (a b) -> p a b", a=2), func=AF.Copy),
                     reads=[pk], writes=[("kcat_n", hp)])
            kcat_keys = ["kcat_r"] + [("kcat_n", hp) for hp in range(4)]
            T.op("act", lambda e: e.activation(out=ksq[0:96], in_=kcat[0:96], func=AF.Square), reads=kcat_keys, writes=["ksq"])
            T.op("dve", lambda e, sn=sn: e.scalar_tensor_tensor(out=kbt[0:96], in0=ksw[0:96], scalar=gkns, in1=sn[0:96], op0=ALU.mult, op1=ALU.mult),
                 reads=["ksw", snk, "vec"], writes=["kbt"])
            for hp in range(4):
                g0, g1, g2 = gen[0], gen[1], gen[2]
                g0f = g0.rearrange("p a b -> p (a b)"); g1f = g1.rearrange("p a b -> p (a b)")
                T.op("pe", lambda e, hp=hp: e.matmul(ps[5][0:96, :], lhsT=ones[0:96, 0:96], rhs=ksq[0:96, hp * 2:hp * 2 + 2, :].rearrange("p a b -> p (a b)"),
                                                    start=True, stop=True),
                     reads=["ones", "ksq"], writes=["ps5"])
                self.rstd_from(g1f[0:96], ps[5][0:96, :], g0f[0:96], 1.0 / 96, EPS, ["ps5"], "g0", "g1")
                csb = cs[0:96].unsqueeze(1).to_broadcast([96, 2, TS])
                T.op("dve", lambda e, g2=g2, hp=hp, csb=csb: e.scalar_tensor_tensor(out=g2[0:96], in0=kcat[0:96, hp * 2:hp * 2 + 2, :], scalar=gkn, in1=csb,
                                                                                op0=ALU.mult, op1=ALU.mult),
                     reads=kcat_keys + [csk, "vec"], writes=["g2"])
                T.op("dve", lambda e, g2=g2: e.tensor_tensor(out=g2[0:96], in0=g2[0:96], in1=kbt[0:96].unsqueeze(1).to_broadcast([96, 2, TS]), op=ALU.add),
                     reads=["g2", "kbt"], writes=["g2"])
                T.op("dve", lambda e, g2=g2, g1=g1, hp=hp: e.tensor_tensor(out=kTo[0:96, hp * 2:hp * 2 + 2, :], in0=g2[0:96], in1=g1[0:96], op=ALU.mult),
                     reads=["g2", "g1"], writes=["kTo"])
            T.dma("sp", "kst", lambda e, b=b, s0=s0: e.dma_start(out=self.k_s[b, :, :, s0:s0 + TS].rearrange("h p n -> p h n"), in_=kTo[0:96]),
                  reads=["kTo"], writes=[("k_s", b)])
            for blk in range(2):
                p, pk = pbank()
                for j in range(2):
                    T.op("pe", lambda e, j=j, blk=blk, p=p: e.matmul(p[:], lhsT=ckvb[:, j, blk * 128:(blk + 1) * 128], rhs=wuv[:, j, :],
                                                                  start=(j == 0), stop=(j == 1)),
                         reads=["wuv", "ckvb"], writes=[pk])
                p3 = p[:].rearrange("p (h d) -> p h d", h=8)
                T.op("act", lambda e, p3=p3, blk=blk: e.activation(out=vto[:, blk, 0:8:2, 0:64], in_=p3[:, 0:8:2, :], func=AF.Copy),
                     reads=[pk], writes=["vto"])
                T.op("act", lambda e, p3=p3, blk=blk: e.activation(out=vto[:, blk, 1:8:2, 64:128], in_=p3[:, 1:8:2, :], func=AF.Copy),
                     reads=[pk], writes=["vto"])
            T.dma("sp", "vst", lambda e, b=b, s0=s0: e.dma_start(out=self.v_s[b, s0:s0 + TS, :].rearrange("(k p) c -> p k c", p=128),
                                                               in_=vto.rearrange("p k h w -> p k (h w)")),
                  reads=["vto"], writes=[("v_s", b)])

            T.mute = 'H' not in A1PARTS
            for (col, dstt, nm) in ((O_HQ, hq, "hq"), (O_HF, hf, "hf"), (O_HG, hg, "hg")):
                for j in range(2):
                    p, pk = pbank()
                    for kc in range(KC):
                        T.op("pe", lambda e, kc=kc, p=p, col=col, j=j: e.matmul(p[:, 0:TS], lhsT=win[:, kc, col + j * 128:col + (j + 1) * 128], rhs=xb[:, kc, :],
                                                                            start=(kc == 0), stop=(kc == KC - 1)),
                             reads=["win", "xb"], writes=[pk])
                    T.op("dve", lambda e, p=p, dstt=dstt, j=j: e.tensor_tensor(out=dstt[0:64, 2 * j, :], in0=p[0:64, 0:TS], in1=rs0[0:64], op=ALU.mult),
                         reads=[pk, "rs0"], writes=[(nm, 2 * j)])
                    T.op("dve", lambda e, p=p, dstt=dstt, j=j: e.tensor_tensor(out=dstt[0:64, 2 * j + 1, :], in0=p[64:128, 0:TS], in1=rs0[64:128], op=ALU.mult),
                         reads=[pk, "rs0"], writes=[(nm, 2 * j + 1)])
            hqk = [("hq", h) for h in range(4)]; hfk = [("hf", h) for h in range(4)]; hgk = [("hg", h) for h in range(4)]
            for blk in range(2):
                p, pk = pbank()
                for kc in range(KC):
                    T.op("pe", lambda e, kc=kc, blk=blk, p=p: e.matmul(p[:, 0:256], lhsT=xb[:, kc, blk * 128:(blk + 1) * 128], rhs=win[:, kc, O_HI:O_HI + 256],
                                                                    start=(kc == 0), stop=(kc == KC - 1)),
                         reads=["win", "xb"], writes=[pk])
                T.op("dve", lambda e, p=p, blk=blk: e.tensor_scalar(out=vtok[0:64, 2 * blk, :], in0=p[0:64, 0:256], scalar1=rtok[0:64, 2 * blk:2 * blk + 1], scalar2=None, op0=ALU.mult),
                     reads=[pk, "rtok"], writes=["vtok"])
                T.op("dve", lambda e, p=p, blk=blk: e.tensor_scalar(out=vtok[0:64, 2 * blk + 1, :], in0=p[64:128, 0:256], scalar1=rtok[64:128, 2 * blk:2 * blk + 1], scalar2=None, op0=ALU.mult),
                     reads=[pk, "rtok"], writes=["vtok"])
            T.default_cost = {"pe": 120, "act": 1900, "dve": 1600, "pool": 1000, "sp": 100}
            E, L1, L2, Bc, Wk = hgen
            H = slice(0, 64)
            lbb = lbv[0:64].unsqueeze(2).to_broadcast([64, 4, TS])
            T.op("act", lambda e: e.activation(out=E[H], in_=hf[H], func=AF.Exp, scale=-1.0), reads=hfk, writes=["E"])
            T.op("act", lambda e: e.activation(out=L1[H], in_=E[H], func=AF.Ln, scale=1.0, bias=1.0), reads=["E"], writes=["L1"])
            T.op("dve", lambda e: e.tensor_tensor(out=E[H], in0=E[H], in1=lbb, op=ALU.mult), reads=["E", "lbv"], writes=["E"])
            T.op("act", lambda e: e.activation(out=L2[H], in_=E[H], func=AF.Ln, scale=1.0, bias=1.0), reads=["E"], writes=["L2"])
            T.op("dve", lambda e: e.tensor_tensor(out=L2[H], in0=L2[H], in1=L1[H], op=ALU.subtract), reads=["L1", "L2"], writes=["L2"])
            T.op("dve", lambda e: e.tensor_tensor_scan(out=Bc[H].rearrange("p a b -> p (a b)"), data0=m01[H].rearrange("p a b -> p (a b)"),
                                                       data1=L2[H].rearrange("p a b -> p (a b)"), initial=0.0, op0=ALU.mult, op1=ALU.add),
                 reads=["m01", "L2"], writes=["Bc"])
            B4 = Bc[H].rearrange("p h (c t) -> p h c t", t=64)
            blast, cmid, e1, e2, ec, scx = sc8
            T.op("dve", lambda e: e.tensor_copy(out=blast[H], in_=B4[:, :, :, 63]), reads=["Bc"], writes=["blast"])
            T.op("dve", lambda e: e.tensor_copy(out=cmid[H], in_=B4[:, :, :, 31]), reads=["Bc"], writes=["cmid"])
            T.op("act", lambda e: e.activation(out=e1[H], in_=blast[H], func=AF.Exp), reads=["blast"], writes=["e1"])
            T.op("act", lambda e: e.activation(out=ec[H], in_=cmid[H], func=AF.Exp), reads=["cmid"], writes=["ec"])
            T.op("dve", lambda e: e.tensor_tensor(out=scx[H], in0=blast[H], in1=cmid[H], op=ALU.subtract), reads=["blast", "cmid"], writes=["scx"])
            T.op("act", lambda e: e.activation(out=e2[H], in_=scx[H], func=AF.Exp), reads=["scx"], writes=["e2"])
            T.op("dve", lambda e: e.tensor_tensor(out=B4, in0=B4, in1=cmid[H].unsqueeze(3).to_broadcast([64, 4, 4, 64]), op=ALU.subtract),
                 reads=["Bc", "cmid"], writes=["Bc"])
            T.op("act", lambda e: e.activation(out=L1[H], in_=L2[H], func=AF.Exp), reads=["L2"], writes=["L1"])
            T.op("dve", lambda e: e.tensor_scalar(out=L1[H], in0=L1[H], scalar1=-1.0, scalar2=1.0, op0=ALU.mult, op1=ALU.add), reads=["L1"], writes=["L1"])
            T.op("act", lambda e: e.activation(out=Wk[H], in_=Bc[H], func=AF.Exp, scale=-1.0), reads=["Bc"], writes=["Wk"])
            T.op("dve", lambda e: e.tensor_tensor(out=ktl[H], in0=L1[H], in1=Wk[H], op=ALU.mult), reads=["L1", "Wk"], writes=["ktl"])
            T.op("act", lambda e: e.activation(out=E[H], in_=hq[H], func=AF.Exp, scale=-1.0), reads=hqk, writes=["E"])
            T.op("act", lambda e: e.activation(out=E[H], in_=E[H], func=AF.Ln, scale=1.0, bias=1.0), reads=["E"], writes=["E"])
            T.op("dve", lambda e: e.tensor_tensor(out=E[H], in0=Bc[H], in1=E[H], op=ALU.subtract), reads=["E", "Bc"], writes=["E"])
            T.op("act", lambda e: e.activation(out=Wk[H], in_=E[H], func=AF.Exp), reads=["E"], writes=["Wk"])
            T.op("dve", lambda e: e.tensor_tensor(out=qtl[H], in0=hq[H], in1=Wk[H], op=ALU.mult), reads=hqk + ["Wk"], writes=["qtl"])
            T.op("act", lambda e: e.activation(out=L2[H], in_=hg[H], func=AF.Exp, scale=-1.0), reads=hgk, writes=["L2"])
            T.op("act", lambda e: e.activation(out=L2[H], in_=L2[H], func=AF.Ln, scale=1.0, bias=1.0), reads=["L2"], writes=["L2"])
            T.op("act", lambda e: e.activation(out=L2[H], in_=L2[H], func=AF.Exp, scale=-1.0), reads=["L2"], writes=["L2"])
            T.op("dve", lambda e: e.tensor_tensor(out=L2[H], in0=L2[H], in1=hg[H], op=ALU.mult), reads=["L2"] + hgk, writes=["L2"])
            T.default_cost = {"pe": 120, "act": 600, "dve": 500, "pool": 1000, "sp": 100}
            ptr = ps[4][:].bitcast(BF16)
            for c in range(4):
                for h in range(4):
                    T.op("pe", lambda e, c=c, h=h: e.transpose(ptr[0:64, (c * 4 + h) * 64:(c * 4 + h + 1) * 64], ktl[0:64, h, c * 64:(c + 1) * 64], ident[0:64, 0:64]),
                         reads=["ktl", "ident"], writes=["ps4"])
            T.op("act", lambda e: e.activation(out=ktok[H].rearrange("p a b c -> p (a b c)"), in_=ptr[0:64, 0:1024], func=AF.Copy), reads=["ps4"], writes=["ktok"])
            for h in range(4):
                pb_ = ps[5] if h < 2 else ps[6]
                for c in range(4):
                    T.op("pe", lambda e, h=h, c=c, pb_=pb_: e.matmul(pb_[0:64, ((h % 2) * 4 + c) * 64:((h % 2) * 4 + c + 1) * 64],
                                                                   lhsT=ktl[0:64, h, c * 64:(c + 1) * 64], rhs=qtl[0:64, h, c * 64:(c + 1) * 64], start=True, stop=True),
                         reads=["ktl", "qtl"], writes=["ps5" if h < 2 else "ps6"])
            for hh2 in range(2):
                pb_ = ps[5 + hh2]
                T.op("dve", lambda e, hh2=hh2, pb_=pb_: e.tensor_tensor(out=attb[0:64, hh2 * 2:hh2 * 2 + 2], in0=pb_[0:64, :].rearrange("p (a b c) -> p a b c", a=2, b=4),
                                                                      in1=tri[0:64, 0:2], op=ALU.mult),
                     reads=[f"ps{5 + hh2}", "tri"], writes=["attb"])
            for h in range(4):
                pb_ = ps[7] if h < 2 else ps[4]
                for c in range(4):
                    T.op("pe", lambda e, h=h, c=c, pb_=pb_: e.matmul(pb_[0:64, ((h % 2) * 4 + c) * 64:((h % 2) * 4 + c + 1) * 64],
                                                                   lhsT=ktok[0:64, c, h, :], rhs=vtok[0:64, c, h * 64:(h + 1) * 64], start=True, stop=True),
                         reads=["ktok", "vtok"], writes=["ps7" if h < 2 else "ps4"])
            U7 = ps[7][0:64, :].rearrange("p (a c d) -> p a c d", a=2, c=4)
            U4 = ps[4][0:64, :].rearrange("p (a c d) -> p a c d", a=2, c=4)
            for c in range(4):
                T.op("dve", lambda e, c=c: e.tensor_tensor(out=Ssc[0:64, :, c, :], in0=Sst[H], in1=ec[0:64, :, c:c + 1].to_broadcast([64, 4, 64]), op=ALU.mult),
                     reads=["Sst", "ec"], writes=[("Ssc", c)])
                T.op("dve", lambda e, c=c: e.tensor_tensor(out=tmpU[0:64, 0:2, :], in0=U7[:, :, c, :], in1=e2[0:64, 0:2, c:c + 1].to_broadcast([64, 2, 64]), op=ALU.mult),
                     reads=["ps7", "e2"], writes=["tmpU0"])
                T.op("dve", lambda e, c=c: e.tensor_tensor(out=tmpU[0:64, 2:4, :], in0=U4[:, :, c, :], in1=e2[0:64, 2:4, c:c + 1].to_broadcast([64, 2, 64]), op=ALU.mult),
                     reads=["ps4", "e2"], writes=["tmpU1"])
                T.op("dve", lambda e, c=c: e.tensor_tensor(out=Sst[H], in0=Sst[H], in1=e1[0:64, :, c:c + 1].to_broadcast([64, 4, 64]), op=ALU.mult),
                     reads=["Sst", "e1", ("Ssc", c)], writes=["Sst"])
                T.op("dve", lambda e: e.tensor_tensor(out=Sst[H], in0=Sst[H], in1=tmpU[H], op=ALU.add), reads=["Sst", "tmpU0", "tmpU1"], writes=["Sst"])
            for h in range(4):
                p, pk = pbank()
                for c in range(4):
                    T.op("pe", lambda e, h=h, c=c, p=p: e.matmul(p[0:64, c * 64:(c + 1) * 64], lhsT=vtok[0:64, c, h * 64:(h + 1) * 64], rhs=attb[0:64, h, c, :],
                                                              start=True, stop=False),
                         reads=["vtok", "attb"], writes=[pk])
                    T.op("pe", lambda e, h=h, c=c, p=p: e.matmul(p[0:64, c * 64:(c + 1) * 64], lhsT=Ssc[0:64, h, c, :], rhs=qtl[0:64, h, c * 64:(c + 1) * 64],
                                                              start=False, stop=True),
                         reads=[("Ssc", c), "qtl"], writes=[pk])
                T.op("act", lambda e, p=p, h=h: e.activation(out=osq[0:64, h, :], in_=p[0:64, 0:TS], func=AF.Square), reads=[pk], writes=[("osq", h)])
                T.op("pe", lambda e, h=h: e.matmul(ps[5][0:64, 0:TS], lhsT=ones[0:64, 0:64], rhs=osq[0:64, h, :], start=True, stop=True),
                     reads=["ones", ("osq", h), "attb"], writes=["ps5"])
                g0, g1 = gen[4], gen[5]
                g0f = g0.rearrange("p a b -> p (a b)"); g1f = g1.rearrange("p a b -> p (a b)")
                self.rstd_from(g1f[0:64, 0:TS], ps[5][0:64, 0:TS], g0f[0:64, 0:TS], 1.0 / 64, EPS, ["ps5"], "g4", "g5")
                T.op("dve", lambda e, p=p, h=h, g0f=g0f, g1f=g1f: e.scalar_tensor_tensor(out=g0f[0:64, 0:TS], in0=p[0:64, 0:TS], scalar=go[:, h:h + 1], in1=g1f[0:64, 0:TS],
                                                                                     op0=ALU.mult, op1=ALU.mult),
                     reads=[pk, "g5", "vec"], writes=["g4"])
                T.op("dve", lambda e, h=h, g0f=g0f: e.tensor_tensor(out=yho[0:64, h, :], in0=g0f[0:64, 0:TS], in1=L2[0:64, h, :], op=ALU.mult),
                     reads=["g4", "L2"], writes=["yho"])
            T.dma("sp", "yhst", lambda e, b=b, s0=s0: e.dma_start(out=self.yh_s[b, :, :, s0:s0 + TS].rearrange("h p n -> p h n"), in_=yho[0:64]),
                  reads=["yho"], writes=[("yh_s", b)])

            T.mute = 'C' not in A1PARTS
            for j in range(2):
                inproj(O_CB + j * 128, 128, cbt[:, j, :], ("cb", j))
                inproj(O_CC + j * 128, 128, cct[:, j, :], ("cc", j))
                inproj(O_CX + j * 128, 128, cxt[:, j, :], ("cx", j))
            ck = [("cc", 0), ("cc", 1), ("cx", 0), ("cx", 1)]
            T.op("dve", lambda e: e.tensor_tensor(out=ubuf[:, :, 2:TS + 2], in0=cct, in1=cxt, op=ALU.mult), reads=ck + ["ucarry"], writes=["ubuf"])
            for j in range(2):
                T.op("dve", lambda e, j=j: e.tensor_scalar(out=yacc[:, j, :], in0=ubuf[:, j, 2:TS + 2], scalar1=cw[:, j * 3 + 2:j * 3 + 3], scalar2=None, op0=ALU.mult),
                     reads=["ubuf", "vec"], writes=[("yacc", j)])
                T.op("dve", lambda e, j=j: e.scalar_tensor_tensor(out=yacc[:, j, :], in0=ubuf[:, j, 1:TS + 1], scalar=cw[:, j * 3 + 1:j * 3 + 2], in1=yacc[:, j, :],
                                                                  op0=ALU.mult, op1=ALU.add),
                     reads=["ubuf", "vec", ("yacc", j)], writes=[("yacc", j)])
                T.op("dve", lambda e, j=j: e.scalar_tensor_tensor(out=yacc[:, j, :], in0=ubuf[:, j, 0:TS], scalar=cw[:, j * 3:j * 3 + 1], in1=yacc[:, j, :],
                                                                  op0=ALU.mult, op1=ALU.add),
                     reads=["ubuf", "vec", ("yacc", j)], writes=[("yacc", j)])
            T.op("dve", lambda e: e.tensor_tensor(out=yco, in0=yacc, in1=cbt, op=ALU.mult),
                 reads=[("yacc", 0), ("yacc", 1), ("cb", 0), ("cb", 1)], writes=["yco"])
            T.op("dve", lambda e: e.tensor_copy(out=ubuf[:, :, 0:2], in_=ubuf[:, :, TS:TS + 2]), reads=["ubuf"], writes=["ucarry", "ubuf"])
            T.dma("sp", "ycst", lambda e, b=b, s0=s0: e.dma_start(out=self.yc_s[b, :, s0:s0 + TS].rearrange("(j p) n -> p j n", p=128), in_=yco),
                  reads=["yco"], writes=[("yc_s", b)])
            T.mute = False

    def phase_A2(self, l, src):
        T, A, nc, ps = self.T, self.A, self.nc, self.ps
        NB, S = self.NB, self.S
        T.default_cost = {"pe": 260, "act": 550, "dve": 700, "pool": 1000, "sp": 100}
        TS = 512
        NKB = S // 128
        A.reset()
        al = A.alloc
        kc_ = al([8, S], BF16)
        vc_ = al([NKB, VROW], BF16)
        wom = al([4, D], BF16)
        woh = al([2, D], BF16)
        woc = al([2, D], BF16)
        cm = al([4, 512], BF16)
        ident = al([128], BF16)
        cst = al([128 + 64 + 256], F32)
        qt = [al([8, TS], BF16) for _ in range(2)]
        yh = al([2, TS], BF16)
        yc = al([2, TS], BF16)
        ym = al([4, TS], BF16)
        PT = [al([TS], BF16) for _ in range(4)]
        rc2 = [al([TS], F32) for _ in range(2)]
        xr = [al([TS], F32) for _ in range(3)]
        yT = self.yT
        T.dma("sp", "cst", lambda e: e.dma_start(out=cst, in_=self.consts), writes=["cst"])
        T.op("dve", lambda e: e.tensor_copy(out=ident, in_=cst[:, 0:128]), reads=["cst"], writes=["ident"])
        T.dma("pool", "cm", lambda e: e.dma_start(out=cm.rearrange("p a b -> p (a b)"), in_=self.cmask), writes=["cm"])
        self.wload(wom.rearrange("p a b -> p (a b)"), self.w_om[l], "wom", 4096, 4 * D)
        self.wload(woh.rearrange("p a b -> p (a b)"), self.w_oh[l], "woh", 2048, 2 * D)
        self.wload(woc.rearrange("p a b -> p (a b)"), self.w_oc[l], "woc", 2048, 2 * D)
        scale = 1.0 / math.sqrt(96.0)
        tps = S // TS
        sci = 0
        xri = 0
        for b in range(NB):
            NCH = S // 512
            for c in range(NCH):
                T.dma("sp", f"kcl{c}", lambda e, b=b, c=c: e.dma_start(out=kc_[0:96, :, c * 512:(c + 1) * 512],
                                                                     in_=self.k_s[b, :, :, c * 512:(c + 1) * 512].rearrange("h p n -> p h n")),
                      reads=[("k_s", b)], writes=[("kc", c)])
                T.dma("sp", f"vcl{c}", lambda e, b=b, c=c: e.dma_start(out=vc_[:, 4 * c:4 * c + 4, :],
                                                                     in_=self.v_s[b, c * 512:(c + 1) * 512, :].rearrange("(k p) c -> p k c", p=128)),
                      reads=[("v_s", b)], writes=[("vc", c)])
            vc4 = vc_.rearrange("p k (h w) -> p k h w", w=128)
            for i in range(tps):
                s0 = i * TS
                t = b * tps + i
                c0 = b * S + s0
                q = qt[t % 2]; qk = f"qt{t % 2}"
                T.dma("sp", qk, lambda e, q=q, b=b, s0=s0: e.dma_start(out=q[0:96], in_=self.q_s[b, :, :, s0:s0 + TS].rearrange("h p n -> p h n")),
                      reads=[("q_s", b)], writes=[qk])
                T.dma("sp", "yhl", lambda e, b=b, s0=s0: e.dma_start(out=yh, in_=self.yh_s[b, :, :, s0:s0 + TS].rearrange("(j hh) p n -> (hh p) j n", hh=2)),
                      reads=[("yh_s", b)], writes=["yh"])
                T.dma("sp", "ycl", lambda e, b=b, s0=s0: e.dma_start(out=yc, in_=self.yc_s[b, :, s0:s0 + TS].rearrange("(j p) n -> p j n", p=128)),
                      reads=[("yc_s", b)], writes=["yc"])
                nkb = 4 * i + 4
                LA = 2
                ND = 0
                steps = [(h, kb) for h in range(8) for kb in range(nkb)]
                slots = {}
                pending_norm = []

                def emit_qk(idx):
                    nonlocal sci
                    h, kb = steps[idx]
                    sb_ = sci % 4; sci += 1
                    pt = PT[sb_]; ptk = f"PT{sb_}"
                    dj = kb - 4 * i
                    q0 = 128 * dj if dj > 0 else 0
                    slots[idx] = (pt, ptk, q0)
                    T.op("pe", lambda e, h=h, kb=kb, sb_=sb_, q=q, dj=dj, q0=q0: e.matmul(ps[sb_][:, q0:], lhsT=kc_[0:96, h, kb * 128:(kb + 1) * 128], rhs=q[0:96, h, q0:],
                                                                                       start=True, stop=(dj < 0)),
                         reads=[("kc", kb // 4), qk], writes=[f"ps{sb_}"])
                    if dj >= 0:
                        T.op("pe", lambda e, sb_=sb_, q0=q0: e.matmul(ps[sb_][:, q0:q0 + 128], lhsT=ident, rhs=cm[:, 0, 0:128], start=False, stop=True),
                             reads=["ident", "cm"], writes=[f"ps{sb_}"], cost=90)
                    T.op("act", lambda e, pt=pt, sb_=sb_, q0=q0: e.activation(out=pt[:, q0:], in_=ps[sb_][:, q0:], func=AF.Exp, scale=scale),
                         reads=[f"ps{sb_}"], writes=[ptk])

                def emit_pv(idx):
                    h, kb = steps[idx]
                    pt, ptk, q0 = slots.pop(idx)
                    po = ps[4 + h % 2]; pok = f"ps{4 + h % 2}"
                    T.op("pe", lambda e, h=h, kb=kb, pt=pt, po=po, nkb=nkb, vc4=vc4, q0=q0: e.matmul(po[:, q0:], lhsT=vc4[:, kb, h, :], rhs=pt[:, q0:], start=(kb == 0), stop=(kb == nkb - 1)),
                         reads=[("vc", kb // 4), ptk], writes=[pok])
                    if kb == nkb - 1:
                        return h
                    return None

                def emit_norm(h):
                    po = ps[4 + h % 2]; pok = f"ps{4 + h % 2}"
                    rch = rc2[h % 2]; rck = f"rc{h % 2}"
                    if h % 2 == 0:
                        T.op("dve", lambda e, po=po, rch=rch: e.reciprocal(out=rch[64:128], in_=po[64:128, :]), reads=[pok], writes=[rck])
                        T.op("dve", lambda e, po=po, h=h, rch=rch: e.tensor_tensor(out=ym[0:64, h // 2, :], in0=po[0:64, :], in1=rch[64:128], op=ALU.mult),
                             reads=[pok, rck], writes=[("ym", h)])
                    else:
                        T.op("dve", lambda e, po=po, rch=rch: e.reciprocal(out=rch[0:64], in_=po[0:64, :]), reads=[pok], writes=[rck])
                        T.op("dve", lambda e, po=po, h=h, rch=rch: e.tensor_tensor(out=ym[64:128, h // 2, :], in0=po[64:128, :], in1=rch[0:64], op=ALU.mult),
                             reads=[pok, rck], writes=[("ym", h)])

                nst = len(steps)
                for idx in range(nst + LA):
                    if idx < nst:
                        emit_qk(idx)
                    if idx - LA >= 0:
                        hdone = emit_pv(idx - LA)
                        if hdone is not None:
                            pending_norm.append((idx + ND, hdone))
                    while pending_norm and pending_norm[0][0] <= idx:
                        emit_norm(pending_norm.pop(0)[1])
                for _, hh_ in pending_norm:
                    emit_norm(hh_)
                for oc in range(8):
                    p = ps[6 + oc % 2]; pk = f"ps{6 + oc % 2}"
                    x_ = xr[xri % 3]; xk = f"xrA{xri % 3}"; xri += 1
                    T.dma("sp", xk, lambda e, x_=x_, oc=oc, c0=c0: e.dma_start(out=x_, in_=src[oc * 128:(oc + 1) * 128, c0:c0 + TS]),
                          reads=[("y", c0 // 512)], writes=[xk])
                    for j in range(4):
                        T.op("pe", lambda e, j=j, oc=oc, p=p: e.matmul(p[:], lhsT=wom[:, j, oc * 128:(oc + 1) * 128], rhs=ym[:, j, :], start=(j == 0), stop=False),
                             reads=["wom", ("ym", 2 * j), ("ym", 2 * j + 1)], writes=[pk])
                    for j in range(2):
                        T.op("pe", lambda e, j=j, oc=oc, p=p: e.matmul(p[:], lhsT=woh[:, j, oc * 128:(oc + 1) * 128], rhs=yh[:, j, :], start=False, stop=False),
                             reads=["woh", "yh"], writes=[pk])
                    for j in range(2):
                        T.op("pe", lambda e, j=j, oc=oc, p=p: e.matmul(p[:], lhsT=woc[:, j, oc * 128:(oc + 1) * 128], rhs=yc[:, j, :], start=False, stop=(j == 1)),
                             reads=["woc", "yc"], writes=[pk])
                    T.op("dve", lambda e, p=p, x_=x_: e.tensor_tensor(out=x_, in0=p[:], in1=x_, op=ALU.add), reads=[pk, xk], writes=[xk])
                    T.dma("sp", xk, lambda e, x_=x_, oc=oc, c0=c0: e.dma_start(out=yT[oc * 128:(oc + 1) * 128, c0:c0 + TS], in_=x_),
                          reads=[xk], writes=[("y", c0 // 512)])


def _kpn(w, p=128):
    K, N = w.shape
    return np.ascontiguousarray(w.reshape(K // p, p, N).transpose(1, 0, 2).reshape(p, -1))


def host_consts(S, positions):
    ident = np.eye(128, dtype=np.float32)
    tri = np.zeros((128, 64), np.float32)
    tri[0:64] = (np.arange(64)[:, None] <= np.arange(64)[None, :]).astype(np.float32)
    m01 = np.ones((128, 256), np.float32)
    m01[:, ::64] = 0.0
    consts = np.concatenate([ident, tri, m01], axis=1)
    cmask = np.zeros((128, 4, 512), np.float32)
    p = np.arange(128)[:, None]
    n = np.arange(512)[None, :]
    for j in range(4):
        cmask[:, j, :] = np.where(128 * j + p <= n, 0.0, NEG)
    inv_freq = (10000.0 ** (-np.arange(0, 32, 2, dtype=np.float32) / 32)).astype(np.float32)
    invf = np.zeros((96, 2), np.float32)
    invf[64:80, 0] = -inv_freq
    invf[80:96, 0] = inv_freq
    posr = np.ascontiguousarray(np.broadcast_to(np.asarray(positions).astype(np.int32)[None, :], (96, S)))
    cosT, sinS = posr, invf
    return consts, cmask.reshape(128, -1), cosT, sinS


def host_weights(inp, L):
    out = {}
    perm = np.concatenate([np.arange(64), np.arange(80, 96), np.arange(64, 80)])
    w_in = np.asarray(inp["w_in"])
    kr = w_in[:, :, 640:672]
    krsw = np.concatenate([kr[:, :, 16:32], kr[:, :, 0:16]], axis=2)
    w_in2 = np.concatenate([w_in[:, :, :672], krsw, w_in[:, :, 672:]], axis=2)
    out["w_in"] = np.stack([_kpn(w_in2[l]) for l in range(L)])
    w_uq = np.asarray(inp["w_uq"])
    out["w_uq"] = np.stack([_kpn(w_uq[l]) for l in range(L)])
    w_uqsw = w_uq.reshape(L, 384, 8, 96)[:, :, :, perm].reshape(L, 384, 768)
    out["w_uqsw"] = np.stack([_kpn(w_uqsw[l]) for l in range(L)])
    w_ukv = np.asarray(inp["w_ukv"]).reshape(L, 256, 8, 128)
    w_uk = w_ukv[:, :, :, :64].reshape(L, 256, 512)
    w_uv = w_ukv[:, :, :, 64:].reshape(L, 256, 512)
    out["w_uk"] = np.stack([_kpn(w_uk[l]) for l in range(L)])
    out["w_uv"] = np.stack([_kpn(w_uv[l]) for l in range(L)])
    w_out = np.asarray(inp["w_out"])
    out["w_om"] = np.stack([_kpn(w_out[l, 0:512]) for l in range(L)])
    out["w_oh"] = np.stack([_kpn(w_out[l, 512:768]) for l in range(L)])
    out["w_oc"] = np.stack([_kpn(w_out[l, 768:1024]) for l in range(L)])
    for nm, key in (("w_xq", "w_xq"), ("w_xkv", "w_xkv"), ("w_xo", "w_xo"), ("w_up", "w_up"), ("w_dn", "w_down")):
        w = np.asarray(inp[key])
        out[nm] = np.stack([_kpn(w[l]) for l in range(L)])
    vecs = np.zeros((L, 128, NV), np.float32)

    def col(v):
        v = np.asarray(v)
        return v.reshape(-1, 128).T
    for l in range(L):
        vecs[l, :, V_GMIX:V_GMIX + 8] = col(inp["mix_norm_g"][l])
        vecs[l, :, V_GQ1:V_GQ1 + 3] = col(inp["mla_q_norm_g"][l])
        vecs[l, :, V_GKV:V_GKV + 2] = col(inp["mla_kv_norm_g"][l])
        gq = np.asarray(inp["mla_qn_g"][l]); gk = np.asarray(inp["mla_kn_g"][l])
        vecs[l, 0:96, V_GQN] = gq
        vecs[l, 0:96, V_GQNSW] = gq[perm]
        vecs[l, 0:96, V_GKN] = gk
        vecs[l, 0:96, V_GKNSW] = gk[perm]
        vecs[l, 0:64, V_GO:V_GO + 4] = np.asarray(inp["hgrn_o_norm_g"][l]).reshape(4, 64).T
        cwl = np.asarray(inp["conv_w"][l])
        vecs[l, :, V_CW:V_CW + 6] = cwl.reshape(3, 2, 128).transpose(2, 1, 0).reshape(128, 6)
        vecs[l, :, V_GXA:V_GXA + 8] = col(inp["xattn_norm_g"][l])
        vecs[l, :, V_GXQ] = np.asarray(inp["xq_norm_g"][l])
        vecs[l, :, V_GXK] = np.asarray(inp["xk_norm_g"][l])
        vecs[l, :, V_GMLP:V_GMLP + 8] = col(inp["mlp_norm_g"][l])
        vecs[l, :, V_GMEM:V_GMEM + 8] = col(inp["mem_norm_g"][l])
    out["vecs"] = vecs
    lb = np.asarray(inp["hgrn_lb_logits"])
    lbl = np.zeros((64, 4, 4), np.float32)
    lbl[:, :, :L] = lb.reshape(L, 4, 64).transpose(2, 1, 0)
    if L < 4:
        lbl[:, :, L:] = -1e4
    out["lbl"] = lbl.reshape(64, 16)
    return out


_CACHE = {}


def run(inputs, NB, S, L, ncores, phases="A1,A2,B,C", trace=False):
    x = np.asarray(inputs["x"], dtype=np.float32)
    mem = np.asarray(inputs["mem"], dtype=np.float32)
    MEM = mem.shape[1]
    key = (NB, S, L, MEM, phases)
    if key not in _CACHE:
        bld = Builder(NB, S, L, MEM, phases)
        bld.build()
        _CACHE[key] = bld
    bld = _CACHE[key]
    hw = host_weights(inputs, L)
    consts, cmask, cosT, sinS = host_consts(S, np.asarray(inputs["positions"]))
    in_maps = []
    for c in range(ncores):
        xs = x[c * NB:(c + 1) * NB].reshape(NB * S, D)
        m = dict(hw)
        m["xT"] = np.ascontiguousarray(xs.T)
        m["memT"] = np.ascontiguousarray(mem[c * NB:(c + 1) * NB].transpose(0, 2, 1))
        m["posr"] = cosT
        m["invf"] = sinS
        m["cmask"] = cmask
        m["consts"] = consts
        in_maps.append(m)
    res = run_bass_kernel_spmd(bld.nc, in_maps, core_ids=list(range(ncores)), trace=trace)
    outs = [np.ascontiguousarray(r["yT"].T).reshape(NB, S, D) for r in res.results]
    return np.concatenate(outs, axis=0), res


def kernel(**inputs):
    out, _ = run(inputs, NB=2, S=4096, L=4, ncores=8)
    return out.astype(np.float32)
```
